# Optimizing a Trainium2 kernel written in Bass

```python
import jax, jax.numpy as jnp
from jax import lax
import numpy as np

D_MODEL = 1024
BATCH = 8
SEQ = 2048
DEPTH = 1

N_MEM = 256
D_MIX = D_MODEL
D_SGU = D_MIX // 2
D_ATT = D_MIX - D_SGU
SGU_GROUPS = 8
SGU_GROUP_DIM = D_SGU // SGU_GROUPS
CHUNK = 128
N_HEADS = 8
HEAD_DIM = D_ATT // N_HEADS
DILATED_CONFIGS = ((128, 1), (512, 4), (2048, 16))
ROPE_THETA = 500000.0
ROPE_DIM = HEAD_DIM // 4
X_HEADS = 4
X_HEAD_DIM = D_MODEL // X_HEADS
D_FF = 2816
D_IN = 2 * D_SGU + 3 * D_ATT
NORM_EPS = 1e-6
LN_EPS = 1e-5
MAX_POS_OFFSET = 4096

kernel_name = "hybrid_sgu_dilated_attn_macaron_layer"

F32 = jnp.float32


def rmsnorm(t, g):
    tf = t.astype(F32)
    y = tf * lax.rsqrt(jnp.mean(tf * tf, axis=-1, keepdims=True) + NORM_EPS)
    return (y * g.astype(F32)).astype(t.dtype)


def group_layernorm(t, g, b):
    G, C = t.shape[-2], t.shape[-1]
    tf = t.astype(F32)
    mu = jnp.mean(tf, axis=-1, keepdims=True)
    d = tf - mu
    y = d * lax.rsqrt(jnp.mean(d * d, axis=-1, keepdims=True) + LN_EPS)
    return (y * g.astype(F32).reshape(G, C) + b.astype(F32).reshape(G, C)).astype(t.dtype)


def swiglu(h, w_gate, w_up, w_down):
    return (jax.nn.silu(h @ w_gate) * (h @ w_up)) @ w_down


def partial_rotary(t, positions):
    half = ROPE_DIM // 2
    inv_freq = ROPE_THETA ** (-2.0 * jnp.arange(half, dtype=F32) / ROPE_DIM)
    ang = positions.astype(F32)[..., None] * inv_freq
    cos = jnp.cos(ang)[:, :, None, :]
    sin = jnp.sin(ang)[:, :, None, :]
    tf = t.astype(F32)
    x1 = tf[..., :half]
    x2 = tf[..., half:ROPE_DIM]
    out = jnp.concatenate([x1 * cos - x2 * sin, x2 * cos + x1 * sin, tf[..., ROPE_DIM:]], axis=-1)
    return out.astype(t.dtype)


def chunked_spatial_gating(z, ln_g, ln_b, w_s, b_s):
    b, s, _ = z.shape
    u, v = z[..., :D_SGU], z[..., D_SGU:]
    v = group_layernorm(v.reshape(b, s, SGU_GROUPS, SGU_GROUP_DIM), ln_g, ln_b)
    v = v.reshape(b, s // CHUNK, CHUNK, SGU_GROUPS, SGU_GROUP_DIM)
    causal = jnp.tril(jnp.ones((CHUNK, CHUNK), dtype=bool))
    ws = w_s * causal.astype(w_s.dtype)
    mixed = jnp.einsum('gij,bnjgc->bnigc', ws, v) + b_s.T[:, :, None]
    return u * mixed.reshape(b, s, D_SGU)


def dilated_window_attention(q, k, v, window, dilation):
    b, s, h, e = q.shape
    band = window // dilation
    L = s // dilation
    nb = -(-L // band)
    Lp = nb * band
    qs = q.reshape(b, L, dilation, h, e)
    ks = k.reshape(b, L, dilation, h, e)
    vs = v.reshape(b, L, dilation, h, e)
    qb = jnp.pad(qs, ((0, 0), (0, Lp - L), (0, 0), (0, 0), (0, 0))).reshape(b, nb, band, dilation, h, e)

    def kv_blocks(t):
        t = jnp.pad(t, ((0, 0), (band, Lp - L), (0, 0), (0, 0), (0, 0))).reshape(b, nb + 1, band, dilation, h, e)
        return jnp.concatenate([t[:, :-1], t[:, 1:]], axis=2)

    kb, vb = kv_blocks(ks), kv_blocks(vs)
    scores = jnp.einsum('bnqrhe,bnkrhe->bnrhqk', qb.astype(F32), kb.astype(F32)) * (e ** -0.5)
    qi = jnp.arange(band)[:, None]
    kj = jnp.arange(2 * band)[None, :]
    dist = band + qi - kj
    key_pos = (jnp.arange(nb)[:, None, None] - 1) * band + kj[None]
    mask = ((dist >= 0) & (dist <= band))[None] & (key_pos >= 0)
    scores = jnp.where(mask[None, :, None, None], scores, -jnp.inf)
    lse = jax.nn.logsumexp(scores, axis=-1)
    p = jnp.exp(scores - lse[..., None])
    o = jnp.einsum('bnrhqk,bnkrhe->bnqrhe', p.astype(v.dtype), vb)
    o = o.reshape(b, Lp, dilation, h, e)[:, :L].reshape(b, s, h, e)
    lse = jnp.transpose(lse, (0, 1, 4, 2, 3)).reshape(b, Lp, dilation, h)[:, :L].reshape(b, s, h)
    return o, lse


def dilated_mixture_attention(q, k, v):
    outs, lses = [], []
    for window, dilation in DILATED_CONFIGS:
        o, l = dilated_window_attention(q, k, v, window, dilation)
        outs.append(o)
        lses.append(l)
    wts = jax.nn.softmax(jnp.stack(lses, axis=0), axis=0)
    return jnp.einsum('gbsh,gbshe->bshe', wts.astype(q.dtype), jnp.stack(outs, axis=0))


def memory_cross_attention(h, m, wq, wk, wv, wo):
    b, s, _ = h.shape
    nm = m.shape[1]
    q = (h @ wq).reshape(b, s, X_HEADS, X_HEAD_DIM)
    k = (m @ wk).reshape(b, nm, X_HEADS, X_HEAD_DIM)
    v = (m @ wv).reshape(b, nm, X_HEADS, X_HEAD_DIM)
    scores = jnp.einsum('bshe,bmhe->bhsm', q.astype(F32), k.astype(F32)) * (X_HEAD_DIM ** -0.5)
    p = jax.nn.softmax(scores, axis=-1)
    o = jnp.einsum('bhsm,bmhe->bshe', p.astype(v.dtype), v)
    return o.reshape(b, s, D_MODEL) @ wo


def setup_inputs(seed: int = 0) -> dict:
    key = jax.random.key(seed)
    ks = jax.random.split(key, 32)

    def w(k, shape, fan_in):
        return jax.random.normal(k, shape, F32) * (fan_in ** -0.5)

    def gain(k, shape):
        return 1.0 + 0.02 * jax.random.normal(k, shape, F32)

    L = DEPTH
    x = jax.random.normal(ks[0], (BATCH, SEQ, D_MODEL), F32)
    mem = jax.random.normal(ks[1], (BATCH, N_MEM, D_MODEL), F32)
    start = jax.random.randint(ks[2], (BATCH, 1), 0, MAX_POS_OFFSET, dtype=jnp.int32)
    positions = (start + jnp.arange(SEQ, dtype=jnp.int32)[None, :]).astype(jnp.int32)
    return {
        "x": x,
        "mem": mem,
        "positions": positions,
        "ffn1_norm": gain(ks[3], (L, D_MODEL)),
        "ffn1_w_gate": w(ks[4], (L, D_MODEL, D_FF), D_MODEL),
        "ffn1_w_up": w(ks[5], (L, D_MODEL, D_FF), D_MODEL),
        "ffn1_w_down": w(ks[6], (L, D_FF, D_MODEL), D_FF),
        "mix_norm": gain(ks[7], (L, D_MODEL)),
        "w_in": w(ks[8], (L, D_MODEL, D_IN), D_MODEL),
        "sgu_ln_g": gain(ks[9], (L, D_SGU)),
        "sgu_ln_b": 0.02 * jax.random.normal(ks[10], (L, D_SGU), F32),
        "sgu_w_s": w(ks[11], (L, SGU_GROUPS, CHUNK, CHUNK), CHUNK),
        "sgu_b_s": 1.0 + 0.1 * jax.random.normal(ks[12], (L, SGU_GROUPS, CHUNK), F32),
        "out_norm_a": gain(ks[13], (L, D_SGU)),
        "out_norm_b": gain(ks[14], (L, D_ATT)),
        "w_out": w(ks[15], (L, D_MIX, D_MODEL), D_MIX),
        "cross_norm": gain(ks[16], (L, D_MODEL)),
        "mem_norm": gain(ks[17], (L, D_MODEL)),
        "cross_wq": w(ks[18], (L, D_MODEL, D_MODEL), D_MODEL),
        "cross_wk": w(ks[19], (L, D_MODEL, D_MODEL), D_MODEL),
        "cross_wv": w(ks[20], (L, D_MODEL, D_MODEL), D_MODEL),
        "cross_wo": w(ks[21], (L, D_MODEL, D_MODEL), D_MODEL),
        "ffn2_norm": gain(ks[22], (L, D_MODEL)),
        "ffn2_w_gate": w(ks[23], (L, D_MODEL, D_FF), D_MODEL),
        "ffn2_w_up": w(ks[24], (L, D_MODEL, D_FF), D_MODEL),
        "ffn2_w_down": w(ks[25], (L, D_FF, D_MODEL), D_FF),
        "final_norm": gain(ks[26], (D_MODEL,)),
    }


def reference(x, mem, positions, ffn1_norm, ffn1_w_gate, ffn1_w_up, ffn1_w_down,
              mix_norm, w_in, sgu_ln_g, sgu_ln_b, sgu_w_s, sgu_b_s,
              out_norm_a, out_norm_b, w_out, cross_norm, mem_norm,
              cross_wq, cross_wk, cross_wv, cross_wo,
              ffn2_norm, ffn2_w_gate, ffn2_w_up, ffn2_w_down, final_norm):
    b, s, _ = x.shape
    for l in range(DEPTH):
        h = rmsnorm(x, ffn1_norm[l])
        x = x + 0.5 * swiglu(h, ffn1_w_gate[l], ffn1_w_up[l], ffn1_w_down[l])

        h = rmsnorm(x, mix_norm[l])
        z = h @ w_in[l]
        z_sgu = jax.nn.gelu(z[..., :2 * D_SGU], approximate=False)
        o0 = 2 * D_SGU
        q = z[..., o0:o0 + D_ATT].reshape(b, s, N_HEADS, HEAD_DIM)
        k = z[..., o0 + D_ATT:o0 + 2 * D_ATT].reshape(b, s, N_HEADS, HEAD_DIM)
        v = z[..., o0 + 2 * D_ATT:o0 + 3 * D_ATT].reshape(b, s, N_HEADS, HEAD_DIM)
        q = partial_rotary(q, positions)
        k = partial_rotary(k, positions)

        y_a = chunked_spatial_gating(z_sgu, sgu_ln_g[l], sgu_ln_b[l], sgu_w_s[l], sgu_b_s[l])
        y_b = dilated_mixture_attention(q, k, v).reshape(b, s, D_ATT)
        y = jnp.concatenate([rmsnorm(y_a, out_norm_a[l]), rmsnorm(y_b, out_norm_b[l])], axis=-1)
        x = x + y @ w_out[l]

        h = rmsnorm(x, cross_norm[l])
        m = rmsnorm(mem, mem_norm[l])
        x = x + memory_cross_attention(h, m, cross_wq[l], cross_wk[l], cross_wv[l], cross_wo[l])

        h = rmsnorm(x, ffn2_norm[l])
        x = x + 0.5 * swiglu(h, ffn2_w_gate[l], ffn2_w_up[l], ffn2_w_down[l])
    return rmsnorm(x, final_norm)
```

```python
import numpy as np
from contextlib import ExitStack
import concourse.bass as bass
import concourse.mybir as mybir
from concourse.bass_utils import run_bass_kernel_spmd

F32 = mybir.dt.float32
BF16 = mybir.dt.bfloat16
I32 = mybir.dt.int32
AF = mybir.ActivationFunctionType
ALU = mybir.AluOpType
AX = mybir.AxisListType

ENGS = ("pe", "act", "dve", "pool", "sp")

S = 2048
D = 1024
NT = 16
DFF = 2816
NFC = 22
NMEM = 256
EPS = 1e-6
LN_EPS = 1e-5
import os as _os
MIXSTOP = _os.environ.get("MIXSTOP", "")
SAFE_SAME_ENGINE = True


class _Stop(Exception):
    pass


_PROG = [None]


def _stop(tag):
    if MIXSTOP == tag:
        _PROG[0].muted = True


class _Op:
    __slots__ = ("eng", "fn", "deps", "signal", "dma_key", "dma_cnt", "idx", "sigcnt")


class Prog:
    def __init__(self, nc):
        self.nc = nc
        self.ops = {e: [] for e in ENGS}
        self.last_w = {}
        self.readers = {}
        self.dma_cnt = {}
        self.bar_tokens = []
        self.bar_gen = 0
        self.eng_gen = {e: 0 for e in ENGS}
        self.muted = False
        _PROG[0] = self

    def _add(self, eng, fn, reads, writes, signal, dma_key):
        if self.muted:
            return None
        op = _Op()
        op.eng = eng
        op.fn = fn
        op.signal = signal
        op.dma_key = dma_key
        op.idx = len(self.ops[eng])
        deps = []
        if self.eng_gen[eng] < self.bar_gen:
            self.eng_gen[eng] = self.bar_gen
            for tok in self.bar_tokens:
                deps.append((tok, True))
        for r in reads:
            w = self.last_w.get(r)
            if w is not None:
                deps.append((w, True))
        for r in writes:
            w = self.last_w.get(r)
            if w is not None:
                deps.append((w, False))
            for rd in self.readers.get(r, ()):
                deps.append((rd, False))
        op.deps = deps
        if dma_key is not None:
            c = self.dma_cnt.get(dma_key, 0) + 16
            self.dma_cnt[dma_key] = c
            op.dma_cnt = c
            tok = ("d", dma_key, c)
        else:
            tok = ("c", eng, op.idx)
        for r in writes:
            self.last_w[r] = tok
            self.readers[r] = []
        for r in reads:
            self.readers.setdefault(r, []).append(tok)
        self.ops[eng].append(op)
        return op

    def op(self, eng, fn, reads=(), writes=(), signal=True):
        return self._add(eng, fn, tuple(reads), tuple(writes), signal, None)

    def dma(self, fn, key, reads=(), writes=(), eng="sp"):
        return self._add(eng, fn, tuple(reads), tuple(writes), False, key)

    def barrier(self):
        if self.muted:
            return
        toks = []
        for e in ENGS:
            lst = self.ops[e]
            for i in range(len(lst) - 1, -1, -1):
                if lst[i].dma_key is None:
                    lst[i].signal = True
                    toks.append(("c", e, i))
                    break
        for k, c in self.dma_cnt.items():
            toks.append(("d", k, c))
        self.bar_tokens = toks
        self.bar_gen += 1
        self.last_w = {}
        self.readers = {}

    def emit(self):
        nc = self.nc
        sig_at = {}
        for e in ENGS:
            cnt = 0
            lst = self.ops[e]
            for op in lst:
                if op.dma_key is None and op.signal:
                    cnt += 1
                    op.sigcnt = cnt
            need = [None] * len(lst)
            nxt = None
            for i in range(len(lst) - 1, -1, -1):
                op = lst[i]
                if op.dma_key is None and op.signal:
                    nxt = op.sigcnt
                need[i] = nxt
            sig_at[e] = need
        es = ExitStack()
        sems = {}
        for e in ENGS:
            sems[e] = es.enter_context(nc.semaphore("s_" + e))
        dsems = {}
        for i, k in enumerate(self.dma_cnt):
            dsems[k] = es.enter_context(nc.semaphore("d%d" % i))
        block = es.enter_context(nc.Block())

        def run_engine(e, eo):
            waited = {}
            for op in self.ops[e]:
                wl = {}
                for (tok, raw) in op.deps:
                    if tok[0] == "c":
                        pe_, idx = tok[1], tok[2]
                        if pe_ == e:
                            if e == "pe" or e == "sp" or ((not raw) and not SAFE_SAME_ENGINE):
                                continue
                        val = sig_at[pe_][idx]
                        if val is None:
                            raise RuntimeError("dep on unsignaled tail op %s %d" % (pe_, idx))
                        s = sems[pe_]
                        k = ("c", pe_)
                    else:
                        s = dsems[tok[1]]
                        val = tok[2]
                        k = ("d", tok[1])
                    if waited.get(k, 0) >= val:
                        continue
                    if wl.get(k, (None, 0))[1] < val:
                        wl[k] = (s, val)
                for k, (s, val) in wl.items():
                    eo.wait_ge(s, val)
                    waited[k] = val
                ins = op.fn(eo)
                if op.dma_key is not None:
                    ins.then_inc(dsems[op.dma_key], 16)
                elif op.signal:
                    ins.then_inc(sems[e], 1)

        @block.tensor
        def _(eo):
            run_engine("pe", eo)

        @block.scalar
        def _(eo):
            run_engine("act", eo)

        @block.vector
        def _(eo):
            run_engine("dve", eo)

        @block.gpsimd
        def _(eo):
            run_engine("pool", eo)

        @block.sync
        def _(eo):
            run_engine("sp", eo)
            for k, c in self.dma_cnt.items():
                eo.wait_ge(dsems[k], c)

        es.close()


def build_nc(stages=("ffn1", "mix", "cross", "ffn2"), dbg=False):
    nc = bass.Bass("TRN2", target_bir_lowering=False)
    dram_in = lambda name, shape, dt=F32: nc.dram_tensor(name, shape, dt, kind="ExternalInput").ap()
    x_d = dram_in("x", [S, D])
    mem_d = dram_in("mem", [NMEM, D])
    pos_d = dram_in("pos", [128, NT], I32)
    ident_d = dram_in("ident", [128, 128])
    tri_d = dram_in("tri", [128, 128])
    cmask_d = dram_in("cmask", [128, S])
    invf_d = dram_in("invf", [128, 128])
    W = {}
    for nm, sh in [("ffn1_norm", [1, D]), ("ffn1_w_gate", [D, DFF]), ("ffn1_w_up", [D, DFF]), ("ffn1_w_down", [DFF, D]),
                   ("mix_norm", [1, D]), ("w_in", [D, 2560]), ("sgu_ln_g", [1, 512]), ("sgu_ln_b", [1, 512]),
                   ("sgu_w_s", [8, 128, 128]), ("sgu_b_s", [8, 128]), ("out_norm_a", [1, 512]), ("out_norm_b", [1, 512]),
                   ("w_out", [D, D]), ("cross_norm", [1, D]), ("mem_norm", [1, D]), ("cross_wq", [D, D]),
                   ("cross_wk", [D, D]), ("cross_wv", [D, D]), ("cross_wo", [D, D]), ("ffn2_norm", [1, D]),
                   ("ffn2_w_gate", [D, DFF]), ("ffn2_w_up", [D, DFF]), ("ffn2_w_down", [DFF, D]), ("final_norm", [1, D])]:
        W[nm] = dram_in(nm, sh)
    out_d = nc.dram_tensor("out", [S, D], F32, kind="ExternalOutput").ap()

    top = ExitStack()
    _uid = [0]

    def sbt(st, name, shape, dt):
        _uid[0] += 1
        return st.enter_context(nc.sbuf_tensor("s%d_%s" % (_uid[0], name), shape, dt))
    p = Prog(nc)

    X = sbt(top, "X", [128, NT, D], F32)
    HT = sbt(top, "HT", [128, 8, S], BF16)
    identb = sbt(top, "identb", [128, 128], BF16)
    identf = sbt(top, "identf", [128, 128], F32)
    ss = sbt(top, "ss", [128, NT], F32)
    rstd = sbt(top, "rstd", [128, NT], F32)
    gbc = [sbt(top, "gbc%d" % i, [128, D], F32) for i in range(1)]
    xn = [sbt(top, "xn%d" % i, [128, D], BF16) for i in range(2)]
    junk = sbt(top, "junk", [128, D], BF16)
    PSP = [top.enter_context(nc.psum_tensor("psp%d" % i, [128, 2, 512], F32)) for i in range(4)]
    PS = [PSP[i // 2][:, i % 2, :] for i in range(8)]

    def MM(out, lhsT, rhs, start, stop, reads, writes, signal, sgc=False):
        if sgc:
            p.op("pe", lambda e: e.matmul(out, lhsT=lhsT, rhs=rhs, start=start, stop=stop, skip_group_check=True), reads, writes, signal)
        else:
            p.op("pe", lambda e: e.matmul(out, lhsT=lhsT, rhs=rhs, start=start, stop=stop), reads, writes, signal)

    def TR(out, in_, ident, reads, writes, signal=True):
        p.op("pe", lambda e: e.transpose(out=out, in_=in_, identity=ident), reads, writes, signal)

    def ACT(out, in_, func, reads, writes, bias=None, scale=None, accum_out=None):
        kw = {}
        if bias is not None:
            kw["bias"] = bias
        if scale is not None:
            kw["scale"] = scale
        if accum_out is not None:
            kw["accum_out"] = accum_out
        p.op("act", lambda e: e.activation(out=out, in_=in_, func=func, **kw), reads, writes)

    def TT(eng, out, in0, in1, op, reads, writes):
        p.op(eng, lambda e: e.tensor_tensor(out=out, in0=in0, in1=in1, op=op), reads, writes)

    def TS(eng, out, in0, s1, s2, op0, op1, reads, writes):
        if op1 is None:
            p.op(eng, lambda e: e.tensor_scalar(out=out, in0=in0, scalar1=s1, scalar2=None, op0=op0), reads, writes)
        else:
            p.op(eng, lambda e: e.tensor_scalar(out=out, in0=in0, scalar1=s1, scalar2=s2, op0=op0, op1=op1), reads, writes)

    def STT(eng, out, in0, scalar, in1, op0, op1, reads, writes):
        p.op(eng, lambda e: e.scalar_tensor_tensor(out=out, in0=in0, scalar=scalar, in1=in1, op0=op0, op1=op1), reads, writes)

    def CP(eng, out, in_, reads, writes):
        p.op(eng, lambda e: e.tensor_copy(out=out, in_=in_), reads, writes)

    def MEMSET(eng, ap, val, writes):
        p.op(eng, lambda e: e.memset(ap, val), (), writes)

    def DMA(out, in_, key, reads, writes, eng="sp", slow=False):
        if slow:
            p.dma(lambda e: e.dma_start(out=out, in_=in_, allow_slow_non_contiguous=True), key, reads, writes, eng)
        else:
            p.dma(lambda e: e.dma_start(out=out, in_=in_), key, reads, writes, eng)

    def RECIP(out, in_, reads, writes):
        p.op("dve", lambda e: e.reciprocal(out=out, in_=in_), reads, writes)

    def REDUCE(out, in_, reads, writes):
        p.op("dve", lambda e: e.tensor_reduce(out=out, in_=in_, axis=AX.X, op=ALU.add), reads, writes)

    XR = lambda t: [("X", t, 0), ("X", t, 1)]

    for t in range(NT):
        DMA(X[:, t, :], x_d[t * 128:(t + 1) * 128, :], ("ldx", t), (), XR(t))
    DMA(identf[:], ident_d, "ldidf", (), ["identf"])
    DMA(identb[:], ident_d, "ldidb", (), ["identb"], eng="pool")

    gslot = [0]

    def load_gain(name):
        s = 0
        DMA(gbc[s][:], W[name].partition_broadcast(128), ("ldg", s), (), [("gbc", s)], eng="act")
        return s

    def rms_stats(t):
        ACT(junk[:], X[:, t, :], AF.Square, XR(t), ["junk", ("ss", t)], scale=1.0 / 32.0, accum_out=ss[:, t:t + 1])
        ACT(rstd[:, t:t + 1], ss[:, t:t + 1], AF.Sqrt, [("ss", t)], [("rstd", t)], bias=EPS)

    def rms_recip(t):
        p.op("dve", (lambda t: lambda e: e.reciprocal(out=rstd[:, t:t + 1], in_=rstd[:, t:t + 1]))(t), [("rstd", t)], [("rstd", t)])

    norm_state = {"ready": False, "out_done": False}

    def _stepper(apply_a, apply_b):
        def apply(t):
            apply_a(t)
            apply_b(t)

        def pre(t):
            if t >= 2:
                apply_a(t - 2)

        def mid(t):
            if t >= 2:
                apply_b(t - 2)

        def post(t):
            rms_stats(t)
            if t == NT - 1:
                apply(t - 1)
                apply(t)

        def step(t):
            pre(t)
            mid(t)
            post(t)
        step.stats = rms_stats
        step.apply = apply
        step.pre = pre
        step.mid = mid
        step.post = post
        return step

    def make_norm_HT(gain_name):
        def factory(st, tbanks=(6, 7)):
            gs = load_gain(gain_name)
            MEMSET("dve", ss[:], 0.0, [("ss", t) for t in range(NT)])

            def apply_a(t):
                s = t % 2
                rms_recip(t)
                STT("dve", xn[s][:], X[:, t, :], rstd[:, t:t + 1], gbc[gs][:], ALU.mult, ALU.mult,
                    XR(t) + [("rstd", t), ("gbc", gs)], [("xn", s)])

            def apply_b(t):
                s = t % 2
                tbk = tbanks[t % len(tbanks)]
                pst = PS[tbk].bitcast(BF16)
                for c in range(8):
                    TR(pst[:, c * 128:(c + 1) * 128], xn[s][:, c * 128:(c + 1) * 128], identb[:],
                       [("xn", s), "identb"], [("ps", tbk)], signal=(c == 7))
                src = pst[:, :].rearrange("p (c n) -> p c n", c=8)
                if t % 2 == 0 or len(tbanks) == 1:
                    ACT(HT[:, :, t * 128:(t + 1) * 128], src, AF.Copy, [("ps", tbk)], [("HT", t)])
                else:
                    CP("dve", HT[:, :, t * 128:(t + 1) * 128], src, [("ps", tbk)], [("HT", t)])
                if t == NT - 1:
                    norm_state["ready"] = True
            return _stepper(apply_a, apply_b)
        return factory

    def make_norm_out():
        def factory(st, tbanks=None):
            gs = load_gain("final_norm")
            MEMSET("dve", ss[:], 0.0, [("ss", t) for t in range(NT)])
            ot = [sbt(st, "ot%d" % i, [128, D], F32) for i in range(2)]

            def apply_a(t):
                s = t % 2
                rms_recip(t)
                STT("dve", ot[s][:], X[:, t, :], rstd[:, t:t + 1], gbc[gs][:], ALU.mult, ALU.mult,
                    XR(t) + [("rstd", t), ("gbc", gs)], [("ot", s)])

            def apply_b(t):
                s = t % 2
                DMA(out_d[t * 128:(t + 1) * 128, :], ot[s][:], ("st", s), [("ot", s)], ())
                if t == NT - 1:
                    norm_state["out_done"] = True
            return _stepper(apply_a, apply_b)
        return factory

    def ensure_norm(gain_name, st):
        if not norm_state["ready"]:
            step = make_norm_HT(gain_name)(st)
            for t in range(NT):
                step(t)
        norm_state["ready"] = False

    def ffn(pref, tail_factory):
        with ExitStack() as st:
            actT = sbt(st, "actT", [128, 11, S], BF16)
            wdt = sbt(st, "wdt", [128, 11, D], BF16)
            NSL = 4
            wgt = [sbt(st, "wgt%d" % i, [128, 8, 128], BF16) for i in range(NSL)]
            wut = [sbt(st, "wut%d" % i, [128, 8, 128], BF16) for i in range(NSL)]
            sg = [sbt(st, "sg%d" % i, [128, 512], F32) for i in range(2)]
            wgv = W[pref + "_w_gate"].rearrange("(kc p) n -> p kc n", p=128)
            wuv = W[pref + "_w_up"].rearrange("(kc p) n -> p kc n", p=128)
            wdd = W[pref + "_w_down"]

            def issue_gu(fc):
                s = fc % NSL
                DMA(wgt[s][:], wgv[:, :, fc * 128:(fc + 1) * 128], ("wg", s), (), [("wg", s)], eng="pool")
                DMA(wut[s][:], wuv[:, :, fc * 128:(fc + 1) * 128], ("wu", s), (), [("wu", s)], eng="pool")

            for fc in range(NSL - 1):
                issue_gu(fc)
            ensure_norm(pref + "_norm", st)
            step = 0
            for hf in range(2):
                for fcl in range(11):
                    fc = hf * 11 + fcl
                    if fc + NSL - 1 < NFC:
                        issue_gu(fc + NSL - 1)
                    DMA(wdt[:, fcl, :], wdd[fc * 128:(fc + 1) * 128, :], ("wd", fcl), (), [("wd", fcl)], eng="pool")
                    s = fc % NSL
                    for tb in range(4):
                        b = step % 2
                        step += 1
                        g, u = PS[b], PS[2 + b]
                        htr = [("HT", 4 * tb + i) for i in range(4)]
                        for kc in range(8):
                            MM(g[:], wgt[s][:, kc, :], HT[:, kc, tb * 512:(tb + 1) * 512], kc == 0, kc == 7,
                               [("wg", s)] + htr, [("ps", b)], kc == 7)
                        for kc in range(8):
                            MM(u[:], wut[s][:, kc, :], HT[:, kc, tb * 512:(tb + 1) * 512], kc == 0, kc == 7,
                               [("wu", s)] + htr, [("ps", 2 + b)], kc == 7)
                        ACT(sg[b][:], g[:], AF.Silu, [("ps", b)], [("sg", b)])
                        TT("dve", actT[:, fcl, tb * 512:(tb + 1) * 512], sg[b][:], u[:], ALU.mult,
                           [("sg", b), ("ps", 2 + b)], [("actT", fcl, tb)])
                dstep = 0
                tail = tail_factory(st) if hf == 1 else None
                for t in range(NT):
                    if tail is not None:
                        tail.pre(t)
                    for dh in range(2):
                        if tail is not None and dh == 1:
                            tail.mid(t)
                        b = 4 + dstep % 2
                        dstep += 1
                        for fcl in range(11):
                            MM(PS[b][:], actT[:, fcl, t * 128:(t + 1) * 128], wdt[:, fcl, dh * 512:(dh + 1) * 512],
                               fcl == 0, fcl == 10, [("actT", fcl, t // 4), ("wd", fcl)], [("ps", b)], fcl == 10)
                        STT("dve", X[:, t, dh * 512:(dh + 1) * 512], PS[b][:], 0.5, X[:, t, dh * 512:(dh + 1) * 512],
                            ALU.mult, ALU.add, [("ps", b), ("X", t, dh)], [("X", t, dh)])
                    if tail is not None:
                        tail.post(t)
        p.barrier()

    def mixer(tail_factory):
        import math
        wv_in = W["w_in"].rearrange("(kc p) n -> p kc n", p=128)
        with ExitStack() as stM:
            yaT = sbt(stM, "yaT", [128, 4, S], BF16)
            SC = sbt(stM, "SC", [128, 2, NT, 8], F32)
            rstdb = sbt(stM, "rstdb", [128, NT], F32)
            with ExitStack() as stA:
                wblk = [sbt(stA, "wblkA%d" % i, [128, 8, 512], BF16) for i in range(2)]
                DMA(wblk[0][:], wv_in[:, :, 512:1024], ("wb", 0), (), [("wblk", 0)], eng="pool")
                wsn = sbt(stA, "wsn", [128, 8, 128], F32)
                tri = sbt(stA, "tri", [128, 128], F32)
                wsT = sbt(stA, "wsT", [128, 8, 128], BF16)
                bsT = sbt(stA, "bsT", [128, 8], F32)
                lng = sbt(stA, "lng", [128, 512], F32)
                lnb = sbt(stA, "lnb", [128, 512], F32)
                gabc = sbt(stA, "gabc", [128, 512], F32)
                Gs = [sbt(stA, "Gs%d" % i, [128, 4, 512], F32) for i in range(2)]
                Ug = [sbt(stA, "Ug%d" % i, [128, 4, 512], F32) for i in range(2)]
                VLN = [sbt(stA, "VLN%d" % i, [128, 4, 512], BF16) for i in range(2)]
                sqA = sbt(stA, "sqA", [128, 512], F32)
                tmpA = [sbt(stA, "tmpA%d" % i, [128, 512], F32) for i in range(2)]
                yan = [sbt(stA, "yan%d" % i, [128, 512], BF16) for i in range(8)]
                s1 = [sbt(stA, "s1_%d" % i, [128, 32], F32) for i in range(2)]
                s2 = [sbt(stA, "s2_%d" % i, [128, 32], F32) for i in range(2)]
                mean = [sbt(stA, "mean%d" % i, [128, 32], F32) for i in range(2)]
                var = [sbt(stA, "var%d" % i, [128, 32], F32) for i in range(2)]
                rsg = [sbt(stA, "rsg%d" % i, [128, 32], F32) for i in range(2)]
                msa = sbt(stA, "msa", [128, NT], F32)
                rsa = sbt(stA, "rsa", [128, NT], F32)
                DMA(wsn[:], W["sgu_w_s"].rearrange("g i j -> i g j"), "ldws", (), ["wsn"])
                DMA(tri[:], tri_d, "ldtri", (), ["tri"])
                DMA(bsT[:], W["sgu_b_s"].rearrange("g i -> i g"), "ldbs", (), ["bsT"], slow=True)
                DMA(lng[:], W["sgu_ln_g"].partition_broadcast(128), "ldlng", (), ["lng"])
                DMA(lnb[:], W["sgu_ln_b"].partition_broadcast(128), "ldlnb", (), ["lnb"])
                DMA(gabc[:], W["out_norm_a"].partition_broadcast(128), "ldga", (), ["gabc"])
                ensure_norm("mix_norm", stA)
                for g in range(8):
                    b = 4 + g % 2
                    TR(PS[b][:, 0:128], wsn[:, g, :], identf[:], ["wsn", "identf"], [("ps", b)])
                    TT("dve", wsT[:, g, :], PS[b][:, 0:128], tri[:], ALU.mult, [("ps", b), "tri"], [("wsT", g)])
                MEMSET("dve", msa[:], 0.0, ["msa"])
                _stop("A0")
                def stage1a(G):
                    st_ = G % 2
                    for i in range(4):
                        t = 4 * G + i
                        b = t % 2
                        for kc in range(8):
                            MM(PS[b][:], HT[:, kc, t * 128:(t + 1) * 128], wblk[0][:, kc, :], kc == 0, kc == 7,
                               [("HT", t), ("wblk", 0)], [("ps", b)], kc == 7)
                        ACT(Gs[st_][:, i, :], PS[b][:], AF.Gelu, [("ps", b)], [("Gs", st_, i)])
                        REDUCE(s1[st_][:, i * 8:(i + 1) * 8], Gs[st_][:, i, :].rearrange("p (g c) -> p g c", g=8),
                               [("Gs", st_, i)], [("s1", st_, i)])
                        TT("dve", sqA[:], Gs[st_][:, i, :], Gs[st_][:, i, :], ALU.mult, [("Gs", st_, i)], ["sqA"])
                        REDUCE(s2[st_][:, i * 8:(i + 1) * 8], sqA[:].rearrange("p (g c) -> p g c", g=8),
                               ["sqA"], [("s2", st_, i)])
                    _stop("A1")
                    s1r = [("s1", st_, i) for i in range(4)]
                    s2r = [("s2", st_, i) for i in range(4)]
                    TS("dve", mean[st_][:], s1[st_][:], 1.0 / 64, None, ALU.mult, None, s1r, [("mean", st_)])
                    TT("dve", var[st_][:], mean[st_][:], mean[st_][:], ALU.mult, [("mean", st_)], [("var", st_)])
                    STT("dve", var[st_][:], s2[st_][:], 1.0 / 64, var[st_][:], ALU.mult, ALU.subtract,
                        s2r + [("var", st_)], [("var", st_)])
                    ACT(rsg[st_][:], var[st_][:], AF.Sqrt, [("var", st_)], [("rsg", st_)], bias=LN_EPS)

                def stage1b(G):
                    st_ = G % 2
                    RECIP(rsg[st_][:], rsg[st_][:], [("rsg", st_)], [("rsg", st_)])
                    _stop("A2")
                    for i in range(4):
                        ts_ = i % 2
                        g3 = Gs[st_][:, i, :].rearrange("p (g c) -> p g c", g=8)
                        t3 = tmpA[ts_][:].rearrange("p (g c) -> p g c", g=8)
                        TT("dve", t3, g3, mean[st_][:, i * 8:(i + 1) * 8].unsqueeze(2).to_broadcast([128, 8, 64]), ALU.subtract,
                           [("Gs", st_, i), ("mean", st_)], [("tmpA", ts_)])
                        TT("dve", t3, t3, rsg[st_][:, i * 8:(i + 1) * 8].unsqueeze(2).to_broadcast([128, 8, 64]), ALU.mult,
                           [("tmpA", ts_), ("rsg", st_)], [("tmpA", ts_)])
                        TT("pool", tmpA[ts_][:], tmpA[ts_][:], lng[:], ALU.mult, [("tmpA", ts_), "lng"], [("tmpA", ts_)])
                        TT("pool", VLN[st_][:, i, :], tmpA[ts_][:], lnb[:], ALU.add, [("tmpA", ts_), "lnb"], [("VLN", st_, i)])
                    _stop("A3")
                    for i in range(4):
                        t = 4 * G + i
                        b = t % 2
                        for kc in range(8):
                            MM(PS[b][:], HT[:, kc, t * 128:(t + 1) * 128], wblk[1][:, kc, :], kc == 0, kc == 7,
                               [("HT", t), ("wblk", 1)], [("ps", b)], kc == 7)
                        ACT(Ug[st_][:, i, :], PS[b][:], AF.Gelu, [("ps", b)], [("Ug", st_, i)])
                def stage2a(G):
                    st_ = G % 2
                    vr = [("VLN", st_, i) for i in range(4)]
                    ur = [("Ug", st_, i) for i in range(4)]
                    for g in range(8):
                        b = 2 + g % 2
                        MM(PS[b][:, 0:256], wsT[:, g, :], VLN[st_][:, :, g * 64:(g + 1) * 64], True, True,
                           [("wsT", g)] + vr, [("ps", b)], True)
                        STT("dve", Ug[st_][:, :, g * 64:(g + 1) * 64], PS[b][:, 0:256].rearrange("p (n c) -> p n c", n=4),
                            bsT[:, g:g + 1], Ug[st_][:, :, g * 64:(g + 1) * 64], ALU.add, ALU.mult,
                            [("ps", b), "bsT"] + ur, ur)
                    _stop("A5")
                    for i in range(4):
                        t = 4 * G + i
                        ACT(junk[:, 0:512], Ug[st_][:, i, :], AF.Square, [("Ug", st_, i)], ["junk", "msa"],
                            scale=1.0 / math.sqrt(512.0), accum_out=msa[:, t:t + 1])
                    ACT(rsa[:, 4 * G:4 * G + 4], msa[:, 4 * G:4 * G + 4], AF.Sqrt, ["msa"], [("rsa", G)], bias=EPS)

                def stage2b(G):
                    st_ = G % 2
                    RECIP(rsa[:, 4 * G:4 * G + 4], rsa[:, 4 * G:4 * G + 4], [("rsa", G)], [("rsa", G)])
                    _stop("A6")
                    for i in range(4):
                        t = 4 * G + i
                        s = t % 8
                        STT("dve", yan[s][:], Ug[st_][:, i, :], rsa[:, t:t + 1], gabc[:], ALU.mult, ALU.mult,
                            [("Ug", st_, i), ("rsa", G), "gabc"], [("yan", s)])

                def stage3(G):
                    for i in range(4):
                        t = 4 * G + i
                        s = t % 8
                        pst = PS[6 + t % 2].bitcast(BF16)
                        for c in range(4):
                            TR(pst[:, c * 128:(c + 1) * 128], yan[s][:, c * 128:(c + 1) * 128], identb[:],
                               [("yan", s), "identb"], [("ps", 6 + t % 2)], signal=(c == 3))
                        ACT(yaT[:, :, t * 128:(t + 1) * 128], pst[:, 0:512].rearrange("p (c n) -> p c n", c=4), AF.Copy,
                            [("ps", 6 + t % 2)], [("yaT", t)])

                stage1a(0)
                DMA(wblk[1][:], wv_in[:, :, 0:512], ("wb", 1), (), [("wblk", 1)], eng="pool")
                stage1a(1)
                stage1b(0)
                stage2a(0)
                stage1b(1)
                stage2b(0)
                stage1a(2)
                stage2a(1)
                stage3(0)
                stage1b(2)
                stage2b(1)
                stage1a(3)
                stage2a(2)
                stage3(1)
                stage1b(3)
                stage2b(2)
                stage2a(3)
                stage3(2)
                stage2b(3)
                stage3(3)
            p.barrier()
            _stop("A")
            with ExitStack() as stB:
                QT = sbt(stB, "QT", [128, 4, S], BF16)
                KT = sbt(stB, "KT", [128, 4, S], BF16)
                VA = sbt(stB, "VA", [128, NT, 8, 65], BF16)
                with ExitStack() as stB1:
                    wblk = [sbt(stB1, "wblkB%d" % i, [128, 8, 512], BF16) for i in range(2)]
                    DMA(wblk[0][:], wv_in[:, :, 1024:1536], ("wb", 0), (), [("wblk", 0)], eng="pool")
                    DMA(wblk[1][:], wv_in[:, :, 1536:2048], ("wb", 1), (), [("wblk", 1)], eng="pool")
                    posi = sbt(stB1, "posi", [128, NT], I32)
                    posf = sbt(stB1, "posf", [128, NT], F32)
                    invf = sbt(stB1, "invf", [128, 128], F32)
                    ANG = sbt(stB1, "ANG", [128, 2, 128], F32)
                    KF = sbt(stB1, "KF", [128, 256], F32)
                    KI = sbt(stB1, "KI", [128, 256], I32)
                    QR = [sbt(stB1, "QR%d" % i, [128, 512], BF16) for i in range(4)]
                    rt = [sbt(stB1, "rt%d" % i, [128, 8, 8], F32) for i in range(8)]
                    DMA(posi[:], pos_d, "ldpos", (), ["posi"])
                    DMA(invf[:], invf_d, "ldinvf", (), ["invf"])
                    MEMSET("pool", VA[:, :, :, 64:65], 1.0, ["VAones"])
                    CP("dve", posf[:], posi[:], ["posi"], ["posf"])
                    TT("dve", ANG[:, 0, :].rearrange("p (t i) -> p t i", i=8), invf[:].rearrange("p (t i) -> p t i", i=8),
                       posf[:].unsqueeze(2).to_broadcast([128, NT, 8]), ALU.mult, ["invf", "posf"], ["ANG"])
                    TS("dve", ANG[:, 1, :], ANG[:, 0, :], math.pi / 2, None, ALU.add, None, ["ANG"], ["ANG"])
                    A2 = ANG[:].rearrange("p a n -> p (a n)")
                    TS("dve", KF[:], A2, 1.0 / (2 * math.pi), None, ALU.mult, None, ["ANG"], ["KF"])
                    CP("dve", KI[:], KF[:], ["KF"], ["KI"])
                    CP("dve", KF[:], KI[:], ["KI"], ["KF"])
                    C1 = 6.28125
                    C2 = 2 * math.pi - C1
                    STT("dve", A2, KF[:], -C1, A2, ALU.mult, ALU.add, ["KF", "ANG"], ["ANG"])
                    STT("dve", A2, KF[:], -C2, A2, ALU.mult, ALU.add, ["KF", "ANG"], ["ANG"])
                    TS("dve", A2, A2, -math.pi, math.pi, ALU.max, ALU.min, ["ANG"], ["ANG"])
                    ACT(SC[:].rearrange("p a t i -> p (a t i)"), A2, AF.Sin, ["ANG"], ["SC"])

                    _stop("B1a")

                    def rotary(ps, bank, dst, dkey, t, k0):
                        ps3 = ps[:].rearrange("p (h e) -> p h e", h=8)
                        d3 = dst[:].rearrange("p (h e) -> p h e", h=8)
                        sinb = SC[:, 0, t:t + 1, :].to_broadcast([128, 8, 8])
                        cosb = SC[:, 1, t:t + 1, :].to_broadcast([128, 8, 8])
                        pr = [("ps", bank), "SC"]
                        CP("dve", d3[:, :, 16:64], ps3[:, :, 16:64], [("ps", bank)], [dkey])
                        TT("dve", rt[k0][:], ps3[:, :, 0:8], cosb, ALU.mult, pr, [("rt", k0)])
                        TT("dve", rt[k0 + 1][:], ps3[:, :, 8:16], sinb, ALU.mult, pr, [("rt", k0 + 1)])
                        TT("dve", d3[:, :, 0:8], rt[k0][:], rt[k0 + 1][:], ALU.subtract, [("rt", k0), ("rt", k0 + 1)], [dkey])
                        TT("dve", rt[k0 + 2][:], ps3[:, :, 8:16], cosb, ALU.mult, pr, [("rt", k0 + 2)])
                        TT("dve", rt[k0 + 3][:], ps3[:, :, 0:8], sinb, ALU.mult, pr, [("rt", k0 + 3)])
                        TT("dve", d3[:, :, 8:16], rt[k0 + 2][:], rt[k0 + 3][:], ALU.add, [("rt", k0 + 2), ("rt", k0 + 3)], [dkey])

                    def projqk(t):
                        for qk in range(2):
                            b = qk * 2 + t % 2
                            for kc in range(8):
                                MM(PS[b][:], HT[:, kc, t * 128:(t + 1) * 128], wblk[qk][:, kc, :], kc == 0, kc == 7,
                                   [("HT", t), ("wblk", qk)], [("ps", b)], kc == 7)
                            qs = qk * 2 + t % 2
                            rotary(PS[b], b, QR[qs], ("QR", qs), t, 4 * qk)

                    def trqk(t):
                        for qk in range(2):
                            qs = qk * 2 + t % 2
                            tb_ = 6 + qk
                            pst = PS[tb_].bitcast(BF16)
                            for c in range(4):
                                TR(pst[:, c * 128:(c + 1) * 128], QR[qs][:, c * 128:(c + 1) * 128], identb[:],
                                   [("QR", qs), "identb"], [("ps", tb_)], signal=(c == 3))
                            dstT = QT if qk == 0 else KT
                            ACT(dstT[:, :, t * 128:(t + 1) * 128], pst[:, 0:512].rearrange("p (c n) -> p c n", c=4),
                                AF.Copy, [("ps", tb_)], [("QKT", qk, t)])

                    projqk(0)
                    for t in range(NT):
                        if t + 1 < NT:
                            projqk(t + 1)
                        trqk(t)
                    _stop("B1c")
                    DMA(wblk[0][:], wv_in[:, :, 2048:2560], ("wb", 0), (), [("wblk", 0)], eng="pool")
                    for t in range(NT):
                        b = 4 + t % 2
                        for kc in range(8):
                            MM(PS[b][:], HT[:, kc, t * 128:(t + 1) * 128], wblk[0][:, kc, :], kc == 0, kc == 7,
                               [("HT", t), ("wblk", 0)], [("ps", b)], kc == 7)
                        ACT(VA[:, t, :, 0:64], PS[b][:].rearrange("p (h e) -> p h e", h=8), AF.Copy,
                            [("ps", b)], [("VA", t)])
                p.barrier()
                _stop("B1")
                with ExitStack() as stB2:
                    cm = sbt(stB2, "cm", [128, S], BF16)
                    NPB2 = 4
                    pb2 = [sbt(stB2, "pb%d" % i, [128, 2, 512], BF16) for i in range(NPB2)]
                    YB = [sbt(stB2, "YB%d" % i, [128, 4, 512], F32) for i in range(2)]
                    rzt = [sbt(stB2, "rzt%d" % i, [128, 4], F32) for i in range(4)]
                    ybn = [sbt(stB2, "ybn%d" % i, [128, 512], BF16) for i in range(2)]
                    gbbc = sbt(stB2, "gbbc", [128, 512], F32)
                    msb = sbt(stB2, "msb", [128, NT], F32)
                    YTb = HT[:, 0:4, :]
                    WO = HT[:, 4:8, :].rearrange("p a (b n) -> p (a b) n", n=1024)
                    DMA(cm[:], cmask_d, "ldcm", (), ["cm"], eng="pool")
                    DMA(gbbc[:], W["out_norm_b"].partition_broadcast(128), "ldgb", (), ["gbbc"])
                    DMA(WO, W["w_out"].rearrange("(kc p) n -> p kc n", p=128), "ldwo", (), ["WO"], eng="pool")
                    MEMSET("dve", msb[:], 0.0, ["msb"])
                    steps = []
                    for b in range(4):
                        for c in range(4):
                            nj = 4 * b + 4
                            for j in range(nj):
                                steps.append((b, c, j, nj))
                    LA = 2
                    SBK = [(0, 1), (6, 7)]
                    pending = []

                    def front(i):
                        b, c, j, nj = steps[i]
                        qlo = max(512 * b, 128 * j)
                        N = 512 * (b + 1) - qlo
                        for e in range(2):
                            r0 = e * 64
                            sb_ = SBK[i % 2][e]
                            MM(PS[sb_][:, 0:N], KT[r0:r0 + 64, c, j * 128:(j + 1) * 128], QT[r0:r0 + 64, c, qlo:qlo + N],
                               True, True, (), [("ps", sb_)], True)
                        pp = PSP[SBK[i % 2][0] // 2]
                        ps_ = i % NPB2
                        prs = [("ps", SBK[i % 2][0]), ("ps", SBK[i % 2][1])]
                        ACT(pb2[ps_][:, :, 0:N], pp[:, :, 0:N], AF.Exp, prs, [("pb", ps_)], scale=0.125)
                        TT("pool" if i % 3 == 0 else "dve", pb2[ps_][:, :, 0:N], pb2[ps_][:, :, 0:N],
                           cm[:, qlo - 128 * j:qlo - 128 * j + N].unsqueeze(1).to_broadcast([128, 2, N]), ALU.mult,
                           [("pb", ps_), "cm"], [("pb", ps_)])

                    def back(i):
                        b, c, j, nj = steps[i]
                        qlo = max(512 * b, 128 * j)
                        N = 512 * (b + 1) - qlo
                        nqb = N // 128
                        qb0 = 4 - nqb
                        g = b * 4 + c
                        for e in range(2):
                            h = 2 * c + e
                            ab = (2, 3)[e] if g % 2 == 0 else (4, 5)[e]
                            ps_ = i % NPB2
                            for qi in range(nqb):
                                qb = qb0 + qi
                                last = (qi == nqb - 1)
                                MM(PS[ab][:, qb * 65:(qb + 1) * 65], pb2[ps_][:, e, qi * 128:(qi + 1) * 128], VA[:, j, h, :],
                                   (j == 0 and qi == 0), (j == nj - 1), [("pb", ps_), "VAones"], [("ps", ab)],
                                   last and (j == nj - 1), sgc=True)
                        if j == nj - 1:
                            pending.append((i + 1, epi, b, c))

                    def epi(b, c):
                        g = b * 4 + c
                        ys = b % 2
                        for e in range(2):
                            h = 2 * c + e
                            ab = (2, 3)[e] if g % 2 == 0 else (4, 5)[e]
                            acc3 = PS[ab][:, 0:260].rearrange("p (q e) -> p q e", q=4)
                            rs = (2 * g + e) % 4
                            RECIP(rzt[rs][:].unsqueeze(2), acc3[:, :, 64:65], [("ps", ab)], [("rzt", rs)])
                            TT("dve", YB[ys][:, :, h * 64:(h + 1) * 64], acc3[:, :, 0:64],
                               rzt[rs][:].unsqueeze(2).to_broadcast([128, 4, 64]), ALU.mult,
                               [("ps", ab), ("rzt", rs)], [("YB", ys, h)])
                        if c == 3:
                            pending.append((0, bank_tail, b, 0))

                    def bank_tail(b, _):
                        ys = b % 2
                        ybr_ = [("YB", ys, h) for h in range(8)]
                        for i4 in range(4):
                            t = 4 * b + i4
                            s = t % 2
                            ACT(junk[:, 0:512], YB[ys][:, i4, :], AF.Square, ybr_, ["junk", "msb"],
                                scale=1.0 / math.sqrt(512.0), accum_out=msb[:, t:t + 1])
                            TT("pool", ybn[s][:], YB[ys][:, i4, :], gbbc[:], ALU.mult, ybr_ + ["gbbc"], [("ybn", s)])
                            tbk = 6 + s
                            pst = PS[tbk].bitcast(BF16)
                            for cc in range(4):
                                TR(pst[:, cc * 128:(cc + 1) * 128], ybn[s][:, cc * 128:(cc + 1) * 128], identb[:],
                                   [("ybn", s), "identb"], [("ps", tbk)], signal=(cc == 3))
                            ACT(YTb[:, :, t * 128:(t + 1) * 128], pst[:, 0:512].rearrange("p (c n) -> p c n", c=4),
                                AF.Copy, [("ps", tbk)], [("YTb", t)])

                    nst = len(steps)
                    for i in range(nst + LA + 4):
                        if i < nst:
                            front(i)
                        if 0 <= i - LA < nst:
                            back(i - LA)
                        due = [q for q in pending if q[0] <= i - LA]
                        for q in due:
                            pending.remove(q)
                            q[1](q[2], q[3])
                    while pending:
                        q = pending.pop(0)
                        q[1](q[2], q[3])
                    _stop("B2a")
                    ACT(rstdb[:], msb[:], AF.Sqrt, ["msb"], ["rstdb"], bias=EPS)
                    RECIP(rstdb[:], rstdb[:], ["rstdb"], ["rstdb"])
                    ybr = []
                    k = 0
                    tail = tail_factory(stB2)
                    for t in range(NT):
                        for dh in range(2):
                            ba = k % 2
                            bb = 2 + k % 2
                            k += 1
                            for c in range(4):
                                MM(PS[ba][:], yaT[:, c, t * 128:(t + 1) * 128], WO[:, c, dh * 512:(dh + 1) * 512],
                                   c == 0, c == 3, ["WO"], [("ps", ba)], c == 3)
                            for c in range(4):
                                MM(PS[bb][:], YTb[:, c, t * 128:(t + 1) * 128], WO[:, 4 + c, dh * 512:(dh + 1) * 512],
                                   c == 0, c == 3, ["WO", ("YTb", t)], [("ps", bb)], c == 3)
                            xs = X[:, t, dh * 512:(dh + 1) * 512]
                            TT("dve", xs, xs, PS[ba][:], ALU.add, [("ps", ba), ("X", t, dh)], [("X", t, dh)])
                            STT("dve", xs, PS[bb][:], rstdb[:, t:t + 1], xs, ALU.mult, ALU.add,
                                [("ps", bb), ("X", t, dh), "rstdb"], [("X", t, dh)])
                        tail.stats(t)
                    p.barrier()
                    for t in range(NT):
                        tail.apply(t)
        p.barrier()

    def cross(tail_factory):
        with ExitStack() as st:
            ensure_norm("cross_norm", st)
            memt = sbt(st, "memt", [128, 2, D], F32)
            MT = sbt(st, "MT", [128, 8, NMEM], BF16)
            KxT = sbt(st, "KxT", [128, 8, NMEM], BF16)
            Vx = sbt(st, "Vx", [128, 2, D], BF16)
            QxT = sbt(st, "QxT", [128, 8, S], BF16)
            wsl = [sbt(st, "wx%d" % i, [128, 8, 512], BF16) for i in range(3)]
            onesb = sbt(st, "onesb", [128, 128], BF16)
            pbx = [sbt(st, "pbx%d" % i, [128, 512], BF16) for i in range(4)]
            rzx = [sbt(st, "rzx%d" % i, [128, 512], F32) for i in range(2)]
            mss = sbt(st, "mss", [128, 2], F32)
            mrs = sbt(st, "mrs", [128, 2], F32)
            wplan = [("cross_wq", 0), ("cross_wk", 0), ("cross_wk", 1), ("cross_wv", 0), ("cross_wv", 1),
                     ("cross_wq", 1), ("cross_wo", 0), ("cross_wo", 1)]
            sl_of = {}

            def load_w(i):
                sl = i % 3
                nm, blk = wplan[i]
                src = W[nm].rearrange("(kc p) n -> p kc n", p=128)[:, :, blk * 512:(blk + 1) * 512]
                DMA(wsl[sl][:], src, ("wx", sl), (), [("wx", sl)], eng="pool")
                sl_of[i] = sl

            load_w(0)
            MEMSET("pool", onesb[:], 1.0, ["onesb"])
            kk = [0]

            def QP(hc, tb):
                sl = sl_of[0] if hc < 4 else sl_of[5]
                b = kk[0] % 2
                kk[0] += 1
                htr = [("HT", 4 * tb + i) for i in range(4)]
                for kc in range(8):
                    MM(PS[b][:], wsl[sl][:, kc, (hc % 4) * 128:(hc % 4 + 1) * 128], HT[:, kc, tb * 512:(tb + 1) * 512],
                       kc == 0, kc == 7, [("wx", sl)] + htr, [("ps", b)], kc == 7)
                if kk[0] % 2 == 0 or hc >= 4:
                    ACT(QxT[:, hc, tb * 512:(tb + 1) * 512], PS[b][:], AF.Copy, [("ps", b)], [("QxT", hc, tb)])
                else:
                    CP("dve", QxT[:, hc, tb * 512:(tb + 1) * 512], PS[b][:], [("ps", b)], [("QxT", hc, tb)])

            for hc in range(4):
                for tb in range(4):
                    QP(hc, tb)
                if hc == 0:
                    load_w(1)
                    load_w(2)
                    for mt in range(2):
                        DMA(memt[:, mt, :], mem_d[mt * 128:(mt + 1) * 128, :], ("ldm", mt), (), [("memt", mt)])
            load_w(3)
            gm = load_gain("mem_norm")
            MEMSET("dve", mss[:], 0.0, ["mss"])
            for mt in range(2):
                ACT(junk[:], memt[:, mt, :], AF.Square, [("memt", mt)], ["junk", "mss"], scale=1.0 / 32.0,
                    accum_out=mss[:, mt:mt + 1])
            ACT(mrs[:], mss[:], AF.Sqrt, ["mss"], ["mrs"], bias=EPS)
            RECIP(mrs[:], mrs[:], ["mrs"], ["mrs"])
            for mt in range(2):
                s = mt % 2
                STT("dve", xn[s][:], memt[:, mt, :], mrs[:, mt:mt + 1], gbc[gm][:], ALU.mult, ALU.mult,
                    [("memt", mt), "mrs", ("gbc", gm)], [("xn", s)])
                pst = PS[6 + s].bitcast(BF16)
                for c in range(8):
                    TR(pst[:, c * 128:(c + 1) * 128], xn[s][:, c * 128:(c + 1) * 128], identb[:],
                       [("xn", s), "identb"], [("ps", 6 + s)], signal=(c == 7))
                ACT(MT[:, :, mt * 128:(mt + 1) * 128], pst[:, :].rearrange("p (c n) -> p c n", c=8), AF.Copy,
                    [("ps", 6 + s)], [("MT", mt)])
            mtr = [("MT", 0), ("MT", 1)]
            for hc in range(8):
                sl = sl_of[1 + hc // 4]
                b = kk[0] % 2
                kk[0] += 1
                for kc in range(8):
                    MM(PS[b][:, 0:NMEM], wsl[sl][:, kc, (hc % 4) * 128:(hc % 4 + 1) * 128], MT[:, kc, :], kc == 0, kc == 7,
                       [("wx", sl)] + mtr, [("ps", b)], kc == 7)
                ACT(KxT[:, hc, :], PS[b][:, 0:NMEM], AF.Copy, [("ps", b)], [("KxT", hc)])
                if hc == 3:
                    load_w(4)
            load_w(5)
            for dh in range(2):
                sl = sl_of[3 + dh]
                for mt in range(2):
                    b = kk[0] % 2
                    kk[0] += 1
                    for kc in range(8):
                        MM(PS[b][:], MT[:, kc, mt * 128:(mt + 1) * 128], wsl[sl][:, kc, :], kc == 0, kc == 7,
                           [("wx", sl), ("MT", mt)], [("ps", b)], kc == 7)
                    ACT(Vx[:, mt, dh * 512:(dh + 1) * 512], PS[b][:], AF.Copy, [("ps", b)], [("Vx", mt, dh)])
                load_w(6 + dh)
            itc = [0]

            def ATT(h, tb, alt=False):
                par = itc[0] % 2
                itc[0] += 1
                sbk = (2, 3)
                ob = (4, 5)
                zb = 7
                if alt and par == 1:
                    ob = (0, 1)
                    zb = 6
                cols = slice(tb * 512, (tb + 1) * 512)
                for mt in range(2):
                    for ec in range(2):
                        MM(PS[sbk[mt]][:], KxT[:, 2 * h + ec, mt * 128:(mt + 1) * 128], QxT[:, 2 * h + ec, cols], ec == 0, ec == 1,
                           [("KxT", 2 * h + ec), ("QxT", 2 * h + ec, tb)], [("ps", sbk[mt])], ec == 1)
                    ACT(pbx[par * 2 + mt][:], PS[sbk[mt]][:], AF.Exp, [("ps", sbk[mt])], [("pbx", par * 2 + mt)], scale=1.0 / 16.0)
                for ec in range(2):
                    for mt in range(2):
                        MM(PS[ob[ec]][:], Vx[:, mt, (2 * h + ec) * 128:(2 * h + ec + 1) * 128], pbx[par * 2 + mt][:], mt == 0, mt == 1,
                           [("Vx", mt, (2 * h + ec) // 4), ("pbx", par * 2 + mt)], [("ps", ob[ec])], mt == 1)
                for mt in range(2):
                    MM(PS[zb][:], onesb[:], pbx[par * 2 + mt][:], mt == 0, mt == 1, ["onesb", ("pbx", par * 2 + mt)],
                       [("ps", zb)], mt == 1)
                RECIP(rzx[par][:], PS[zb][:], [("ps", zb)], [("rzx", par)])
                for ec in range(2):
                    TT("dve", QxT[:, 2 * h + ec, cols], PS[ob[ec]][:], rzx[par][:], ALU.mult,
                       [("ps", ob[ec]), ("rzx", par)], [("QxT", 2 * h + ec, tb)])

            qpl = [(hc, tb) for hc in range(4, 8) for tb in range(4)]
            atl = [(h, tb) for h in range(2) for tb in range(4)]
            for i, (hc, tb) in enumerate(qpl):
                QP(hc, tb)
                if i % 2 == 1:
                    ATT(*atl[i // 2])
            for h in range(2, 4):
                for tb in range(4):
                    ATT(h, tb, alt=True)
            tail = tail_factory(st)
            for t in range(NT):
                tail.pre(t)
                for dh in range(2):
                    if dh == 1:
                        tail.mid(t)
                    b = kk[0] % 2
                    kk[0] += 1
                    sl = sl_of[6 + dh]
                    for hc in range(8):
                        MM(PS[b][:], QxT[:, hc, t * 128:(t + 1) * 128], wsl[sl][:, hc, :], hc == 0, hc == 7,
                           [("QxT", hc, t // 4), ("wx", sl)], [("ps", b)], hc == 7)
                    xs = X[:, t, dh * 512:(dh + 1) * 512]
                    TT("dve", xs, xs, PS[b][:], ALU.add, [("ps", b), ("X", t, dh)], [("X", t, dh)])
                tail.post(t)
        p.barrier()

    seq = [q for q in ("ffn1", "mix", "cross", "ffn2") if q in stages]
    gains = {"ffn1": "ffn1_norm", "mix": "mix_norm", "cross": "cross_norm", "ffn2": "ffn2_norm"}
    for i, sname in enumerate(seq):
        nxt = seq[i + 1] if i + 1 < len(seq) else None
        tf = make_norm_HT(gains[nxt]) if nxt else make_norm_out()
        if sname in ("ffn1", "ffn2"):
            ffn(sname, tf)
        elif sname == "mix":
            mixer(tf)
            p.muted = False
            p.barrier()
        else:
            cross(tf)
    if not norm_state["out_done"]:
        with ExitStack() as st:
            step = make_norm_out()(st)
            for t in range(NT):
                step(t)
    p.emit()
    top.close()
    return nc


_CACHE = {}


def _consts():
    ident = np.eye(128, dtype=np.float32)
    j = np.arange(128)[:, None]
    i = np.arange(128)[None, :]
    tri = (j <= i).astype(np.float32)
    d = np.arange(S)[None, :] - np.arange(128)[:, None]
    cm = ((d >= 0) & (d <= 128)).astype(np.float32) + ((d >= 0) & (d % 4 == 0) & (d <= 512)).astype(np.float32) \
        + ((d >= 0) & (d % 16 == 0)).astype(np.float32)
    ex = (-2.0 * np.arange(8, dtype=np.float32) / np.float32(16.0)).astype(np.float32)
    invf = np.power(np.float32(500000.0), ex).astype(np.float32)
    invf_t = np.tile(invf[None, :], (128, 16)).astype(np.float32)
    return {"ident": ident, "tri": tri, "cmask": cm.astype(np.float32), "invf": invf_t}


def kernel(**inputs):
    n = 8
    if "nc" not in _CACHE:
        _CACHE["nc"] = build_nc()
    nc = _CACHE["nc"]
    cst = _consts()
    shared = {}
    for k, v in inputs.items():
        if k in ("x", "mem", "positions"):
            continue
        a = np.asarray(v)
        if k == "final_norm":
            a = a.reshape(1, D)
        elif k in ("sgu_w_s", "sgu_b_s"):
            a = a[0]
        elif a.ndim == 2:
            pass
        else:
            a = a[0]
        shared[k] = np.ascontiguousarray(a, dtype=np.float32)
    x = np.asarray(inputs["x"], dtype=np.float32)
    mem = np.asarray(inputs["mem"], dtype=np.float32)
    pos = np.asarray(inputs["positions"], dtype=np.int32)
    in_maps = []
    for b in range(n):
        m = dict(shared)
        m.update(cst)
        m["x"] = np.ascontiguousarray(x[b])
        m["mem"] = np.ascontiguousarray(mem[b])
        m["pos"] = np.ascontiguousarray(pos[b].reshape(NT, 128).T)
        in_maps.append(m)
    res = run_bass_kernel_spmd(nc, in_maps, core_ids=list(range(n)))
    return np.stack([np.asarray(r["out"], dtype=np.float32) for r in res.results], axis=0)
```

```python
import numpy as np
from contextlib import ExitStack
import concourse.bass as bass
import concourse.mybir as mybir
from concourse.bass_utils import run_bass_kernel_spmd

F32 = mybir.dt.float32
BF16 = mybir.dt.bfloat16
I32 = mybir.dt.int32
AF = mybir.ActivationFunctionType
ALU = mybir.AluOpType
AX = mybir.AxisListType

ENGS = ("pe", "act", "dve", "pool", "sp")

S = 2048
D = 1024
NT = 16
DFF = 2816
NFC = 22
NMEM = 256
EPS = 1e-6
LN_EPS = 1e-5
import os as _os
MIXSTOP = _os.environ.get("MIXSTOP", "")
SAFE_SAME_ENGINE = True


class _Stop(Exception):
    pass


_PROG = [None]


def _stop(tag):
    if MIXSTOP == tag:
        _PROG[0].muted = True


class _Op:
    __slots__ = ("eng", "fn", "deps", "signal", "dma_key", "dma_cnt", "idx", "sigcnt")


class Prog:
    def __init__(self, nc):
        self.nc = nc
        self.ops = {e: [] for e in ENGS}
        self.last_w = {}
        self.readers = {}
        self.dma_cnt = {}
        self.bar_tokens = []
        self.bar_gen = 0
        self.eng_gen = {e: 0 for e in ENGS}
        self.muted = False
        _PROG[0] = self

    def _add(self, eng, fn, reads, writes, signal, dma_key):
        if self.muted:
            return None
        op = _Op()
        op.eng = eng
        op.fn = fn
        op.signal = signal
        op.dma_key = dma_key
        op.idx = len(self.ops[eng])
        deps = []
        if self.eng_gen[eng] < self.bar_gen:
            self.eng_gen[eng] = self.bar_gen
            for tok in self.bar_tokens:
                deps.append((tok, True))
        for r in reads:
            w = self.last_w.get(r)
            if w is not None:
                deps.append((w, True))
        for r in writes:
            w = self.last_w.get(r)
            if w is not None:
                deps.append((w, False))
            for rd in self.readers.get(r, ()):
                deps.append((rd, False))
        op.deps = deps
        if dma_key is not None:
            c = self.dma_cnt.get(dma_key, 0) + 16
            self.dma_cnt[dma_key] = c
            op.dma_cnt = c
            tok = ("d", dma_key, c)
        else:
            tok = ("c", eng, op.idx)
        for r in writes:
            self.last_w[r] = tok
            self.readers[r] = []
        for r in reads:
            self.readers.setdefault(r, []).append(tok)
        self.ops[eng].append(op)
        return op

    def op(self, eng, fn, reads=(), writes=(), signal=True):
        return self._add(eng, fn, tuple(reads), tuple(writes), signal, None)

    def dma(self, fn, key, reads=(), writes=(), eng="sp"):
        return self._add(eng, fn, tuple(reads), tuple(writes), False, key)

    def barrier(self):
        if self.muted:
            return
        toks = []
        for e in ENGS:
            lst = self.ops[e]
            for i in range(len(lst) - 1, -1, -1):
                if lst[i].dma_key is None:
                    lst[i].signal = True
                    toks.append(("c", e, i))
                    break
        for k, c in self.dma_cnt.items():
            toks.append(("d", k, c))
        self.bar_tokens = toks
        self.bar_gen += 1
        self.last_w = {}
        self.readers = {}

    def emit(self):
        nc = self.nc
        sig_at = {}
        for e in ENGS:
            cnt = 0
            lst = self.ops[e]
            for op in lst:
                if op.dma_key is None and op.signal:
                    cnt += 1
                    op.sigcnt = cnt
            need = [None] * len(lst)
            nxt = None
            for i in range(len(lst) - 1, -1, -1):
                op = lst[i]
                if op.dma_key is None and op.signal:
                    nxt = op.sigcnt
                need[i] = nxt
            sig_at[e] = need
        es = ExitStack()
        sems = {}
        for e in ENGS:
            sems[e] = es.enter_context(nc.semaphore("s_" + e))
        dsems = {}
        for i, k in enumerate(self.dma_cnt):
            dsems[k] = es.enter_context(nc.semaphore("d%d" % i))
        block = es.enter_context(nc.Block())

        def run_engine(e, eo):
            waited = {}
            for op in self.ops[e]:
                wl = {}
                for (tok, raw) in op.deps:
                    if tok[0] == "c":
                        pe_, idx = tok[1], tok[2]
                        if pe_ == e:
                            if e == "pe" or e == "sp" or ((not raw) and not SAFE_SAME_ENGINE):
                                continue
                        val = sig_at[pe_][idx]
                        if val is None:
                            raise RuntimeError("dep on unsignaled tail op %s %d" % (pe_, idx))
                        s = sems[pe_]
                        k = ("c", pe_)
                    else:
                        s = dsems[tok[1]]
                        val = tok[2]
                        k = ("d", tok[1])
                    if waited.get(k, 0) >= val:
                        continue
                    if wl.get(k, (None, 0))[1] < val:
                        wl[k] = (s, val)
                for k, (s, val) in wl.items():
                    eo.wait_ge(s, val)
                    waited[k] = val
                ins = op.fn(eo)
                if op.dma_key is not None:
                    ins.then_inc(dsems[op.dma_key], 16)
                elif op.signal:
                    ins.then_inc(sems[e], 1)

        @block.tensor
        def _(eo):
            run_engine("pe", eo)

        @block.scalar
        def _(eo):
            run_engine("act", eo)

        @block.vector
        def _(eo):
            run_engine("dve", eo)

        @block.gpsimd
        def _(eo):
            run_engine("pool", eo)

        @block.sync
        def _(eo):
            run_engine("sp", eo)
            for k, c in self.dma_cnt.items():
                eo.wait_ge(dsems[k], c)

        es.close()


def build_nc(stages=("ffn1", "mix", "cross", "ffn2"), dbg=False):
    nc = bass.Bass("TRN2", target_bir_lowering=False)
    dram_in = lambda name, shape, dt=F32: nc.dram_tensor(name, shape, dt, kind="ExternalInput").ap()
    x_d = dram_in("x", [S, D])
    mem_d = dram_in("mem", [NMEM, D])
    pos_d = dram_in("pos", [128, NT], I32)
    ident_d = dram_in("ident", [128, 128])
    tri_d = dram_in("tri", [128, 128])
    cmask_d = dram_in("cmask", [128, S])
    invf_d = dram_in("invf", [128, 128])
    W = {}
    for nm, sh in [("ffn1_norm", [1, D]), ("ffn1_w_gate", [D, DFF]), ("ffn1_w_up", [D, DFF]), ("ffn1_w_down", [DFF, D]),
                   ("mix_norm", [1, D]), ("w_in", [D, 2560]), ("sgu_ln_g", [1, 512]), ("sgu_ln_b", [1, 512]),
                   ("sgu_w_s", [8, 128, 128]), ("sgu_b_s", [8, 128]), ("out_norm_a", [1, 512]), ("out_norm_b", [1, 512]),
                   ("w_out", [D, D]), ("cross_norm", [1, D]), ("mem_norm", [1, D]), ("cross_wq", [D, D]),
                   ("cross_wk", [D, D]), ("cross_wv", [D, D]), ("cross_wo", [D, D]), ("ffn2_norm", [1, D]),
                   ("ffn2_w_gate", [D, DFF]), ("ffn2_w_up", [D, DFF]), ("ffn2_w_down", [DFF, D]), ("final_norm", [1, D])]:
        W[nm] = dram_in(nm, sh)
    out_d = nc.dram_tensor("out", [S, D], F32, kind="ExternalOutput").ap()

    top = ExitStack()
    _uid = [0]

    def sbt(st, name, shape, dt):
        _uid[0] += 1
        return st.enter_context(nc.sbuf_tensor("s%d_%s" % (_uid[0], name), shape, dt))
    p = Prog(nc)

    X = sbt(top, "X", [128, NT, D], F32)
    HT = sbt(top, "HT", [128, 8, S], BF16)
    identb = sbt(top, "identb", [128, 128], BF16)
    identf = sbt(top, "identf", [128, 128], F32)
    ss = sbt(top, "ss", [128, NT], F32)
    rstd = sbt(top, "rstd", [128, NT], F32)
    gbc = [sbt(top, "gbc%d" % i, [128, D], F32) for i in range(1)]
    xn = [sbt(top, "xn%d" % i, [128, D], BF16) for i in range(2)]
    junk = sbt(top, "junk", [128, D], BF16)
    PSP = [top.enter_context(nc.psum_tensor("psp%d" % i, [128, 2, 512], F32)) for i in range(4)]
    PS = [PSP[i // 2][:, i % 2, :] for i in range(8)]

    def MM(out, lhsT, rhs, start, stop, reads, writes, signal, sgc=False):
        if sgc:
            p.op("pe", lambda e: e.matmul(out, lhsT=lhsT, rhs=rhs, start=start, stop=stop, skip_group_check=True), reads, writes, signal)
        else:
            p.op("pe", lambda e: e.matmul(out, lhsT=lhsT, rhs=rhs, start=start, stop=stop), reads, writes, signal)

    def TR(out, in_, ident, reads, writes, signal=True):
        p.op("pe", lambda e: e.transpose(out=out, in_=in_, identity=ident), reads, writes, signal)

    def ACT(out, in_, func, reads, writes, bias=None, scale=None, accum_out=None):
        kw = {}
        if bias is not None:
            kw["bias"] = bias
        if scale is not None:
            kw["scale"] = scale
        if accum_out is not None:
            kw["accum_out"] = accum_out
        p.op("act", lambda e: e.activation(out=out, in_=in_, func=func, **kw), reads, writes)

    def TT(eng, out, in0, in1, op, reads, writes):
        p.op(eng, lambda e: e.tensor_tensor(out=out, in0=in0, in1=in1, op=op), reads, writes)

    def TS(eng, out, in0, s1, s2, op0, op1, reads, writes):
        if op1 is None:
            p.op(eng, lambda e: e.tensor_scalar(out=out, in0=in0, scalar1=s1, scalar2=None, op0=op0), reads, writes)
        else:
            p.op(eng, lambda e: e.tensor_scalar(out=out, in0=in0, scalar1=s1, scalar2=s2, op0=op0, op1=op1), reads, writes)

    def STT(eng, out, in0, scalar, in1, op0, op1, reads, writes):
        p.op(eng, lambda e: e.scalar_tensor_tensor(out=out, in0=in0, scalar=scalar, in1=in1, op0=op0, op1=op1), reads, writes)

    def CP(eng, out, in_, reads, writes):
        p.op(eng, lambda e: e.tensor_copy(out=out, in_=in_), reads, writes)

    def MEMSET(eng, ap, val, writes):
        p.op(eng, lambda e: e.memset(ap, val), (), writes)

    def DMA(out, in_, key, reads, writes, eng="sp", slow=False):
        if slow:
            p.dma(lambda e: e.dma_start(out=out, in_=in_, allow_slow_non_contiguous=True), key, reads, writes, eng)
        else:
            p.dma(lambda e: e.dma_start(out=out, in_=in_), key, reads, writes, eng)

    def RECIP(out, in_, reads, writes):
        p.op("dve", lambda e: e.reciprocal(out=out, in_=in_), reads, writes)

    def REDUCE(out, in_, reads, writes):
        p.op("dve", lambda e: e.tensor_reduce(out=out, in_=in_, axis=AX.X, op=ALU.add), reads, writes)

    XR = lambda t: [("X", t, 0), ("X", t, 1)]

    for t in range(NT):
        DMA(X[:, t, :], x_d[t * 128:(t + 1) * 128, :], ("ldx", t), (), XR(t))
    DMA(identf[:], ident_d, "ldidf", (), ["identf"])
    DMA(identb[:], ident_d, "ldidb", (), ["identb"], eng="pool")

    gslot = [0]

    def load_gain(name):
        s = 0
        DMA(gbc[s][:], W[name].partition_broadcast(128), ("ldg", s), (), [("gbc", s)], eng="act")
        return s

    def rms_stats(t):
        ACT(junk[:], X[:, t, :], AF.Square, XR(t), ["junk", ("ss", t)], scale=1.0 / 32.0, accum_out=ss[:, t:t + 1])
        ACT(rstd[:, t:t + 1], ss[:, t:t + 1], AF.Sqrt, [("ss", t)], [("rstd", t)], bias=EPS)

    def rms_recip(t):
        p.op("dve", (lambda t: lambda e: e.reciprocal(out=rstd[:, t:t + 1], in_=rstd[:, t:t + 1]))(t), [("rstd", t)], [("rstd", t)])

    norm_state = {"ready": False, "out_done": False}

    def _stepper(apply_a, apply_b):
        def apply(t):
            apply_a(t)
            apply_b(t)

        def pre(t):
            if t >= 2:
                apply_a(t - 2)

        def mid(t):
            if t >= 2:
                apply_b(t - 2)

        def post(t):
            rms_stats(t)
            if t == NT - 1:
                apply(t - 1)
                apply(t)

        def step(t):
            pre(t)
            mid(t)
            post(t)
        step.stats = rms_stats
        step.apply = apply
        step.pre = pre
        step.mid = mid
        step.post = post
        return step

    def make_norm_HT(gain_name):
        def factory(st, tbanks=(6, 7)):
            gs = load_gain(gain_name)
            MEMSET("dve", ss[:], 0.0, [("ss", t) for t in range(NT)])

            def apply_a(t):
                s = t % 2
                rms_recip(t)
                STT("dve", xn[s][:], X[:, t, :], rstd[:, t:t + 1], gbc[gs][:], ALU.mult, ALU.mult,
                    XR(t) + [("rstd", t), ("gbc", gs)], [("xn", s)])

            def apply_b(t):
                s = t % 2
                tbk = tbanks[t % len(tbanks)]
                pst = PS[tbk].bitcast(BF16)
                for c in range(8):
                    TR(pst[:, c * 128:(c + 1) * 128], xn[s][:, c * 128:(c + 1) * 128], identb[:],
                       [("xn", s), "identb"], [("ps", tbk)], signal=(c == 7))
                src = pst[:, :].rearrange("p (c n) -> p c n", c=8)
                if t % 2 == 0 or len(tbanks) == 1:
                    ACT(HT[:, :, t * 128:(t + 1) * 128], src, AF.Copy, [("ps", tbk)], [("HT", t)])
                else:
                    CP("dve", HT[:, :, t * 128:(t + 1) * 128], src, [("ps", tbk)], [("HT", t)])
                if t == NT - 1:
                    norm_state["ready"] = True
            return _stepper(apply_a, apply_b)
        return factory

    def make_norm_out():
        def factory(st, tbanks=None):
            gs = load_gain("final_norm")
            MEMSET("dve", ss[:], 0.0, [("ss", t) for t in range(NT)])
            ot = [sbt(st, "ot%d" % i, [128, D], F32) for i in range(2)]

            def apply_a(t):
                s = t % 2
                rms_recip(t)
                STT("dve", ot[s][:], X[:, t, :], rstd[:, t:t + 1], gbc[gs][:], ALU.mult, ALU.mult,
                    XR(t) + [("rstd", t), ("gbc", gs)], [("ot", s)])

            def apply_b(t):
                s = t % 2
                DMA(out_d[t * 128:(t + 1) * 128, :], ot[s][:], ("st", s), [("ot", s)], ())
                if t == NT - 1:
                    norm_state["out_done"] = True
            return _stepper(apply_a, apply_b)
        return factory

    def ensure_norm(gain_name, st):
        if not norm_state["ready"]:
            step = make_norm_HT(gain_name)(st)
            for t in range(NT):
                step(t)
        norm_state["ready"] = False

    def ffn(pref, tail_factory):
        with ExitStack() as st:
            actT = sbt(st, "actT", [128, 11, S], BF16)
            wdt = sbt(st, "wdt", [128, 11, D], BF16)
            NSL = 4
            wgt = [sbt(st, "wgt%d" % i, [128, 8, 128], BF16) for i in range(NSL)]
            wut = [sbt(st, "wut%d" % i, [128, 8, 128], BF16) for i in range(NSL)]
            sg = [sbt(st, "sg%d" % i, [128, 512], F32) for i in range(2)]
            wgv = W[pref + "_w_gate"].rearrange("(kc p) n -> p kc n", p=128)
            wuv = W[pref + "_w_up"].rearrange("(kc p) n -> p kc n", p=128)
            wdd = W[pref + "_w_down"]

            def issue_gu(fc):
                s = fc % NSL
                DMA(wgt[s][:], wgv[:, :, fc * 128:(fc + 1) * 128], ("wg", s), (), [("wg", s)], eng="pool")
                DMA(wut[s][:], wuv[:, :, fc * 128:(fc + 1) * 128], ("wu", s), (), [("wu", s)], eng="pool")

            for fc in range(NSL - 1):
                issue_gu(fc)
            ensure_norm(pref + "_norm", st)
            step = 0
            for hf in range(2):
                for fcl in range(11):
                    fc = hf * 11 + fcl
                    if fc + NSL - 1 < NFC:
                        issue_gu(fc + NSL - 1)
                    DMA(wdt[:, fcl, :], wdd[fc * 128:(fc + 1) * 128, :], ("wd", fcl), (), [("wd", fcl)], eng="pool")
                    s = fc % NSL
                    for tb in range(4):
                        b = step % 2
                        step += 1
                        g, u = PS[b], PS[2 + b]
                        htr = [("HT", 4 * tb + i) for i in range(4)]
                        for kc in range(8):
                            MM(g[:], wgt[s][:, kc, :], HT[:, kc, tb * 512:(tb + 1) * 512], kc == 0, kc == 7,
                               [("wg", s)] + htr, [("ps", b)], kc == 7)
                        for kc in range(8):
                            MM(u[:], wut[s][:, kc, :], HT[:, kc, tb * 512:(tb + 1) * 512], kc == 0, kc == 7,
                               [("wu", s)] + htr, [("ps", 2 + b)], kc == 7)
                        ACT(sg[b][:], g[:], AF.Silu, [("ps", b)], [("sg", b)])
                        TT("dve", actT[:, fcl, tb * 512:(tb + 1) * 512], sg[b][:], u[:], ALU.mult,
                           [("sg", b), ("ps", 2 + b)], [("actT", fcl, tb)])
                dstep = 0
                tail = tail_factory(st) if hf == 1 else None
                for t in range(NT):
                    if tail is not None:
                        tail.pre(t)
                    for dh in range(2):
                        if tail is not None and dh == 1:
                            tail.mid(t)
                        b = 4 + dstep % 2
                        dstep += 1
                        for fcl in range(11):
                            MM(PS[b][:], actT[:, fcl, t * 128:(t + 1) * 128], wdt[:, fcl, dh * 512:(dh + 1) * 512],
                               fcl == 0, fcl == 10, [("actT", fcl, t // 4), ("wd", fcl)], [("ps", b)], fcl == 10)
                        STT("dve", X[:, t, dh * 512:(dh + 1) * 512], PS[b][:], 0.5, X[:, t, dh * 512:(dh + 1) * 512],
                            ALU.mult, ALU.add, [("ps", b), ("X", t, dh)], [("X", t, dh)])
                    if tail is not None:
                        tail.post(t)
        p.barrier()

    def mixer(tail_factory):
        import math
        wv_in = W["w_in"].rearrange("(kc p) n -> p kc n", p=128)
        with ExitStack() as stM:
            yaT = sbt(stM, "yaT", [128, 4, S], BF16)
            SC = sbt(stM, "SC", [128, 2, NT, 8], F32)
            rstdb = sbt(stM, "rstdb", [128, NT], F32)
            with ExitStack() as stA:
                wblk = [sbt(stA, "wblkA%d" % i, [128, 8, 512], BF16) for i in range(2)]
                DMA(wblk[0][:], wv_in[:, :, 512:1024], ("wb", 0), (), [("wblk", 0)], eng="pool")
                wsn = sbt(stA, "wsn", [128, 8, 128], F32)
                tri = sbt(stA, "tri", [128, 128], F32)
                wsT = sbt(stA, "wsT", [128, 8, 128], BF16)
                bsT = sbt(stA, "bsT", [128, 8], F32)
                lng = sbt(stA, "lng", [128, 512], F32)
                lnb = sbt(stA, "lnb", [128, 512], F32)
                gabc = sbt(stA, "gabc", [128, 512], F32)
                Gs = [sbt(stA, "Gs%d" % i, [128, 4, 512], F32) for i in range(2)]
                Ug = [sbt(stA, "Ug%d" % i, [128, 4, 512], F32) for i in range(2)]
                VLN = [sbt(stA, "VLN%d" % i, [128, 4, 512], BF16) for i in range(2)]
                sqA = sbt(stA, "sqA", [128, 512], F32)
                tmpA = [sbt(stA, "tmpA%d" % i, [128, 512], F32) for i in range(2)]
                yan = [sbt(stA, "yan%d" % i, [128, 512], BF16) for i in range(8)]
                s1 = [sbt(stA, "s1_%d" % i, [128, 32], F32) for i in range(2)]
                s2 = [sbt(stA, "s2_%d" % i, [128, 32], F32) for i in range(2)]
                mean = [sbt(stA, "mean%d" % i, [128, 32], F32) for i in range(2)]
                var = [sbt(stA, "var%d" % i, [128, 32], F32) for i in range(2)]
                rsg = [sbt(stA, "rsg%d" % i, [128, 32], F32) for i in range(2)]
                msa = sbt(stA, "msa", [128, NT], F32)
                rsa = sbt(stA, "rsa", [128, NT], F32)
                DMA(wsn[:], W["sgu_w_s"].rearrange("g i j -> i g j"), "ldws", (), ["wsn"])
                DMA(tri[:], tri_d, "ldtri", (), ["tri"])
                DMA(bsT[:], W["sgu_b_s"].rearrange("g i -> i g"), "ldbs", (), ["bsT"], slow=True)
                DMA(lng[:], W["sgu_ln_g"].partition_broadcast(128), "ldlng", (), ["lng"])
                DMA(lnb[:], W["sgu_ln_b"].partition_broadcast(128), "ldlnb", (), ["lnb"])
                DMA(gabc[:], W["out_norm_a"].partition_broadcast(128), "ldga", (), ["gabc"])
                ensure_norm("mix_norm", stA)
                for g in range(8):
                    b = 4 + g % 2
                    TR(PS[b][:, 0:128], wsn[:, g, :], identf[:], ["wsn", "identf"], [("ps", b)])
                    TT("dve", wsT[:, g, :], PS[b][:, 0:128], tri[:], ALU.mult, [("ps", b), "tri"], [("wsT", g)])
                MEMSET("dve", msa[:], 0.0, ["msa"])
                _stop("A0")
                def stage1a(G):
                    st_ = G % 2
                    for i in range(4):
                        t = 4 * G + i
                        b = t % 2
                        for kc in range(8):
                            MM(PS[b][:], HT[:, kc, t * 128:(t + 1) * 128], wblk[0][:, kc, :], kc == 0, kc == 7,
                               [("HT", t), ("wblk", 0)], [("ps", b)], kc == 7)
                        ACT(Gs[st_][:, i, :], PS[b][:], AF.Gelu, [("ps", b)], [("Gs", st_, i)])
                        REDUCE(s1[st_][:, i * 8:(i + 1) * 8], Gs[st_][:, i, :].rearrange("p (g c) -> p g c", g=8),
                               [("Gs", st_, i)], [("s1", st_, i)])
                        TT("dve", sqA[:], Gs[st_][:, i, :], Gs[st_][:, i, :], ALU.mult, [("Gs", st_, i)], ["sqA"])
                        REDUCE(s2[st_][:, i * 8:(i + 1) * 8], sqA[:].rearrange("p (g c) -> p g c", g=8),
                               ["sqA"], [("s2", st_, i)])
                    _stop("A1")
                    s1r = [("s1", st_, i) for i in range(4)]
                    s2r = [("s2", st_, i) for i in range(4)]
                    TS("dve", mean[st_][:], s1[st_][:], 1.0 / 64, None, ALU.mult, None, s1r, [("mean", st_)])
                    TT("dve", var[st_][:], mean[st_][:], mean[st_][:], ALU.mult, [("mean", st_)], [("var", st_)])
                    STT("dve", var[st_][:], s2[st_][:], 1.0 / 64, var[st_][:], ALU.mult, ALU.subtract,
                        s2r + [("var", st_)], [("var", st_)])
                    ACT(rsg[st_][:], var[st_][:], AF.Sqrt, [("var", st_)], [("rsg", st_)], bias=LN_EPS)

                def stage1b(G):
                    st_ = G % 2
                    RECIP(rsg[st_][:], rsg[st_][:], [("rsg", st_)], [("rsg", st_)])
                    _stop("A2")
                    for i in range(4):
                        ts_ = i % 2
                        g3 = Gs[st_][:, i, :].rearrange("p (g c) -> p g c", g=8)
                        t3 = tmpA[ts_][:].rearrange("p (g c) -> p g c", g=8)
                        TT("dve", t3, g3, mean[st_][:, i * 8:(i + 1) * 8].unsqueeze(2).to_broadcast([128, 8, 64]), ALU.subtract,
                           [("Gs", st_, i), ("mean", st_)], [("tmpA", ts_)])
                        TT("dve", t3, t3, rsg[st_][:, i * 8:(i + 1) * 8].unsqueeze(2).to_broadcast([128, 8, 64]), ALU.mult,
                           [("tmpA", ts_), ("rsg", st_)], [("tmpA", ts_)])
                        TT("pool", tmpA[ts_][:], tmpA[ts_][:], lng[:], ALU.mult, [("tmpA", ts_), "lng"], [("tmpA", ts_)])
                        TT("pool", VLN[st_][:, i, :], tmpA[ts_][:], lnb[:], ALU.add, [("tmpA", ts_), "lnb"], [("VLN", st_, i)])
                    _stop("A3")
                    for i in range(4):
                        t = 4 * G + i
                        b = t % 2
                        for kc in range(8):
                            MM(PS[b][:], HT[:, kc, t * 128:(t + 1) * 128], wblk[1][:, kc, :], kc == 0, kc == 7,
                               [("HT", t), ("wblk", 1)], [("ps", b)], kc == 7)
                        ACT(Ug[st_][:, i, :], PS[b][:], AF.Gelu, [("ps", b)], [("Ug", st_, i)])
                def stage2a(G):
                    st_ = G % 2
                    vr = [("VLN", st_, i) for i in range(4)]
                    ur = [("Ug", st_, i) for i in range(4)]
                    for g in range(8):
                        b = 2 + g % 2
                        MM(PS[b][:, 0:256], wsT[:, g, :], VLN[st_][:, :, g * 64:(g + 1) * 64], True, True,
                           [("wsT", g)] + vr, [("ps", b)], True)
                        STT("dve", Ug[st_][:, :, g * 64:(g + 1) * 64], PS[b][:, 0:256].rearrange("p (n c) -> p n c", n=4),
                            bsT[:, g:g + 1], Ug[st_][:, :, g * 64:(g + 1) * 64], ALU.add, ALU.mult,
                            [("ps", b), "bsT"] + ur, ur)
                    _stop("A5")
                    for i in range(4):
                        t = 4 * G + i
                        ACT(junk[:, 0:512], Ug[st_][:, i, :], AF.Square, [("Ug", st_, i)], ["junk", "msa"],
                            scale=1.0 / math.sqrt(512.0), accum_out=msa[:, t:t + 1])
                    ACT(rsa[:, 4 * G:4 * G + 4], msa[:, 4 * G:4 * G + 4], AF.Sqrt, ["msa"], [("rsa", G)], bias=EPS)

                def stage2b(G):
                    st_ = G % 2
                    RECIP(rsa[:, 4 * G:4 * G + 4], rsa[:, 4 * G:4 * G + 4], [("rsa", G)], [("rsa", G)])
                    _stop("A6")
                    for i in range(4):
                        t = 4 * G + i
                        s = t % 8
                        STT("dve", yan[s][:], Ug[st_][:, i, :], rsa[:, t:t + 1], gabc[:], ALU.mult, ALU.mult,
                            [("Ug", st_, i), ("rsa", G), "gabc"], [("yan", s)])

                def stage3(G):
                    for i in range(4):
                        t = 4 * G + i
                        s = t % 8
                        pst = PS[6 + t % 2].bitcast(BF16)
                        for c in range(4):
                            TR(pst[:, c * 128:(c + 1) * 128], yan[s][:, c * 128:(c + 1) * 128], identb[:],
                               [("yan", s), "identb"], [("ps", 6 + t % 2)], signal=(c == 3))
                        ACT(yaT[:, :, t * 128:(t + 1) * 128], pst[:, 0:512].rearrange("p (c n) -> p c n", c=4), AF.Copy,
                            [("ps", 6 + t % 2)], [("yaT", t)])

                stage1a(0)
                DMA(wblk[1][:], wv_in[:, :, 0:512], ("wb", 1), (), [("wblk", 1)], eng="pool")
                stage1a(1)
                stage1b(0)
                stage2a(0)
                stage1b(1)
                stage2b(0)
                stage1a(2)
                stage2a(1)
                stage3(0)
                stage1b(2)
                stage2b(1)
                stage1a(3)
                stage2a(2)
                stage3(1)
                stage1b(3)
                stage2b(2)
                stage2a(3)
                stage3(2)
                stage2b(3)
                stage3(3)
            p.barrier()
            _stop("A")
            with ExitStack() as stB:
                QT = sbt(stB, "QT", [128, 4, S], BF16)
                KT = sbt(stB, "KT", [128, 4, S], BF16)
                VA = sbt(stB, "VA", [128, NT, 8, 65], BF16)
                with ExitStack() as stB1:
                    wblk = [sbt(stB1, "wblkB%d" % i, [128, 8, 512], BF16) for i in range(2)]
                    DMA(wblk[0][:], wv_in[:, :, 1024:1536], ("wb", 0), (), [("wblk", 0)], eng="pool")
                    DMA(wblk[1][:], wv_in[:, :, 1536:2048], ("wb", 1), (), [("wblk", 1)], eng="pool")
                    posi = sbt(stB1, "posi", [128, NT], I32)
                    posf = sbt(stB1, "posf", [128, NT], F32)
                    invf = sbt(stB1, "invf", [128, 128], F32)
                    ANG = sbt(stB1, "ANG", [128, 2, 128], F32)
                    KF = sbt(stB1, "KF", [128, 256], F32)
                    KI = sbt(stB1, "KI", [128, 256], I32)
                    QR = [sbt(stB1, "QR%d" % i, [128, 512], BF16) for i in range(4)]
                    rt = [sbt(stB1, "rt%d" % i, [128, 8, 8], F32) for i in range(8)]
                    DMA(posi[:], pos_d, "ldpos", (), ["posi"])
                    DMA(invf[:], invf_d, "ldinvf", (), ["invf"])
                    MEMSET("pool", VA[:, :, :, 64:65], 1.0, ["VAones"])
                    CP("dve", posf[:], posi[:], ["posi"], ["posf"])
                    TT("dve", ANG[:, 0, :].rearrange("p (t i) -> p t i", i=8), invf[:].rearrange("p (t i) -> p t i", i=8),
                       posf[:].unsqueeze(2).to_broadcast([128, NT, 8]), ALU.mult, ["invf", "posf"], ["ANG"])
                    TS("dve", ANG[:, 1, :], ANG[:, 0, :], math.pi / 2, None, ALU.add, None, ["ANG"], ["ANG"])
                    A2 = ANG[:].rearrange("p a n -> p (a n)")
                    TS("dve", KF[:], A2, 1.0 / (2 * math.pi), None, ALU.mult, None, ["ANG"], ["KF"])
                    CP("dve", KI[:], KF[:], ["KF"], ["KI"])
                    CP("dve", KF[:], KI[:], ["KI"], ["KF"])
                    C1 = 6.28125
                    C2 = 2 * math.pi - C1
                    STT("dve", A2, KF[:], -C1, A2, ALU.mult, ALU.add, ["KF", "ANG"], ["ANG"])
                    STT("dve", A2, KF[:], -C2, A2, ALU.mult, ALU.add, ["KF", "ANG"], ["ANG"])
                    TS("dve", A2, A2, -math.pi, math.pi, ALU.max, ALU.min, ["ANG"], ["ANG"])
                    ACT(SC[:].rearrange("p a t i -> p (a t i)"), A2, AF.Sin, ["ANG"], ["SC"])

                    _stop("B1a")

                    def rotary(ps, bank, dst, dkey, t, k0):
                        ps3 = ps[:].rearrange("p (h e) -> p h e", h=8)
                        d3 = dst[:].rearrange("p (h e) -> p h e", h=8)
                        sinb = SC[:, 0, t:t + 1, :].to_broadcast([128, 8, 8])
                        cosb = SC[:, 1, t:t + 1, :].to_broadcast([128, 8, 8])
                        pr = [("ps", bank), "SC"]
                        CP("dve", d3[:, :, 16:64], ps3[:, :, 16:64], [("ps", bank)], [dkey])
                        TT("dve", rt[k0][:], ps3[:, :, 0:8], cosb, ALU.mult, pr, [("rt", k0)])
                        TT("dve", rt[k0 + 1][:], ps3[:, :, 8:16], sinb, ALU.mult, pr, [("rt", k0 + 1)])
                        TT("dve", d3[:, :, 0:8], rt[k0][:], rt[k0 + 1][:], ALU.subtract, [("rt", k0), ("rt", k0 + 1)], [dkey])
                        TT("dve", rt[k0 + 2][:], ps3[:, :, 8:16], cosb, ALU.mult, pr, [("rt", k0 + 2)])
                        TT("dve", rt[k0 + 3][:], ps3[:, :, 0:8], sinb, ALU.mult, pr, [("rt", k0 + 3)])
                        TT("dve", d3[:, :, 8:16], rt[k0 + 2][:], rt[k0 + 3][:], ALU.add, [("rt", k0 + 2), ("rt", k0 + 3)], [dkey])

                    def projqk(t):
                        for qk in range(2):
                            b = qk * 2 + t % 2
                            for kc in range(8):
                                MM(PS[b][:], HT[:, kc, t * 128:(t + 1) * 128], wblk[qk][:, kc, :], kc == 0, kc == 7,
                                   [("HT", t), ("wblk", qk)], [("ps", b)], kc == 7)
                            qs = qk * 2 + t % 2
                            rotary(PS[b], b, QR[qs], ("QR", qs), t, 4 * qk)

                    def trqk(t):
                        for qk in range(2):
                            qs = qk * 2 + t % 2
                            tb_ = 6 + qk
                            pst = PS[tb_].bitcast(BF16)
                            for c in range(4):
                                TR(pst[:, c * 128:(c + 1) * 128], QR[qs][:, c * 128:(c + 1) * 128], identb[:],
                                   [("QR", qs), "identb"], [("ps", tb_)], signal=(c == 3))
                            dstT = QT if qk == 0 else KT
                            ACT(dstT[:, :, t * 128:(t + 1) * 128], pst[:, 0:512].rearrange("p (c n) -> p c n", c=4),
                                AF.Copy, [("ps", tb_)], [("QKT", qk, t)])

                    projqk(0)
                    for t in range(NT):
                        if t + 1 < NT:
                            projqk(t + 1)
                        trqk(t)
                    _stop("B1c")
                    DMA(wblk[0][:], wv_in[:, :, 2048:2560], ("wb", 0), (), [("wblk", 0)], eng="pool")
                    for t in range(NT):
                        b = 4 + t % 2
                        for kc in range(8):
                            MM(PS[b][:], HT[:, kc, t * 128:(t + 1) * 128], wblk[0][:, kc, :], kc == 0, kc == 7,
                               [("HT", t), ("wblk", 0)], [("ps", b)], kc == 7)
                        ACT(VA[:, t, :, 0:64], PS[b][:].rearrange("p (h e) -> p h e", h=8), AF.Copy,
                            [("ps", b)], [("VA", t)])
                p.barrier()
                _stop("B1")
                with ExitStack() as stB2:
                    cm = sbt(stB2, "cm", [128, S], BF16)
                    NPB2 = 5
                    pb2 = [sbt(stB2, "pb%d" % i, [128, 2, 512], BF16) for i in range(NPB2)]
                    YB = [sbt(stB2, "YB%d" % i, [128, 4, 512], F32) for i in range(2)]
                    rzt = [sbt(stB2, "rzt%d" % i, [128, 4], F32) for i in range(4)]
                    ybn = [sbt(stB2, "ybn%d" % i, [128, 512], BF16) for i in range(2)]
                    gbbc = sbt(stB2, "gbbc", [128, 512], F32)
                    msb = sbt(stB2, "msb", [128, NT], F32)
                    YTb = HT[:, 0:4, :]
                    WO = HT[:, 4:8, :].rearrange("p a (b n) -> p (a b) n", n=1024)
                    DMA(cm[:], cmask_d, "ldcm", (), ["cm"], eng="pool")
                    DMA(gbbc[:], W["out_norm_b"].partition_broadcast(128), "ldgb", (), ["gbbc"])
                    DMA(WO, W["w_out"].rearrange("(kc p) n -> p kc n", p=128), "ldwo", (), ["WO"], eng="pool")
                    MEMSET("dve", msb[:], 0.0, ["msb"])
                    steps = []
                    for b in range(4):
                        for c in range(4):
                            nj = 4 * b + 4
                            for j in range(nj):
                                steps.append((b, c, j, nj))
                    LA = 4
                    SBK = [(0, 1), (6, 7)]
                    pending = []

                    def front(i):
                        b, c, j, nj = steps[i]
                        qlo = max(512 * b, 128 * j)
                        N = 512 * (b + 1) - qlo
                        for e in range(2):
                            r0 = e * 64
                            sb_ = SBK[i % 2][e]
                            MM(PS[sb_][:, 0:N], KT[r0:r0 + 64, c, j * 128:(j + 1) * 128], QT[r0:r0 + 64, c, qlo:qlo + N],
                               True, True, (), [("ps", sb_)], True)
                        pp = PSP[SBK[i % 2][0] // 2]
                        ps_ = i % NPB2
                        prs = [("ps", SBK[i % 2][0]), ("ps", SBK[i % 2][1])]
                        ACT(pb2[ps_][:, :, 0:N], pp[:, :, 0:N], AF.Exp, prs, [("pb", ps_)], scale=0.125)
                        TT("pool" if i % 3 == 0 else "dve", pb2[ps_][:, :, 0:N], pb2[ps_][:, :, 0:N],
                           cm[:, qlo - 128 * j:qlo - 128 * j + N].unsqueeze(1).to_broadcast([128, 2, N]), ALU.mult,
                           [("pb", ps_), "cm"], [("pb", ps_)])

                    def back(i):
                        b, c, j, nj = steps[i]
                        qlo = max(512 * b, 128 * j)
                        N = 512 * (b + 1) - qlo
                        nqb = N // 128
                        qb0 = 4 - nqb
                        g = b * 4 + c
                        for e in range(2):
                            h = 2 * c + e
                            ab = (2, 3)[e] if g % 2 == 0 else (4, 5)[e]
                            ps_ = i % NPB2
                            for qi in range(nqb):
                                qb = qb0 + qi
                                last = (qi == nqb - 1)
                                MM(PS[ab][:, qb * 65:(qb + 1) * 65], pb2[ps_][:, e, qi * 128:(qi + 1) * 128], VA[:, j, h, :],
                                   (j == 0 and qi == 0), (j == nj - 1), [("pb", ps_), "VAones"], [("ps", ab)],
                                   last and (j == nj - 1), sgc=True)
                        if j == nj - 1:
                            pending.append((i + 1, epi, b, c))

                    def epi(b, c):
                        g = b * 4 + c
                        ys = b % 2
                        for e in range(2):
                            h = 2 * c + e
                            ab = (2, 3)[e] if g % 2 == 0 else (4, 5)[e]
                            acc3 = PS[ab][:, 0:260].rearrange("p (q e) -> p q e", q=4)
                            rs = (2 * g + e) % 4
                            RECIP(rzt[rs][:].unsqueeze(2), acc3[:, :, 64:65], [("ps", ab)], [("rzt", rs)])
                            TT("dve", YB[ys][:, :, h * 64:(h + 1) * 64], acc3[:, :, 0:64],
                               rzt[rs][:].unsqueeze(2).to_broadcast([128, 4, 64]), ALU.mult,
                               [("ps", ab), ("rzt", rs)], [("YB", ys, h)])
                        if c == 3:
                            pending.append((0, bank_tail, b, 0))

                    def bank_tail(b, _):
                        ys = b % 2
                        ybr_ = [("YB", ys, h) for h in range(8)]
                        for i4 in range(4):
                            t = 4 * b + i4
                            s = t % 2
                            ACT(junk[:, 0:512], YB[ys][:, i4, :], AF.Square, ybr_, ["junk", "msb"],
                                scale=1.0 / math.sqrt(512.0), accum_out=msb[:, t:t + 1])
                            TT("pool", ybn[s][:], YB[ys][:, i4, :], gbbc[:], ALU.mult, ybr_ + ["gbbc"], [("ybn", s)])
                            tbk = 6 + s
                            pst = PS[tbk].bitcast(BF16)
                            for cc in range(4):
                                TR(pst[:, cc * 128:(cc + 1) * 128], ybn[s][:, cc * 128:(cc + 1) * 128], identb[:],
                                   [("ybn", s), "identb"], [("ps", tbk)], signal=(cc == 3))
                            ACT(YTb[:, :, t * 128:(t + 1) * 128], pst[:, 0:512].rearrange("p (c n) -> p c n", c=4),
                                AF.Copy, [("ps", tbk)], [("YTb", t)])

                    nst = len(steps)
                    for i in range(nst + LA + 4):
                        if i < nst:
                            front(i)
                        if 0 <= i - LA < nst:
                            back(i - LA)
                        due = [q for q in pending if q[0] <= i - LA]
                        for q in due:
                            pending.remove(q)
                            q[1](q[2], q[3])
                    while pending:
                        q = pending.pop(0)
                        q[1](q[2], q[3])
                    _stop("B2a")
                    ACT(rstdb[:], msb[:], AF.Sqrt, ["msb"], ["rstdb"], bias=EPS)
                    RECIP(rstdb[:], rstdb[:], ["rstdb"], ["rstdb"])
                    ybr = []
                    k = 0
                    tail = tail_factory(stB2)
                    for t in range(NT):
                        for dh in range(2):
                            ba = k % 2
                            bb = 2 + k % 2
                            k += 1
                            for c in range(4):
                                MM(PS[ba][:], yaT[:, c, t * 128:(t + 1) * 128], WO[:, c, dh * 512:(dh + 1) * 512],
                                   c == 0, c == 3, ["WO"], [("ps", ba)], c == 3)
                            for c in range(4):
                                MM(PS[bb][:], YTb[:, c, t * 128:(t + 1) * 128], WO[:, 4 + c, dh * 512:(dh + 1) * 512],
                                   c == 0, c == 3, ["WO", ("YTb", t)], [("ps", bb)], c == 3)
                            xs = X[:, t, dh * 512:(dh + 1) * 512]
                            TT("dve", xs, xs, PS[ba][:], ALU.add, [("ps", ba), ("X", t, dh)], [("X", t, dh)])
                            STT("dve", xs, PS[bb][:], rstdb[:, t:t + 1], xs, ALU.mult, ALU.add,
                                [("ps", bb), ("X", t, dh), "rstdb"], [("X", t, dh)])
                        tail.stats(t)
                    p.barrier()
                    for t in range(NT):
                        tail.apply(t)
        p.barrier()

    def cross(tail_factory):
        with ExitStack() as st:
            ensure_norm("cross_norm", st)
            memt = sbt(st, "memt", [128, 2, D], F32)
            MT = sbt(st, "MT", [128, 8, NMEM], BF16)
            KxT = sbt(st, "KxT", [128, 8, NMEM], BF16)
            Vx = sbt(st, "Vx", [128, 2, D], BF16)
            QxT = sbt(st, "QxT", [128, 8, S], BF16)
            wsl = [sbt(st, "wx%d" % i, [128, 8, 512], BF16) for i in range(3)]
            onesb = sbt(st, "onesb", [128, 128], BF16)
            pbx = [sbt(st, "pbx%d" % i, [128, 512], BF16) for i in range(4)]
            rzx = [sbt(st, "rzx%d" % i, [128, 512], F32) for i in range(2)]
            mss = sbt(st, "mss", [128, 2], F32)
            mrs = sbt(st, "mrs", [128, 2], F32)
            wplan = [("cross_wq", 0), ("cross_wk", 0), ("cross_wk", 1), ("cross_wv", 0), ("cross_wv", 1),
                     ("cross_wq", 1), ("cross_wo", 0), ("cross_wo", 1)]
            sl_of = {}

            def load_w(i):
                sl = i % 3
                nm, blk = wplan[i]
                src = W[nm].rearrange("(kc p) n -> p kc n", p=128)[:, :, blk * 512:(blk + 1) * 512]
                DMA(wsl[sl][:], src, ("wx", sl), (), [("wx", sl)], eng="pool")
                sl_of[i] = sl

            load_w(0)
            MEMSET("pool", onesb[:], 1.0, ["onesb"])
            kk = [0]

            def QP(hc, tb):
                sl = sl_of[0] if hc < 4 else sl_of[5]
                b = kk[0] % 2
                kk[0] += 1
                htr = [("HT", 4 * tb + i) for i in range(4)]
                for kc in range(8):
                    MM(PS[b][:], wsl[sl][:, kc, (hc % 4) * 128:(hc % 4 + 1) * 128], HT[:, kc, tb * 512:(tb + 1) * 512],
                       kc == 0, kc == 7, [("wx", sl)] + htr, [("ps", b)], kc == 7)
                if kk[0] % 2 == 0 or hc >= 4:
                    ACT(QxT[:, hc, tb * 512:(tb + 1) * 512], PS[b][:], AF.Copy, [("ps", b)], [("QxT", hc, tb)])
                else:
                    CP("dve", QxT[:, hc, tb * 512:(tb + 1) * 512], PS[b][:], [("ps", b)], [("QxT", hc, tb)])

            for hc in range(4):
                for tb in range(4):
                    QP(hc, tb)
                if hc == 0:
                    load_w(1)
                    load_w(2)
                    for mt in range(2):
                        DMA(memt[:, mt, :], mem_d[mt * 128:(mt + 1) * 128, :], ("ldm", mt), (), [("memt", mt)])
            load_w(3)
            gm = load_gain("mem_norm")
            MEMSET("dve", mss[:], 0.0, ["mss"])
            for mt in range(2):
                ACT(junk[:], memt[:, mt, :], AF.Square, [("memt", mt)], ["junk", "mss"], scale=1.0 / 32.0,
                    accum_out=mss[:, mt:mt + 1])
            ACT(mrs[:], mss[:], AF.Sqrt, ["mss"], ["mrs"], bias=EPS)
            RECIP(mrs[:], mrs[:], ["mrs"], ["mrs"])
            for mt in range(2):
                s = mt % 2
                STT("dve", xn[s][:], memt[:, mt, :], mrs[:, mt:mt + 1], gbc[gm][:], ALU.mult, ALU.mult,
                    [("memt", mt), "mrs", ("gbc", gm)], [("xn", s)])
                pst = PS[6 + s].bitcast(BF16)
                for c in range(8):
                    TR(pst[:, c * 128:(c + 1) * 128], xn[s][:, c * 128:(c + 1) * 128], identb[:],
                       [("xn", s), "identb"], [("ps", 6 + s)], signal=(c == 7))
                ACT(MT[:, :, mt * 128:(mt + 1) * 128], pst[:, :].rearrange("p (c n) -> p c n", c=8), AF.Copy,
                    [("ps", 6 + s)], [("MT", mt)])
            mtr = [("MT", 0), ("MT", 1)]
            for hc in range(8):
                sl = sl_of[1 + hc // 4]
                b = kk[0] % 2
                kk[0] += 1
                for kc in range(8):
                    MM(PS[b][:, 0:NMEM], wsl[sl][:, kc, (hc % 4) * 128:(hc % 4 + 1) * 128], MT[:, kc, :], kc == 0, kc == 7,
                       [("wx", sl)] + mtr, [("ps", b)], kc == 7)
                ACT(KxT[:, hc, :], PS[b][:, 0:NMEM], AF.Copy, [("ps", b)], [("KxT", hc)])
                if hc == 3:
                    load_w(4)
            load_w(5)
            for dh in range(2):
                sl = sl_of[3 + dh]
                for mt in range(2):
                    b = kk[0] % 2
                    kk[0] += 1
                    for kc in range(8):
                        MM(PS[b][:], MT[:, kc, mt * 128:(mt + 1) * 128], wsl[sl][:, kc, :], kc == 0, kc == 7,
                           [("wx", sl), ("MT", mt)], [("ps", b)], kc == 7)
                    ACT(Vx[:, mt, dh * 512:(dh + 1) * 512], PS[b][:], AF.Copy, [("ps", b)], [("Vx", mt, dh)])
                load_w(6 + dh)
            itc = [0]

            def ATT(h, tb, alt=False):
                par = itc[0] % 2
                itc[0] += 1
                sbk = (2, 3)
                ob = (4, 5)
                zb = 7
                if alt and par == 1:
                    ob = (0, 1)
                    zb = 6
                cols = slice(tb * 512, (tb + 1) * 512)
                for mt in range(2):
                    for ec in range(2):
                        MM(PS[sbk[mt]][:], KxT[:, 2 * h + ec, mt * 128:(mt + 1) * 128], QxT[:, 2 * h + ec, cols], ec == 0, ec == 1,
                           [("KxT", 2 * h + ec), ("QxT", 2 * h + ec, tb)], [("ps", sbk[mt])], ec == 1)
                    ACT(pbx[par * 2 + mt][:], PS[sbk[mt]][:], AF.Exp, [("ps", sbk[mt])], [("pbx", par * 2 + mt)], scale=1.0 / 16.0)
                for ec in range(2):
                    for mt in range(2):
                        MM(PS[ob[ec]][:], Vx[:, mt, (2 * h + ec) * 128:(2 * h + ec + 1) * 128], pbx[par * 2 + mt][:], mt == 0, mt == 1,
                           [("Vx", mt, (2 * h + ec) // 4), ("pbx", par * 2 + mt)], [("ps", ob[ec])], mt == 1)
                for mt in range(2):
                    MM(PS[zb][:], onesb[:], pbx[par * 2 + mt][:], mt == 0, mt == 1, ["onesb", ("pbx", par * 2 + mt)],
                       [("ps", zb)], mt == 1)
                RECIP(rzx[par][:], PS[zb][:], [("ps", zb)], [("rzx", par)])
                for ec in range(2):
                    TT("dve", QxT[:, 2 * h + ec, cols], PS[ob[ec]][:], rzx[par][:], ALU.mult,
                       [("ps", ob[ec]), ("rzx", par)], [("QxT", 2 * h + ec, tb)])

            qpl = [(hc, tb) for hc in range(4, 8) for tb in range(4)]
            atl = [(h, tb) for h in range(2) for tb in range(4)]
            for i, (hc, tb) in enumerate(qpl):
                QP(hc, tb)
                if i % 2 == 1:
                    ATT(*atl[i // 2])
            for h in range(2, 4):
                for tb in range(4):
                    ATT(h, tb, alt=True)
            tail = tail_factory(st)
            for t in range(NT):
                tail.pre(t)
                for dh in range(2):
                    if dh == 1:
                        tail.mid(t)
                    b = kk[0] % 2
                    kk[0] += 1
                    sl = sl_of[6 + dh]
                    for hc in range(8):
                        MM(PS[b][:], QxT[:, hc, t * 128:(t + 1) * 128], wsl[sl][:, hc, :], hc == 0, hc == 7,
                           [("QxT", hc, t // 4), ("wx", sl)], [("ps", b)], hc == 7)
                    xs = X[:, t, dh * 512:(dh + 1) * 512]
                    TT("dve", xs, xs, PS[b][:], ALU.add, [("ps", b), ("X", t, dh)], [("X", t, dh)])
                tail.post(t)
        p.barrier()

    seq = [q for q in ("ffn1", "mix", "cross", "ffn2") if q in stages]
    gains = {"ffn1": "ffn1_norm", "mix": "mix_norm", "cross": "cross_norm", "ffn2": "ffn2_norm"}
    for i, sname in enumerate(seq):
        nxt = seq[i + 1] if i + 1 < len(seq) else None
        tf = make_norm_HT(gains[nxt]) if nxt else make_norm_out()
        if sname in ("ffn1", "ffn2"):
            ffn(sname, tf)
        elif sname == "mix":
            mixer(tf)
            p.muted = False
            p.barrier()
        else:
            cross(tf)
    if not norm_state["out_done"]:
        with ExitStack() as st:
            step = make_norm_out()(st)
            for t in range(NT):
                step(t)
    p.emit()
    top.close()
    return nc


_CACHE = {}


def _consts():
    ident = np.eye(128, dtype=np.float32)
    j = np.arange(128)[:, None]
    i = np.arange(128)[None, :]
    tri = (j <= i).astype(np.float32)
    d = np.arange(S)[None, :] - np.arange(128)[:, None]
    cm = ((d >= 0) & (d <= 128)).astype(np.float32) + ((d >= 0) & (d % 4 == 0) & (d <= 512)).astype(np.float32) \
        + ((d >= 0) & (d % 16 == 0)).astype(np.float32)
    ex = (-2.0 * np.arange(8, dtype=np.float32) / np.float32(16.0)).astype(np.float32)
    invf = np.power(np.float32(500000.0), ex).astype(np.float32)
    invf_t = np.tile(invf[None, :], (128, 16)).astype(np.float32)
    return {"ident": ident, "tri": tri, "cmask": cm.astype(np.float32), "invf": invf_t}


def kernel(**inputs):
    n = 8
    if "nc" not in _CACHE:
        _CACHE["nc"] = build_nc()
    nc = _CACHE["nc"]
    cst = _consts()
    shared = {}
    for k, v in inputs.items():
        if k in ("x", "mem", "positions"):
            continue
        a = np.asarray(v)
        if k == "final_norm":
            a = a.reshape(1, D)
        elif k in ("sgu_w_s", "sgu_b_s"):
            a = a[0]
        elif a.ndim == 2:
            pass
        else:
            a = a[0]
        shared[k] = np.ascontiguousarray(a, dtype=np.float32)
    x = np.asarray(inputs["x"], dtype=np.float32)
    mem = np.asarray(inputs["mem"], dtype=np.float32)
    pos = np.asarray(inputs["positions"], dtype=np.int32)
    in_maps = []
    for b in range(n):
        m = dict(shared)
        m.update(cst)
        m["x"] = np.ascontiguousarray(x[b])
        m["mem"] = np.ascontiguousarray(mem[b])
        m["pos"] = np.ascontiguousarray(pos[b].reshape(NT, 128).T)
        in_maps.append(m)
    res = run_bass_kernel_spmd(nc, in_maps, core_ids=list(range(n)))
    return np.stack([np.asarray(r["out"], dtype=np.float32) for r in res.results], axis=0)
```

```python
import numpy as np
from contextlib import ExitStack
import concourse.bass as bass
import concourse.mybir as mybir
from concourse.bass_utils import run_bass_kernel_spmd

F32 = mybir.dt.float32
BF16 = mybir.dt.bfloat16
I32 = mybir.dt.int32
AF = mybir.ActivationFunctionType
ALU = mybir.AluOpType
AX = mybir.AxisListType

ENGS = ("pe", "act", "dve", "pool", "sp")

S = 2048
D = 1024
NT = 16
DFF = 2816
NFC = 22
NMEM = 256
EPS = 1e-6
LN_EPS = 1e-5
import os as _os
MIXSTOP = _os.environ.get("MIXSTOP", "")
SAFE_SAME_ENGINE = True


class _Stop(Exception):
    pass


_PROG = [None]


def _stop(tag):
    if MIXSTOP == tag:
        _PROG[0].muted = True


class _Op:
    __slots__ = ("eng", "fn", "deps", "signal", "dma_key", "dma_cnt", "idx", "sigcnt")


class Prog:
    def __init__(self, nc):
        self.nc = nc
        self.ops = {e: [] for e in ENGS}
        self.last_w = {}
        self.readers = {}
        self.dma_cnt = {}
        self.bar_tokens = []
        self.bar_gen = 0
        self.eng_gen = {e: 0 for e in ENGS}
        self.muted = False
        _PROG[0] = self

    def _add(self, eng, fn, reads, writes, signal, dma_key):
        if self.muted:
            return None
        op = _Op()
        op.eng = eng
        op.fn = fn
        op.signal = signal
        op.dma_key = dma_key
        op.idx = len(self.ops[eng])
        deps = []
        if self.eng_gen[eng] < self.bar_gen:
            self.eng_gen[eng] = self.bar_gen
            for tok in self.bar_tokens:
                deps.append((tok, True))
        for r in reads:
            w = self.last_w.get(r)
            if w is not None:
                deps.append((w, True))
        for r in writes:
            w = self.last_w.get(r)
            if w is not None:
                deps.append((w, False))
            for rd in self.readers.get(r, ()):
                deps.append((rd, False))
        op.deps = deps
        if dma_key is not None:
            c = self.dma_cnt.get(dma_key, 0) + 16
            self.dma_cnt[dma_key] = c
            op.dma_cnt = c
            tok = ("d", dma_key, c)
        else:
            tok = ("c", eng, op.idx)
        for r in writes:
            self.last_w[r] = tok
            self.readers[r] = []
        for r in reads:
            self.readers.setdefault(r, []).append(tok)
        self.ops[eng].append(op)
        return op

    def op(self, eng, fn, reads=(), writes=(), signal=True):
        return self._add(eng, fn, tuple(reads), tuple(writes), signal, None)

    def dma(self, fn, key, reads=(), writes=(), eng="sp"):
        return self._add(eng, fn, tuple(reads), tuple(writes), False, key)

    def barrier(self):
        if self.muted:
            return
        toks = []
        for e in ENGS:
            lst = self.ops[e]
            for i in range(len(lst) - 1, -1, -1):
                if lst[i].dma_key is None:
                    lst[i].signal = True
                    toks.append(("c", e, i))
                    break
        for k, c in self.dma_cnt.items():
            toks.append(("d", k, c))
        self.bar_tokens = toks
        self.bar_gen += 1
        self.last_w = {}
        self.readers = {}

    def emit(self):
        nc = self.nc
        sig_at = {}
        for e in ENGS:
            cnt = 0
            lst = self.ops[e]
            for op in lst:
                if op.dma_key is None and op.signal:
                    cnt += 1
                    op.sigcnt = cnt
            need = [None] * len(lst)
            nxt = None
            for i in range(len(lst) - 1, -1, -1):
                op = lst[i]
                if op.dma_key is None and op.signal:
                    nxt = op.sigcnt
                need[i] = nxt
            sig_at[e] = need
        es = ExitStack()
        sems = {}
        for e in ENGS:
            sems[e] = es.enter_context(nc.semaphore("s_" + e))
        dsems = {}
        for i, k in enumerate(self.dma_cnt):
            dsems[k] = es.enter_context(nc.semaphore("d%d" % i))
        block = es.enter_context(nc.Block())

        def run_engine(e, eo):
            waited = {}
            for op in self.ops[e]:
                wl = {}
                for (tok, raw) in op.deps:
                    if tok[0] == "c":
                        pe_, idx = tok[1], tok[2]
                        if pe_ == e:
                            if e == "pe" or e == "sp" or ((not raw) and not SAFE_SAME_ENGINE):
                                continue
                        val = sig_at[pe_][idx]
                        if val is None:
                            raise RuntimeError("dep on unsignaled tail op %s %d" % (pe_, idx))
                        s = sems[pe_]
                        k = ("c", pe_)
                    else:
                        s = dsems[tok[1]]
                        val = tok[2]
                        k = ("d", tok[1])
                    if waited.get(k, 0) >= val:
                        continue
                    if wl.get(k, (None, 0))[1] < val:
                        wl[k] = (s, val)
                for k, (s, val) in wl.items():
                    eo.wait_ge(s, val)
                    waited[k] = val
                ins = op.fn(eo)
                if op.dma_key is not None:
                    ins.then_inc(dsems[op.dma_key], 16)
                elif op.signal:
                    ins.then_inc(sems[e], 1)

        @block.tensor
        def _(eo):
            run_engine("pe", eo)

        @block.scalar
        def _(eo):
            run_engine("act", eo)

        @block.vector
        def _(eo):
            run_engine("dve", eo)

        @block.gpsimd
        def _(eo):
            run_engine("pool", eo)

        @block.sync
        def _(eo):
            run_engine("sp", eo)
            for k, c in self.dma_cnt.items():
                eo.wait_ge(dsems[k], c)

        es.close()


def build_nc(stages=("ffn1", "mix", "cross", "ffn2"), dbg=False):
    nc = bass.Bass("TRN2", target_bir_lowering=False)
    dram_in = lambda name, shape, dt=F32: nc.dram_tensor(name, shape, dt, kind="ExternalInput").ap()
    x_d = dram_in("x", [S, D])
    mem_d = dram_in("mem", [NMEM, D])
    pos_d = dram_in("pos", [128, NT], I32)
    ident_d = dram_in("ident", [128, 128])
    tri_d = dram_in("tri", [128, 128])
    cmask_d = dram_in("cmask", [128, S])
    invf_d = dram_in("invf", [128, 128])
    W = {}
    for nm, sh in [("ffn1_norm", [1, D]), ("ffn1_w_gate", [D, DFF]), ("ffn1_w_up", [D, DFF]), ("ffn1_w_down", [DFF, D]),
                   ("mix_norm", [1, D]), ("w_in", [D, 2560]), ("sgu_ln_g", [1, 512]), ("sgu_ln_b", [1, 512]),
                   ("sgu_w_s", [8, 128, 128]), ("sgu_b_s", [8, 128]), ("out_norm_a", [1, 512]), ("out_norm_b", [1, 512]),
                   ("w_out", [D, D]), ("cross_norm", [1, D]), ("mem_norm", [1, D]), ("cross_wq", [D, D]),
                   ("cross_wk", [D, D]), ("cross_wv", [D, D]), ("cross_wo", [D, D]), ("ffn2_norm", [1, D]),
                   ("ffn2_w_gate", [D, DFF]), ("ffn2_w_up", [D, DFF]), ("ffn2_w_down", [DFF, D]), ("final_norm", [1, D])]:
        W[nm] = dram_in(nm, sh)
    out_d = nc.dram_tensor("out", [S, D], F32, kind="ExternalOutput").ap()

    top = ExitStack()
    _uid = [0]

    def sbt(st, name, shape, dt):
        _uid[0] += 1
        return st.enter_context(nc.sbuf_tensor("s%d_%s" % (_uid[0], name), shape, dt))
    p = Prog(nc)

    X = sbt(top, "X", [128, NT, D], F32)
    HT = sbt(top, "HT", [128, 8, S], BF16)
    identb = sbt(top, "identb", [128, 128], BF16)
    identf = sbt(top, "identf", [128, 128], F32)
    ss = sbt(top, "ss", [128, NT], F32)
    rstd = sbt(top, "rstd", [128, NT], F32)
    gbc = [sbt(top, "gbc%d" % i, [128, D], F32) for i in range(1)]
    xn = [sbt(top, "xn%d" % i, [128, D], BF16) for i in range(2)]
    junk = sbt(top, "junk", [128, D], BF16)
    PSP = [top.enter_context(nc.psum_tensor("psp%d" % i, [128, 2, 512], F32)) for i in range(4)]
    PS = [PSP[i // 2][:, i % 2, :] for i in range(8)]

    def MM(out, lhsT, rhs, start, stop, reads, writes, signal, sgc=False):
        if sgc:
            p.op("pe", lambda e: e.matmul(out, lhsT=lhsT, rhs=rhs, start=start, stop=stop, skip_group_check=True), reads, writes, signal)
        else:
            p.op("pe", lambda e: e.matmul(out, lhsT=lhsT, rhs=rhs, start=start, stop=stop), reads, writes, signal)

    def TR(out, in_, ident, reads, writes, signal=True):
        p.op("pe", lambda e: e.transpose(out=out, in_=in_, identity=ident), reads, writes, signal)

    def ACT(out, in_, func, reads, writes, bias=None, scale=None, accum_out=None):
        kw = {}
        if bias is not None:
            kw["bias"] = bias
        if scale is not None:
            kw["scale"] = scale
        if accum_out is not None:
            kw["accum_out"] = accum_out
        p.op("act", lambda e: e.activation(out=out, in_=in_, func=func, **kw), reads, writes)

    def TT(eng, out, in0, in1, op, reads, writes):
        p.op(eng, lambda e: e.tensor_tensor(out=out, in0=in0, in1=in1, op=op), reads, writes)

    def TS(eng, out, in0, s1, s2, op0, op1, reads, writes):
        if op1 is None:
            p.op(eng, lambda e: e.tensor_scalar(out=out, in0=in0, scalar1=s1, scalar2=None, op0=op0), reads, writes)
        else:
            p.op(eng, lambda e: e.tensor_scalar(out=out, in0=in0, scalar1=s1, scalar2=s2, op0=op0, op1=op1), reads, writes)

    def STT(eng, out, in0, scalar, in1, op0, op1, reads, writes):
        p.op(eng, lambda e: e.scalar_tensor_tensor(out=out, in0=in0, scalar=scalar, in1=in1, op0=op0, op1=op1), reads, writes)

    def CP(eng, out, in_, reads, writes):
        p.op(eng, lambda e: e.tensor_copy(out=out, in_=in_), reads, writes)

    def MEMSET(eng, ap, val, writes):
        p.op(eng, lambda e: e.memset(ap, val), (), writes)

    def DMA(out, in_, key, reads, writes, eng="sp", slow=False):
        if slow:
            p.dma(lambda e: e.dma_start(out=out, in_=in_, allow_slow_non_contiguous=True), key, reads, writes, eng)
        else:
            p.dma(lambda e: e.dma_start(out=out, in_=in_), key, reads, writes, eng)

    def RECIP(out, in_, reads, writes):
        p.op("dve", lambda e: e.reciprocal(out=out, in_=in_), reads, writes)

    def REDUCE(out, in_, reads, writes):
        p.op("dve", lambda e: e.tensor_reduce(out=out, in_=in_, axis=AX.X, op=ALU.add), reads, writes)

    XR = lambda t: [("X", t, 0), ("X", t, 1)]

    for t in range(NT):
        DMA(X[:, t, :], x_d[t * 128:(t + 1) * 128, :], ("ldx", t), (), XR(t))
    DMA(identf[:], ident_d, "ldidf", (), ["identf"])
    DMA(identb[:], ident_d, "ldidb", (), ["identb"], eng="pool")

    gslot = [0]

    def load_gain(name):
        s = 0
        DMA(gbc[s][:], W[name].partition_broadcast(128), ("ldg", s), (), [("gbc", s)], eng="act")
        return s

    def rms_stats(t):
        ACT(junk[:], X[:, t, :], AF.Square, XR(t), ["junk", ("ss", t)], scale=1.0 / 32.0, accum_out=ss[:, t:t + 1])
        ACT(rstd[:, t:t + 1], ss[:, t:t + 1], AF.Sqrt, [("ss", t)], [("rstd", t)], bias=EPS)

    def rms_recip(t):
        p.op("dve", (lambda t: lambda e: e.reciprocal(out=rstd[:, t:t + 1], in_=rstd[:, t:t + 1]))(t), [("rstd", t)], [("rstd", t)])

    norm_state = {"ready": False, "out_done": False}

    def _stepper(apply_a, apply_b):
        def apply(t):
            apply_a(t)
            apply_b(t)

        def pre(t):
            if t >= 2:
                apply_a(t - 2)

        def mid(t):
            if t >= 2:
                apply_b(t - 2)

        def post(t):
            rms_stats(t)
            if t == NT - 1:
                apply(t - 1)
                apply(t)

        def step(t):
            pre(t)
            mid(t)
            post(t)
        step.stats = rms_stats
        step.apply = apply
        step.pre = pre
        step.mid = mid
        step.post = post
        return step

    def make_norm_HT(gain_name):
        def factory(st, tbanks=(6, 7)):
            gs = load_gain(gain_name)
            MEMSET("dve", ss[:], 0.0, [("ss", t) for t in range(NT)])

            def apply_a(t):
                s = t % 2
                rms_recip(t)
                STT("dve", xn[s][:], X[:, t, :], rstd[:, t:t + 1], gbc[gs][:], ALU.mult, ALU.mult,
                    XR(t) + [("rstd", t), ("gbc", gs)], [("xn", s)])

            def apply_b(t):
                s = t % 2
                tbk = tbanks[t % len(tbanks)]
                pst = PS[tbk].bitcast(BF16)
                for c in range(8):
                    TR(pst[:, c * 128:(c + 1) * 128], xn[s][:, c * 128:(c + 1) * 128], identb[:],
                       [("xn", s), "identb"], [("ps", tbk)], signal=(c == 7))
                src = pst[:, :].rearrange("p (c n) -> p c n", c=8)
                if t % 2 == 0 or len(tbanks) == 1:
                    ACT(HT[:, :, t * 128:(t + 1) * 128], src, AF.Copy, [("ps", tbk)], [("HT", t)])
                else:
                    CP("dve", HT[:, :, t * 128:(t + 1) * 128], src, [("ps", tbk)], [("HT", t)])
                if t == NT - 1:
                    norm_state["ready"] = True
            return _stepper(apply_a, apply_b)
        return factory

    def make_norm_out():
        def factory(st, tbanks=None):
            gs = load_gain("final_norm")
            MEMSET("dve", ss[:], 0.0, [("ss", t) for t in range(NT)])
            ot = [sbt(st, "ot%d" % i, [128, D], F32) for i in range(2)]

            def apply_a(t):
                s = t % 2
                rms_recip(t)
                STT("dve", ot[s][:], X[:, t, :], rstd[:, t:t + 1], gbc[gs][:], ALU.mult, ALU.mult,
                    XR(t) + [("rstd", t), ("gbc", gs)], [("ot", s)])

            def apply_b(t):
                s = t % 2
                DMA(out_d[t * 128:(t + 1) * 128, :], ot[s][:], ("st", s), [("ot", s)], ())
                if t == NT - 1:
                    norm_state["out_done"] = True
            return _stepper(apply_a, apply_b)
        return factory

    def ensure_norm(gain_name, st):
        if not norm_state["ready"]:
            step = make_norm_HT(gain_name)(st)
            for t in range(NT):
                step(t)
        norm_state["ready"] = False

    def ffn(pref, tail_factory):
        with ExitStack() as st:
            actT = sbt(st, "actT", [128, 11, S], BF16)
            wdt = sbt(st, "wdt", [128, 11, D], BF16)
            NSL = 4
            wgt = [sbt(st, "wgt%d" % i, [128, 8, 128], BF16) for i in range(NSL)]
            wut = [sbt(st, "wut%d" % i, [128, 8, 128], BF16) for i in range(NSL)]
            sg = [sbt(st, "sg%d" % i, [128, 512], F32) for i in range(2)]
            wgv = W[pref + "_w_gate"].rearrange("(kc p) n -> p kc n", p=128)
            wuv = W[pref + "_w_up"].rearrange("(kc p) n -> p kc n", p=128)
            wdd = W[pref + "_w_down"]

            def issue_gu(fc):
                s = fc % NSL
                DMA(wgt[s][:], wgv[:, :, fc * 128:(fc + 1) * 128], ("wg", s), (), [("wg", s)], eng="pool")
                DMA(wut[s][:], wuv[:, :, fc * 128:(fc + 1) * 128], ("wu", s), (), [("wu", s)], eng="pool")

            for fc in range(NSL - 1):
                issue_gu(fc)
            ensure_norm(pref + "_norm", st)
            step = 0
            for hf in range(2):
                for fcl in range(11):
                    fc = hf * 11 + fcl
                    if fc + NSL - 1 < NFC:
                        issue_gu(fc + NSL - 1)
                    DMA(wdt[:, fcl, :], wdd[fc * 128:(fc + 1) * 128, :], ("wd", fcl), (), [("wd", fcl)], eng="pool")
                    s = fc % NSL
                    for tb in range(4):
                        b = step % 2
                        step += 1
                        g, u = PS[b], PS[2 + b]
                        htr = [("HT", 4 * tb + i) for i in range(4)]
                        for kc in range(8):
                            MM(g[:], wgt[s][:, kc, :], HT[:, kc, tb * 512:(tb + 1) * 512], kc == 0, kc == 7,
                               [("wg", s)] + htr, [("ps", b)], kc == 7)
                        for kc in range(8):
                            MM(u[:], wut[s][:, kc, :], HT[:, kc, tb * 512:(tb + 1) * 512], kc == 0, kc == 7,
                               [("wu", s)] + htr, [("ps", 2 + b)], kc == 7)
                        ACT(sg[b][:], g[:], AF.Silu, [("ps", b)], [("sg", b)])
                        TT("dve", actT[:, fcl, tb * 512:(tb + 1) * 512], sg[b][:], u[:], ALU.mult,
                           [("sg", b), ("ps", 2 + b)], [("actT", fcl, tb)])
                dstep = 0
                tail = tail_factory(st) if hf == 1 else None
                for t in range(NT):
                    if tail is not None:
                        tail.pre(t)
                    for dh in range(2):
                        if tail is not None and dh == 1:
                            tail.mid(t)
                        b = 4 + dstep % 2
                        dstep += 1
                        for fcl in range(11):
                            MM(PS[b][:], actT[:, fcl, t * 128:(t + 1) * 128], wdt[:, fcl, dh * 512:(dh + 1) * 512],
                               fcl == 0, fcl == 10, [("actT", fcl, t // 4), ("wd", fcl)], [("ps", b)], fcl == 10)
                        STT("dve", X[:, t, dh * 512:(dh + 1) * 512], PS[b][:], 0.5, X[:, t, dh * 512:(dh + 1) * 512],
                            ALU.mult, ALU.add, [("ps", b), ("X", t, dh)], [("X", t, dh)])
                    if tail is not None:
                        tail.post(t)
        p.barrier()

    def mixer(tail_factory):
        import math
        wv_in = W["w_in"].rearrange("(kc p) n -> p kc n", p=128)
        with ExitStack() as stM:
            yaT = sbt(stM, "yaT", [128, 4, S], BF16)
            SC = sbt(stM, "SC", [128, 2, NT, 8], F32)
            rstdb = sbt(stM, "rstdb", [128, NT], F32)
            with ExitStack() as stA:
                wblk = [sbt(stA, "wblkA%d" % i, [128, 8, 512], BF16) for i in range(2)]
                DMA(wblk[0][:], wv_in[:, :, 512:1024], ("wb", 0), (), [("wblk", 0)], eng="pool")
                wsn = sbt(stA, "wsn", [128, 8, 128], F32)
                tri = sbt(stA, "tri", [128, 128], F32)
                wsT = sbt(stA, "wsT", [128, 8, 128], BF16)
                bsT = sbt(stA, "bsT", [128, 8], F32)
                lng = sbt(stA, "lng", [128, 512], F32)
                lnb = sbt(stA, "lnb", [128, 512], F32)
                gabc = sbt(stA, "gabc", [128, 512], F32)
                Gs = [sbt(stA, "Gs%d" % i, [128, 4, 512], F32) for i in range(2)]
                Ug = [sbt(stA, "Ug%d" % i, [128, 4, 512], F32) for i in range(2)]
                VLN = [sbt(stA, "VLN%d" % i, [128, 4, 512], BF16) for i in range(2)]
                sqA = sbt(stA, "sqA", [128, 512], F32)
                tmpA = [sbt(stA, "tmpA%d" % i, [128, 512], F32) for i in range(2)]
                yan = [sbt(stA, "yan%d" % i, [128, 512], BF16) for i in range(8)]
                s1 = [sbt(stA, "s1_%d" % i, [128, 32], F32) for i in range(2)]
                s2 = [sbt(stA, "s2_%d" % i, [128, 32], F32) for i in range(2)]
                mean = [sbt(stA, "mean%d" % i, [128, 32], F32) for i in range(2)]
                var = [sbt(stA, "var%d" % i, [128, 32], F32) for i in range(2)]
                rsg = [sbt(stA, "rsg%d" % i, [128, 32], F32) for i in range(2)]
                msa = sbt(stA, "msa", [128, NT], F32)
                rsa = sbt(stA, "rsa", [128, NT], F32)
                DMA(wsn[:], W["sgu_w_s"].rearrange("g i j -> i g j"), "ldws", (), ["wsn"])
                DMA(tri[:], tri_d, "ldtri", (), ["tri"])
                DMA(bsT[:], W["sgu_b_s"].rearrange("g i -> i g"), "ldbs", (), ["bsT"], slow=True)
                DMA(lng[:], W["sgu_ln_g"].partition_broadcast(128), "ldlng", (), ["lng"])
                DMA(lnb[:], W["sgu_ln_b"].partition_broadcast(128), "ldlnb", (), ["lnb"])
                DMA(gabc[:], W["out_norm_a"].partition_broadcast(128), "ldga", (), ["gabc"])
                ensure_norm("mix_norm", stA)
                for g in range(8):
                    b = 4 + g % 2
                    TR(PS[b][:, 0:128], wsn[:, g, :], identf[:], ["wsn", "identf"], [("ps", b)])
                    TT("dve", wsT[:, g, :], PS[b][:, 0:128], tri[:], ALU.mult, [("ps", b), "tri"], [("wsT", g)])
                MEMSET("dve", msa[:], 0.0, ["msa"])
                _stop("A0")
                def stage1a(G):
                    st_ = G % 2
                    for i in range(4):
                        t = 4 * G + i
                        b = t % 2
                        for kc in range(8):
                            MM(PS[b][:], HT[:, kc, t * 128:(t + 1) * 128], wblk[0][:, kc, :], kc == 0, kc == 7,
                               [("HT", t), ("wblk", 0)], [("ps", b)], kc == 7)
                        ACT(Gs[st_][:, i, :], PS[b][:], AF.Gelu, [("ps", b)], [("Gs", st_, i)])
                        REDUCE(s1[st_][:, i * 8:(i + 1) * 8], Gs[st_][:, i, :].rearrange("p (g c) -> p g c", g=8),
                               [("Gs", st_, i)], [("s1", st_, i)])
                        TT("dve", sqA[:], Gs[st_][:, i, :], Gs[st_][:, i, :], ALU.mult, [("Gs", st_, i)], ["sqA"])
                        REDUCE(s2[st_][:, i * 8:(i + 1) * 8], sqA[:].rearrange("p (g c) -> p g c", g=8),
                               ["sqA"], [("s2", st_, i)])
                    _stop("A1")
                    s1r = [("s1", st_, i) for i in range(4)]
                    s2r = [("s2", st_, i) for i in range(4)]
                    TS("dve", mean[st_][:], s1[st_][:], 1.0 / 64, None, ALU.mult, None, s1r, [("mean", st_)])
                    TT("dve", var[st_][:], mean[st_][:], mean[st_][:], ALU.mult, [("mean", st_)], [("var", st_)])
                    STT("dve", var[st_][:], s2[st_][:], 1.0 / 64, var[st_][:], ALU.mult, ALU.subtract,
                        s2r + [("var", st_)], [("var", st_)])
                    ACT(rsg[st_][:], var[st_][:], AF.Sqrt, [("var", st_)], [("rsg", st_)], bias=LN_EPS)

                def stage1b(G):
                    st_ = G % 2
                    RECIP(rsg[st_][:], rsg[st_][:], [("rsg", st_)], [("rsg", st_)])
                    _stop("A2")
                    for i in range(4):
                        ts_ = i % 2
                        g3 = Gs[st_][:, i, :].rearrange("p (g c) -> p g c", g=8)
                        t3 = tmpA[ts_][:].rearrange("p (g c) -> p g c", g=8)
                        TT("dve", t3, g3, mean[st_][:, i * 8:(i + 1) * 8].unsqueeze(2).to_broadcast([128, 8, 64]), ALU.subtract,
                           [("Gs", st_, i), ("mean", st_)], [("tmpA", ts_)])
                        TT("dve", t3, t3, rsg[st_][:, i * 8:(i + 1) * 8].unsqueeze(2).to_broadcast([128, 8, 64]), ALU.mult,
                           [("tmpA", ts_), ("rsg", st_)], [("tmpA", ts_)])
                        TT("pool", tmpA[ts_][:], tmpA[ts_][:], lng[:], ALU.mult, [("tmpA", ts_), "lng"], [("tmpA", ts_)])
                        TT("pool", VLN[st_][:, i, :], tmpA[ts_][:], lnb[:], ALU.add, [("tmpA", ts_), "lnb"], [("VLN", st_, i)])
                    _stop("A3")
                    for i in range(4):
                        t = 4 * G + i
                        b = t % 2
                        for kc in range(8):
                            MM(PS[b][:], HT[:, kc, t * 128:(t + 1) * 128], wblk[1][:, kc, :], kc == 0, kc == 7,
                               [("HT", t), ("wblk", 1)], [("ps", b)], kc == 7)
                        ACT(Ug[st_][:, i, :], PS[b][:], AF.Gelu, [("ps", b)], [("Ug", st_, i)])
                def stage2a(G):
                    st_ = G % 2
                    vr = [("VLN", st_, i) for i in range(4)]
                    ur = [("Ug", st_, i) for i in range(4)]
                    for g in range(8):
                        b = 2 + g % 2
                        MM(PS[b][:, 0:256], wsT[:, g, :], VLN[st_][:, :, g * 64:(g + 1) * 64], True, True,
                           [("wsT", g)] + vr, [("ps", b)], True)
                        STT("dve", Ug[st_][:, :, g * 64:(g + 1) * 64], PS[b][:, 0:256].rearrange("p (n c) -> p n c", n=4),
                            bsT[:, g:g + 1], Ug[st_][:, :, g * 64:(g + 1) * 64], ALU.add, ALU.mult,
                            [("ps", b), "bsT"] + ur, ur)
                    _stop("A5")
                    for i in range(4):
                        t = 4 * G + i
                        ACT(junk[:, 0:512], Ug[st_][:, i, :], AF.Square, [("Ug", st_, i)], ["junk", "msa"],
                            scale=1.0 / math.sqrt(512.0), accum_out=msa[:, t:t + 1])
                    ACT(rsa[:, 4 * G:4 * G + 4], msa[:, 4 * G:4 * G + 4], AF.Sqrt, ["msa"], [("rsa", G)], bias=EPS)

                def stage2b(G):
                    st_ = G % 2
                    RECIP(rsa[:, 4 * G:4 * G + 4], rsa[:, 4 * G:4 * G + 4], [("rsa", G)], [("rsa", G)])
                    _stop("A6")
                    for i in range(4):
                        t = 4 * G + i
                        s = t % 8
                        STT("dve", yan[s][:], Ug[st_][:, i, :], rsa[:, t:t + 1], gabc[:], ALU.mult, ALU.mult,
                            [("Ug", st_, i), ("rsa", G), "gabc"], [("yan", s)])

                def stage3(G):
                    for i in range(4):
                        t = 4 * G + i
                        s = t % 8
                        pst = PS[6 + t % 2].bitcast(BF16)
                        for c in range(4):
                            TR(pst[:, c * 128:(c + 1) * 128], yan[s][:, c * 128:(c + 1) * 128], identb[:],
                               [("yan", s), "identb"], [("ps", 6 + t % 2)], signal=(c == 3))
                        ACT(yaT[:, :, t * 128:(t + 1) * 128], pst[:, 0:512].rearrange("p (c n) -> p c n", c=4), AF.Copy,
                            [("ps", 6 + t % 2)], [("yaT", t)])

                stage1a(0)
                DMA(wblk[1][:], wv_in[:, :, 0:512], ("wb", 1), (), [("wblk", 1)], eng="pool")
                stage1a(1)
                stage1b(0)
                stage2a(0)
                stage1b(1)
                stage2b(0)
                stage1a(2)
                stage2a(1)
                stage3(0)
                stage1b(2)
                stage2b(1)
                stage1a(3)
                stage2a(2)
                stage3(1)
                stage1b(3)
                stage2b(2)
                stage2a(3)
                stage3(2)
                stage2b(3)
                stage3(3)
            p.barrier()
            _stop("A")
            with ExitStack() as stB:
                QT = sbt(stB, "QT", [128, 4, S], BF16)
                KT = sbt(stB, "KT", [128, 4, S], BF16)
                VA = sbt(stB, "VA", [128, NT, 8, 65], BF16)
                with ExitStack() as stB1:
                    wblk = [sbt(stB1, "wblkB%d" % i, [128, 8, 512], BF16) for i in range(2)]
                    DMA(wblk[0][:], wv_in[:, :, 1024:1536], ("wb", 0), (), [("wblk", 0)], eng="pool")
                    DMA(wblk[1][:], wv_in[:, :, 1536:2048], ("wb", 1), (), [("wblk", 1)], eng="pool")
                    posi = sbt(stB1, "posi", [128, NT], I32)
                    posf = sbt(stB1, "posf", [128, NT], F32)
                    invf = sbt(stB1, "invf", [128, 128], F32)
                    ANG = sbt(stB1, "ANG", [128, 2, 128], F32)
                    KF = sbt(stB1, "KF", [128, 256], F32)
                    KI = sbt(stB1, "KI", [128, 256], I32)
                    QR = [sbt(stB1, "QR%d" % i, [128, 512], BF16) for i in range(4)]
                    rt = [sbt(stB1, "rt%d" % i, [128, 8, 8], F32) for i in range(8)]
                    DMA(posi[:], pos_d, "ldpos", (), ["posi"])
                    DMA(invf[:], invf_d, "ldinvf", (), ["invf"])
                    MEMSET("pool", VA[:, :, :, 64:65], 1.0, ["VAones"])
                    CP("dve", posf[:], posi[:], ["posi"], ["posf"])
                    TT("dve", ANG[:, 0, :].rearrange("p (t i) -> p t i", i=8), invf[:].rearrange("p (t i) -> p t i", i=8),
                       posf[:].unsqueeze(2).to_broadcast([128, NT, 8]), ALU.mult, ["invf", "posf"], ["ANG"])
                    TS("dve", ANG[:, 1, :], ANG[:, 0, :], math.pi / 2, None, ALU.add, None, ["ANG"], ["ANG"])
                    A2 = ANG[:].rearrange("p a n -> p (a n)")
                    TS("dve", KF[:], A2, 1.0 / (2 * math.pi), None, ALU.mult, None, ["ANG"], ["KF"])
                    CP("dve", KI[:], KF[:], ["KF"], ["KI"])
                    CP("dve", KF[:], KI[:], ["KI"], ["KF"])
                    C1 = 6.28125
                    C2 = 2 * math.pi - C1
                    STT("dve", A2, KF[:], -C1, A2, ALU.mult, ALU.add, ["KF", "ANG"], ["ANG"])
                    STT("dve", A2, KF[:], -C2, A2, ALU.mult, ALU.add, ["KF", "ANG"], ["ANG"])
                    TS("dve", A2, A2, -math.pi, math.pi, ALU.max, ALU.min, ["ANG"], ["ANG"])
                    ACT(SC[:].rearrange("p a t i -> p (a t i)"), A2, AF.Sin, ["ANG"], ["SC"])

                    _stop("B1a")

                    def rotary(ps, bank, dst, dkey, t, k0):
                        ps3 = ps[:].rearrange("p (h e) -> p h e", h=8)
                        d3 = dst[:].rearrange("p (h e) -> p h e", h=8)
                        sinb = SC[:, 0, t:t + 1, :].to_broadcast([128, 8, 8])
                        cosb = SC[:, 1, t:t + 1, :].to_broadcast([128, 8, 8])
                        pr = [("ps", bank), "SC"]
                        CP("dve", d3[:, :, 16:64], ps3[:, :, 16:64], [("ps", bank)], [dkey])
                        TT("dve", rt[k0][:], ps3[:, :, 0:8], cosb, ALU.mult, pr, [("rt", k0)])
                        TT("dve", rt[k0 + 1][:], ps3[:, :, 8:16], sinb, ALU.mult, pr, [("rt", k0 + 1)])
                        TT("dve", d3[:, :, 0:8], rt[k0][:], rt[k0 + 1][:], ALU.subtract, [("rt", k0), ("rt", k0 + 1)], [dkey])
                        TT("dve", rt[k0 + 2][:], ps3[:, :, 8:16], cosb, ALU.mult, pr, [("rt", k0 + 2)])
                        TT("dve", rt[k0 + 3][:], ps3[:, :, 0:8], sinb, ALU.mult, pr, [("rt", k0 + 3)])
                        TT("dve", d3[:, :, 8:16], rt[k0 + 2][:], rt[k0 + 3][:], ALU.add, [("rt", k0 + 2), ("rt", k0 + 3)], [dkey])

                    def projqk(t):
                        for qk in range(2):
                            b = qk * 2 + t % 2
                            for kc in range(8):
                                MM(PS[b][:], HT[:, kc, t * 128:(t + 1) * 128], wblk[qk][:, kc, :], kc == 0, kc == 7,
                                   [("HT", t), ("wblk", qk)], [("ps", b)], kc == 7)
                            qs = qk * 2 + t % 2
                            rotary(PS[b], b, QR[qs], ("QR", qs), t, 4 * qk)

                    def trqk(t):
                        for qk in range(2):
                            qs = qk * 2 + t % 2
                            tb_ = 6 + qk
                            pst = PS[tb_].bitcast(BF16)
                            for c in range(4):
                                TR(pst[:, c * 128:(c + 1) * 128], QR[qs][:, c * 128:(c + 1) * 128], identb[:],
                                   [("QR", qs), "identb"], [("ps", tb_)], signal=(c == 3))
                            dstT = QT if qk == 0 else KT
                            ACT(dstT[:, :, t * 128:(t + 1) * 128], pst[:, 0:512].rearrange("p (c n) -> p c n", c=4),
                                AF.Copy, [("ps", tb_)], [("QKT", qk, t)])

                    projqk(0)
                    for t in range(NT):
                        if t + 1 < NT:
                            projqk(t + 1)
                        trqk(t)
                    _stop("B1c")
                    DMA(wblk[0][:], wv_in[:, :, 2048:2560], ("wb", 0), (), [("wblk", 0)], eng="pool")
                    for t in range(NT):
                        b = 4 + t % 2
                        for kc in range(8):
                            MM(PS[b][:], HT[:, kc, t * 128:(t + 1) * 128], wblk[0][:, kc, :], kc == 0, kc == 7,
                               [("HT", t), ("wblk", 0)], [("ps", b)], kc == 7)
                        ACT(VA[:, t, :, 0:64], PS[b][:].rearrange("p (h e) -> p h e", h=8), AF.Copy,
                            [("ps", b)], [("VA", t)])
                p.barrier()
                _stop("B1")
                with ExitStack() as stB2:
                    cm = sbt(stB2, "cm", [128, S], BF16)
                    NPB2 = 5
                    pb2 = [sbt(stB2, "pb%d" % i, [128, 2, 512], BF16) for i in range(NPB2)]
                    YB = [sbt(stB2, "YB%d" % i, [128, 4, 512], F32) for i in range(2)]
                    rzt = [sbt(stB2, "rzt%d" % i, [128, 4], F32) for i in range(4)]
                    ybn = [sbt(stB2, "ybn%d" % i, [128, 512], BF16) for i in range(2)]
                    gbbc = sbt(stB2, "gbbc", [128, 512], F32)
                    msb = sbt(stB2, "msb", [128, NT], F32)
                    YTb = HT[:, 0:4, :]
                    WO = HT[:, 4:8, :].rearrange("p a (b n) -> p (a b) n", n=1024)
                    DMA(cm[:], cmask_d, "ldcm", (), ["cm"], eng="pool")
                    DMA(gbbc[:], W["out_norm_b"].partition_broadcast(128), "ldgb", (), ["gbbc"])
                    DMA(WO, W["w_out"].rearrange("(kc p) n -> p kc n", p=128), "ldwo", (), ["WO"], eng="pool")
                    MEMSET("dve", msb[:], 0.0, ["msb"])
                    steps = []
                    for b in range(4):
                        for c in range(4):
                            nj = 4 * b + 4
                            for j in range(nj):
                                steps.append((b, c, j, nj))
                    LA = 4
                    SBK = [(0, 1), (6, 7)]
                    pending = []

                    def front(i):
                        b, c, j, nj = steps[i]
                        qlo = max(512 * b, 128 * j)
                        N = 512 * (b + 1) - qlo
                        for e in range(2):
                            r0 = e * 64
                            sb_ = SBK[i % 2][e]
                            MM(PS[sb_][:, 0:N], KT[r0:r0 + 64, c, j * 128:(j + 1) * 128], QT[r0:r0 + 64, c, qlo:qlo + N],
                               True, True, (), [("ps", sb_)], True)
                        pp = PSP[SBK[i % 2][0] // 2]
                        ps_ = i % NPB2
                        prs = [("ps", SBK[i % 2][0]), ("ps", SBK[i % 2][1])]
                        ACT(pb2[ps_][:, :, 0:N], pp[:, :, 0:N], AF.Exp, prs, [("pb", ps_)], scale=0.125)
                        TT("pool" if i % 3 == 0 else "dve", pb2[ps_][:, :, 0:N], pb2[ps_][:, :, 0:N],
                           cm[:, qlo - 128 * j:qlo - 128 * j + N].unsqueeze(1).to_broadcast([128, 2, N]), ALU.mult,
                           [("pb", ps_), "cm"], [("pb", ps_)])

                    def back(i):
                        b, c, j, nj = steps[i]
                        qlo = max(512 * b, 128 * j)
                        N = 512 * (b + 1) - qlo
                        nqb = N // 128
                        qb0 = 4 - nqb
                        g = b * 4 + c
                        for e in range(2):
                            h = 2 * c + e
                            ab = (2, 3)[e] if g % 2 == 0 else (4, 5)[e]
                            ps_ = i % NPB2
                            for qi in range(nqb):
                                qb = qb0 + qi
                                last = (qi == nqb - 1)
                                MM(PS[ab][:, qb * 65:(qb + 1) * 65], pb2[ps_][:, e, qi * 128:(qi + 1) * 128], VA[:, j, h, :],
                                   (j == 0 and qi == 0), (j == nj - 1), [("pb", ps_), "VAones"], [("ps", ab)],
                                   last and (j == nj - 1), sgc=True)
                        if j == nj - 1:
                            pending.append((i + 1, epi, b, c))

                    def epi(b, c):
                        g = b * 4 + c
                        ys = b % 2
                        for e in range(2):
                            h = 2 * c + e
                            ab = (2, 3)[e] if g % 2 == 0 else (4, 5)[e]
                            acc3 = PS[ab][:, 0:260].rearrange("p (q e) -> p q e", q=4)
                            rs = (2 * g + e) % 4
                            RECIP(rzt[rs][:].unsqueeze(2), acc3[:, :, 64:65], [("ps", ab)], [("rzt", rs)])
                            TT("dve", YB[ys][:, :, h * 64:(h + 1) * 64], acc3[:, :, 0:64],
                               rzt[rs][:].unsqueeze(2).to_broadcast([128, 4, 64]), ALU.mult,
                               [("ps", ab), ("rzt", rs)], [("YB", ys, h)])
                        if c == 3:
                            pending.append((0, bank_tail, b, 0))

                    def bank_tail(b, _):
                        ys = b % 2
                        ybr_ = [("YB", ys, h) for h in range(8)]
                        for i4 in range(4):
                            t = 4 * b + i4
                            s = t % 2
                            ACT(junk[:, 0:512], YB[ys][:, i4, :], AF.Square, ybr_, ["junk", "msb"],
                                scale=1.0 / math.sqrt(512.0), accum_out=msb[:, t:t + 1])
                            TT("pool", ybn[s][:], YB[ys][:, i4, :], gbbc[:], ALU.mult, ybr_ + ["gbbc"], [("ybn", s)])
                            tbk = 6 + s
                            pst = PS[tbk].bitcast(BF16)
                            for cc in range(4):
                                TR(pst[:, cc * 128:(cc + 1) * 128], ybn[s][:, cc * 128:(cc + 1) * 128], identb[:],
                                   [("ybn", s), "identb"], [("ps", tbk)], signal=(cc == 3))
                            ACT(YTb[:, :, t * 128:(t + 1) * 128], pst[:, 0:512].rearrange("p (c n) -> p c n", c=4),
                                AF.Copy, [("ps", tbk)], [("YTb", t)])

                    nst = len(steps)
                    for i in range(nst + LA + 4):
                        if i < nst:
                            front(i)
                        if 0 <= i - LA < nst:
                            back(i - LA)
                        due = [q for q in pending if q[0] <= i - LA]
                        for q in due:
                            pending.remove(q)
                            q[1](q[2], q[3])
                    while pending:
                        q = pending.pop(0)
                        q[1](q[2], q[3])
                    _stop("B2a")
                    ACT(rstdb[:], msb[:], AF.Sqrt, ["msb"], ["rstdb"], bias=EPS)
                    RECIP(rstdb[:], rstdb[:], ["rstdb"], ["rstdb"])
                    ybr = []
                    k = 0
                    tail = tail_factory(stB2)
                    for t in range(NT):
                        for dh in range(2):
                            ba = k % 2
                            bb = 2 + k % 2
                            k += 1
                            for c in range(4):
                                MM(PS[ba][:], yaT[:, c, t * 128:(t + 1) * 128], WO[:, c, dh * 512:(dh + 1) * 512],
                                   c == 0, c == 3, ["WO"], [("ps", ba)], c == 3)
                            for c in range(4):
                                MM(PS[bb][:], YTb[:, c, t * 128:(t + 1) * 128], WO[:, 4 + c, dh * 512:(dh + 1) * 512],
                                   c == 0, c == 3, ["WO", ("YTb", t)], [("ps", bb)], c == 3)
                            xs = X[:, t, dh * 512:(dh + 1) * 512]
                            TT("dve", xs, xs, PS[ba][:], ALU.add, [("ps", ba), ("X", t, dh)], [("X", t, dh)])
                            STT("dve", xs, PS[bb][:], rstdb[:, t:t + 1], xs, ALU.mult, ALU.add,
                                [("ps", bb), ("X", t, dh), "rstdb"], [("X", t, dh)])
                        tail.stats(t)
                    p.barrier()
                    for t in range(NT):
                        tail.apply(t)
        p.barrier()

    def cross(tail_factory):
        with ExitStack() as st:
            ensure_norm("cross_norm", st)
            memt = sbt(st, "memt", [128, 2, D], F32)
            MT = sbt(st, "MT", [128, 8, NMEM], BF16)
            KxT = sbt(st, "KxT", [128, 8, NMEM], BF16)
            Vx = sbt(st, "Vx", [128, 2, D], BF16)
            QxT = sbt(st, "QxT", [128, 8, S], BF16)
            wsl = [sbt(st, "wx%d" % i, [128, 8, 512], BF16) for i in range(3)]
            onesb = sbt(st, "onesb", [128, 128], BF16)
            pbx = [sbt(st, "pbx%d" % i, [128, 512], BF16) for i in range(4)]
            rzx = [sbt(st, "rzx%d" % i, [128, 512], F32) for i in range(2)]
            mss = sbt(st, "mss", [128, 2], F32)
            mrs = sbt(st, "mrs", [128, 2], F32)
            wplan = [("cross_wq", 0), ("cross_wk", 0), ("cross_wk", 1), ("cross_wv", 0), ("cross_wv", 1),
                     ("cross_wq", 1), ("cross_wo", 0), ("cross_wo", 1)]
            sl_of = {}

            wslot = [0, 1, 2, 1, 2, 1, 2, 0]

            def load_w(i):
                sl = wslot[i]
                nm, blk = wplan[i]
                src = W[nm].rearrange("(kc p) n -> p kc n", p=128)[:, :, blk * 512:(blk + 1) * 512]
                DMA(wsl[sl][:], src, ("wx", sl), (), [("wx", sl)], eng="pool")
                sl_of[i] = sl

            load_w(0)
            MEMSET("pool", onesb[:], 1.0, ["onesb"])
            kk = [0]

            def QP(hc, tb):
                sl = sl_of[0] if hc < 4 else sl_of[5]
                b = kk[0] % 2
                kk[0] += 1
                htr = [("HT", 4 * tb + i) for i in range(4)]
                for kc in range(8):
                    MM(PS[b][:], wsl[sl][:, kc, (hc % 4) * 128:(hc % 4 + 1) * 128], HT[:, kc, tb * 512:(tb + 1) * 512],
                       kc == 0, kc == 7, [("wx", sl)] + htr, [("ps", b)], kc == 7)
                if kk[0] % 2 == 0 or hc >= 2:
                    ACT(QxT[:, hc, tb * 512:(tb + 1) * 512], PS[b][:], AF.Copy, [("ps", b)], [("QxT", hc, tb)])
                else:
                    CP("dve", QxT[:, hc, tb * 512:(tb + 1) * 512], PS[b][:], [("ps", b)], [("QxT", hc, tb)])

            for hc in range(2):
                for tb in range(4):
                    QP(hc, tb)
                if hc == 0:
                    load_w(1)
                    load_w(2)
                    for mt in range(2):
                        DMA(memt[:, mt, :], mem_d[mt * 128:(mt + 1) * 128, :], ("ldm", mt), (), [("memt", mt)])
            gm = load_gain("mem_norm")
            MEMSET("dve", mss[:], 0.0, ["mss"])
            for mt in range(2):
                ACT(junk[:], memt[:, mt, :], AF.Square, [("memt", mt)], ["junk", "mss"], scale=1.0 / 32.0,
                    accum_out=mss[:, mt:mt + 1])
            ACT(mrs[:], mss[:], AF.Sqrt, ["mss"], ["mrs"], bias=EPS)
            RECIP(mrs[:], mrs[:], ["mrs"], ["mrs"])
            for mt in range(2):
                s = mt % 2
                STT("dve", xn[s][:], memt[:, mt, :], mrs[:, mt:mt + 1], gbc[gm][:], ALU.mult, ALU.mult,
                    [("memt", mt), "mrs", ("gbc", gm)], [("xn", s)])
                pst = PS[6 + s].bitcast(BF16)
                for c in range(8):
                    TR(pst[:, c * 128:(c + 1) * 128], xn[s][:, c * 128:(c + 1) * 128], identb[:],
                       [("xn", s), "identb"], [("ps", 6 + s)], signal=(c == 7))
                ACT(MT[:, :, mt * 128:(mt + 1) * 128], pst[:, :].rearrange("p (c n) -> p c n", c=8), AF.Copy,
                    [("ps", 6 + s)], [("MT", mt)])
            mtr = [("MT", 0), ("MT", 1)]
            for hc in range(8):
                sl = sl_of[1 + hc // 4]
                b = kk[0] % 2
                kk[0] += 1
                for kc in range(8):
                    MM(PS[b][:, 0:NMEM], wsl[sl][:, kc, (hc % 4) * 128:(hc % 4 + 1) * 128], MT[:, kc, :], kc == 0, kc == 7,
                       [("wx", sl)] + mtr, [("ps", b)], kc == 7)
                ACT(KxT[:, hc, :], PS[b][:, 0:NMEM], AF.Copy, [("ps", b)], [("KxT", hc)])
                if hc == 3:
                    load_w(3)
            load_w(4)
            for dh in range(2):
                sl = sl_of[3 + dh]
                for mt in range(2):
                    b = kk[0] % 2
                    kk[0] += 1
                    for kc in range(8):
                        MM(PS[b][:], MT[:, kc, mt * 128:(mt + 1) * 128], wsl[sl][:, kc, :], kc == 0, kc == 7,
                           [("wx", sl), ("MT", mt)], [("ps", b)], kc == 7)
                    ACT(Vx[:, mt, dh * 512:(dh + 1) * 512], PS[b][:], AF.Copy, [("ps", b)], [("Vx", mt, dh)])
                load_w(5 + dh)
            itc = [0]

            def ATT(h, tb, alt=False):
                par = itc[0] % 2
                itc[0] += 1
                sbk = (2, 3)
                ob = (4, 5)
                zb = 7
                if alt and par == 1:
                    ob = (0, 1)
                    zb = 6
                cols = slice(tb * 512, (tb + 1) * 512)
                for mt in range(2):
                    for ec in range(2):
                        MM(PS[sbk[mt]][:], KxT[:, 2 * h + ec, mt * 128:(mt + 1) * 128], QxT[:, 2 * h + ec, cols], ec == 0, ec == 1,
                           [("KxT", 2 * h + ec), ("QxT", 2 * h + ec, tb)], [("ps", sbk[mt])], ec == 1)
                    ACT(pbx[par * 2 + mt][:], PS[sbk[mt]][:], AF.Exp, [("ps", sbk[mt])], [("pbx", par * 2 + mt)], scale=1.0 / 16.0)
                for ec in range(2):
                    for mt in range(2):
                        MM(PS[ob[ec]][:], Vx[:, mt, (2 * h + ec) * 128:(2 * h + ec + 1) * 128], pbx[par * 2 + mt][:], mt == 0, mt == 1,
                           [("Vx", mt, (2 * h + ec) // 4), ("pbx", par * 2 + mt)], [("ps", ob[ec])], mt == 1)
                for mt in range(2):
                    MM(PS[zb][:], onesb[:], pbx[par * 2 + mt][:], mt == 0, mt == 1, ["onesb", ("pbx", par * 2 + mt)],
                       [("ps", zb)], mt == 1)
                RECIP(rzx[par][:], PS[zb][:], [("ps", zb)], [("rzx", par)])
                for ec in range(2):
                    TT("dve", QxT[:, 2 * h + ec, cols], PS[ob[ec]][:], rzx[par][:], ALU.mult,
                       [("ps", ob[ec]), ("rzx", par)], [("QxT", 2 * h + ec, tb)])

            for h in range(3):
                qpl = [(hc, tb) for hc in (2 * h + 2, 2 * h + 3) for tb in range(4)]
                for i, (hc, tb) in enumerate(qpl):
                    QP(hc, tb)
                    if i % 2 == 1:
                        ATT(h, i // 2)
                if h == 0:
                    load_w(7)
            for tb in range(4):
                ATT(3, tb, alt=True)
            tail = tail_factory(st)
            for t in range(NT):
                tail.pre(t)
                for dh in range(2):
                    if dh == 1:
                        tail.mid(t)
                    b = kk[0] % 2
                    kk[0] += 1
                    sl = sl_of[6 + dh]
                    for hc in range(8):
                        MM(PS[b][:], QxT[:, hc, t * 128:(t + 1) * 128], wsl[sl][:, hc, :], hc == 0, hc == 7,
                           [("QxT", hc, t // 4), ("wx", sl)], [("ps", b)], hc == 7)
                    xs = X[:, t, dh * 512:(dh + 1) * 512]
                    TT("dve", xs, xs, PS[b][:], ALU.add, [("ps", b), ("X", t, dh)], [("X", t, dh)])
                tail.post(t)
        p.barrier()

    seq = [q for q in ("ffn1", "mix", "cross", "ffn2") if q in stages]
    gains = {"ffn1": "ffn1_norm", "mix": "mix_norm", "cross": "cross_norm", "ffn2": "ffn2_norm"}
    for i, sname in enumerate(seq):
        nxt = seq[i + 1] if i + 1 < len(seq) else None
        tf = make_norm_HT(gains[nxt]) if nxt else make_norm_out()
        if sname in ("ffn1", "ffn2"):
            ffn(sname, tf)
        elif sname == "mix":
            mixer(tf)
            p.muted = False
            p.barrier()
        else:
            cross(tf)
    if not norm_state["out_done"]:
        with ExitStack() as st:
            step = make_norm_out()(st)
            for t in range(NT):
                step(t)
    p.emit()
    top.close()
    return nc


_CACHE = {}


def _consts():
    ident = np.eye(128, dtype=np.float32)
    j = np.arange(128)[:, None]
    i = np.arange(128)[None, :]
    tri = (j <= i).astype(np.float32)
    d = np.arange(S)[None, :] - np.arange(128)[:, None]
    cm = ((d >= 0) & (d <= 128)).astype(np.float32) + ((d >= 0) & (d % 4 == 0) & (d <= 512)).astype(np.float32) \
        + ((d >= 0) & (d % 16 == 0)).astype(np.float32)
    ex = (-2.0 * np.arange(8, dtype=np.float32) / np.float32(16.0)).astype(np.float32)
    invf = np.power(np.float32(500000.0), ex).astype(np.float32)
    invf_t = np.tile(invf[None, :], (128, 16)).astype(np.float32)
    return {"ident": ident, "tri": tri, "cmask": cm.astype(np.float32), "invf": invf_t}


def kernel(**inputs):
    n = 8
    if "nc" not in _CACHE:
        _CACHE["nc"] = build_nc()
    nc = _CACHE["nc"]
    cst = _consts()
    shared = {}
    for k, v in inputs.items():
        if k in ("x", "mem", "positions"):
            continue
        a = np.asarray(v)
        if k == "final_norm":
            a = a.reshape(1, D)
        elif k in ("sgu_w_s", "sgu_b_s"):
            a = a[0]
        elif a.ndim == 2:
            pass
        else:
            a = a[0]
        shared[k] = np.ascontiguousarray(a, dtype=np.float32)
    x = np.asarray(inputs["x"], dtype=np.float32)
    mem = np.asarray(inputs["mem"], dtype=np.float32)
    pos = np.asarray(inputs["positions"], dtype=np.int32)
    in_maps = []
    for b in range(n):
        m = dict(shared)
        m.update(cst)
        m["x"] = np.ascontiguousarray(x[b])
        m["mem"] = np.ascontiguousarray(mem[b])
        m["pos"] = np.ascontiguousarray(pos[b].reshape(NT, 128).T)
        in_maps.append(m)
    res = run_bass_kernel_spmd(nc, in_maps, core_ids=list(range(n)))
    return np.stack([np.asarray(r["out"], dtype=np.float32) for r in res.results], axis=0)
```

```python
import numpy as np
from contextlib import ExitStack
import concourse.bass as bass
import concourse.mybir as mybir
from concourse.bass_utils import run_bass_kernel_spmd

F32 = mybir.dt.float32
BF16 = mybir.dt.bfloat16
I32 = mybir.dt.int32
AF = mybir.ActivationFunctionType
ALU = mybir.AluOpType
AX = mybir.AxisListType

ENGS = ("pe", "act", "dve", "pool", "sp")

S = 2048
D = 1024
NT = 16
DFF = 2816
NFC = 22
NMEM = 256
EPS = 1e-6
LN_EPS = 1e-5
import os as _os
MIXSTOP = _os.environ.get("MIXSTOP", "")
SAFE_SAME_ENGINE = True


class _Stop(Exception):
    pass


_PROG = [None]


def _stop(tag):
    if MIXSTOP == tag:
        _PROG[0].muted = True


class _Op:
    __slots__ = ("eng", "fn", "deps", "signal", "dma_key", "dma_cnt", "idx", "sigcnt")


class Prog:
    def __init__(self, nc):
        self.nc = nc
        self.ops = {e: [] for e in ENGS}
        self.last_w = {}
        self.readers = {}
        self.dma_cnt = {}
        self.bar_tokens = []
        self.bar_gen = 0
        self.eng_gen = {e: 0 for e in ENGS}
        self.muted = False
        _PROG[0] = self

    def _add(self, eng, fn, reads, writes, signal, dma_key):
        if self.muted:
            return None
        op = _Op()
        op.eng = eng
        op.fn = fn
        op.signal = signal
        op.dma_key = dma_key
        op.idx = len(self.ops[eng])
        deps = []
        if self.eng_gen[eng] < self.bar_gen:
            self.eng_gen[eng] = self.bar_gen
            for tok in self.bar_tokens:
                deps.append((tok, True))
        for r in reads:
            w = self.last_w.get(r)
            if w is not None:
                deps.append((w, True))
        for r in writes:
            w = self.last_w.get(r)
            if w is not None:
                deps.append((w, False))
            for rd in self.readers.get(r, ()):
                deps.append((rd, False))
        op.deps = deps
        if dma_key is not None:
            c = self.dma_cnt.get(dma_key, 0) + 16
            self.dma_cnt[dma_key] = c
            op.dma_cnt = c
            tok = ("d", dma_key, c)
        else:
            tok = ("c", eng, op.idx)
        for r in writes:
            self.last_w[r] = tok
            self.readers[r] = []
        for r in reads:
            self.readers.setdefault(r, []).append(tok)
        self.ops[eng].append(op)
        return op

    def op(self, eng, fn, reads=(), writes=(), signal=True):
        return self._add(eng, fn, tuple(reads), tuple(writes), signal, None)

    def dma(self, fn, key, reads=(), writes=(), eng="sp"):
        return self._add(eng, fn, tuple(reads), tuple(writes), False, key)

    def barrier(self):
        if self.muted:
            return
        toks = []
        for e in ENGS:
            lst = self.ops[e]
            for i in range(len(lst) - 1, -1, -1):
                if lst[i].dma_key is None:
                    lst[i].signal = True
                    toks.append(("c", e, i))
                    break
        for k, c in self.dma_cnt.items():
            toks.append(("d", k, c))
        self.bar_tokens = toks
        self.bar_gen += 1
        self.last_w = {}
        self.readers = {}

    def emit(self):
        nc = self.nc
        sig_at = {}
        for e in ENGS:
            cnt = 0
            lst = self.ops[e]
            for op in lst:
                if op.dma_key is None and op.signal:
                    cnt += 1
                    op.sigcnt = cnt
            need = [None] * len(lst)
            nxt = None
            for i in range(len(lst) - 1, -1, -1):
                op = lst[i]
                if op.dma_key is None and op.signal:
                    nxt = op.sigcnt
                need[i] = nxt
            sig_at[e] = need
        es = ExitStack()
        sems = {}
        for e in ENGS:
            sems[e] = es.enter_context(nc.semaphore("s_" + e))
        dsems = {}
        for i, k in enumerate(self.dma_cnt):
            dsems[k] = es.enter_context(nc.semaphore("d%d" % i))
        block = es.enter_context(nc.Block())

        def run_engine(e, eo):
            waited = {}
            for op in self.ops[e]:
                wl = {}
                for (tok, raw) in op.deps:
                    if tok[0] == "c":
                        pe_, idx = tok[1], tok[2]
                        if pe_ == e:
                            if e == "pe" or e == "sp" or ((not raw) and not SAFE_SAME_ENGINE):
                                continue
                        val = sig_at[pe_][idx]
                        if val is None:
                            raise RuntimeError("dep on unsignaled tail op %s %d" % (pe_, idx))
                        s = sems[pe_]
                        k = ("c", pe_)
                    else:
                        s = dsems[tok[1]]
                        val = tok[2]
                        k = ("d", tok[1])
                    if waited.get(k, 0) >= val:
                        continue
                    if wl.get(k, (None, 0))[1] < val:
                        wl[k] = (s, val)
                for k, (s, val) in wl.items():
                    eo.wait_ge(s, val)
                    waited[k] = val
                ins = op.fn(eo)
                if op.dma_key is not None:
                    ins.then_inc(dsems[op.dma_key], 16)
                elif op.signal:
                    ins.then_inc(sems[e], 1)

        @block.tensor
        def _(eo):
            run_engine("pe", eo)

        @block.scalar
        def _(eo):
            run_engine("act", eo)

        @block.vector
        def _(eo):
            run_engine("dve", eo)

        @block.gpsimd
        def _(eo):
            run_engine("pool", eo)

        @block.sync
        def _(eo):
            run_engine("sp", eo)
            for k, c in self.dma_cnt.items():
                eo.wait_ge(dsems[k], c)

        es.close()


def build_nc(stages=("ffn1", "mix", "cross", "ffn2"), dbg=False):
    nc = bass.Bass("TRN2", target_bir_lowering=False)
    dram_in = lambda name, shape, dt=F32: nc.dram_tensor(name, shape, dt, kind="ExternalInput").ap()
    x_d = dram_in("x", [S, D])
    mem_d = dram_in("mem", [NMEM, D])
    pos_d = dram_in("pos", [128, NT], I32)
    ident_d = dram_in("ident", [128, 128])
    tri_d = dram_in("tri", [128, 128])
    cmask_d = dram_in("cmask", [128, S])
    invf_d = dram_in("invf", [128, 128])
    W = {}
    for nm, sh in [("ffn1_norm", [1, D]), ("ffn1_w_gate", [D, DFF]), ("ffn1_w_up", [D, DFF]), ("ffn1_w_down", [DFF, D]),
                   ("mix_norm", [1, D]), ("w_in", [D, 2560]), ("sgu_ln_g", [1, 512]), ("sgu_ln_b", [1, 512]),
                   ("sgu_w_s", [8, 128, 128]), ("sgu_b_s", [8, 128]), ("out_norm_a", [1, 512]), ("out_norm_b", [1, 512]),
                   ("w_out", [D, D]), ("cross_norm", [1, D]), ("mem_norm", [1, D]), ("cross_wq", [D, D]),
                   ("cross_wk", [D, D]), ("cross_wv", [D, D]), ("cross_wo", [D, D]), ("ffn2_norm", [1, D]),
                   ("ffn2_w_gate", [D, DFF]), ("ffn2_w_up", [D, DFF]), ("ffn2_w_down", [DFF, D]), ("final_norm", [1, D])]:
        W[nm] = dram_in(nm, sh)
    out_d = nc.dram_tensor("out", [S, D], F32, kind="ExternalOutput").ap()

    top = ExitStack()
    _uid = [0]

    def sbt(st, name, shape, dt):
        _uid[0] += 1
        return st.enter_context(nc.sbuf_tensor("s%d_%s" % (_uid[0], name), shape, dt))
    p = Prog(nc)

    X = sbt(top, "X", [128, NT, D], F32)
    HT = sbt(top, "HT", [128, 8, S], BF16)
    identb = sbt(top, "identb", [128, 128], BF16)
    identf = sbt(top, "identf", [128, 128], F32)
    ss = sbt(top, "ss", [128, NT], F32)
    rstd = sbt(top, "rstd", [128, NT], F32)
    gbc = [sbt(top, "gbc%d" % i, [128, D], F32) for i in range(1)]
    xn = [sbt(top, "xn%d" % i, [128, D], BF16) for i in range(2)]
    junk = sbt(top, "junk", [128, D], BF16)
    PSP = [top.enter_context(nc.psum_tensor("psp%d" % i, [128, 2, 512], F32)) for i in range(4)]
    PS = [PSP[i // 2][:, i % 2, :] for i in range(8)]

    def MM(out, lhsT, rhs, start, stop, reads, writes, signal, sgc=False):
        if sgc:
            p.op("pe", lambda e: e.matmul(out, lhsT=lhsT, rhs=rhs, start=start, stop=stop, skip_group_check=True), reads, writes, signal)
        else:
            p.op("pe", lambda e: e.matmul(out, lhsT=lhsT, rhs=rhs, start=start, stop=stop), reads, writes, signal)

    def TR(out, in_, ident, reads, writes, signal=True):
        p.op("pe", lambda e: e.transpose(out=out, in_=in_, identity=ident), reads, writes, signal)

    def ACT(out, in_, func, reads, writes, bias=None, scale=None, accum_out=None):
        kw = {}
        if bias is not None:
            kw["bias"] = bias
        if scale is not None:
            kw["scale"] = scale
        if accum_out is not None:
            kw["accum_out"] = accum_out
        p.op("act", lambda e: e.activation(out=out, in_=in_, func=func, **kw), reads, writes)

    def TT(eng, out, in0, in1, op, reads, writes):
        p.op(eng, lambda e: e.tensor_tensor(out=out, in0=in0, in1=in1, op=op), reads, writes)

    def TS(eng, out, in0, s1, s2, op0, op1, reads, writes):
        if op1 is None:
            p.op(eng, lambda e: e.tensor_scalar(out=out, in0=in0, scalar1=s1, scalar2=None, op0=op0), reads, writes)
        else:
            p.op(eng, lambda e: e.tensor_scalar(out=out, in0=in0, scalar1=s1, scalar2=s2, op0=op0, op1=op1), reads, writes)

    def STT(eng, out, in0, scalar, in1, op0, op1, reads, writes):
        p.op(eng, lambda e: e.scalar_tensor_tensor(out=out, in0=in0, scalar=scalar, in1=in1, op0=op0, op1=op1), reads, writes)

    def CP(eng, out, in_, reads, writes):
        p.op(eng, lambda e: e.tensor_copy(out=out, in_=in_), reads, writes)

    def MEMSET(eng, ap, val, writes):
        p.op(eng, lambda e: e.memset(ap, val), (), writes)

    def DMA(out, in_, key, reads, writes, eng="sp", slow=False):
        if slow:
            p.dma(lambda e: e.dma_start(out=out, in_=in_, allow_slow_non_contiguous=True), key, reads, writes, eng)
        else:
            p.dma(lambda e: e.dma_start(out=out, in_=in_), key, reads, writes, eng)

    def RECIP(out, in_, reads, writes):
        p.op("dve", lambda e: e.reciprocal(out=out, in_=in_), reads, writes)

    def REDUCE(out, in_, reads, writes):
        p.op("dve", lambda e: e.tensor_reduce(out=out, in_=in_, axis=AX.X, op=ALU.add), reads, writes)

    XR = lambda t: [("X", t, 0), ("X", t, 1)]

    for t in range(NT):
        DMA(X[:, t, :], x_d[t * 128:(t + 1) * 128, :], ("ldx", t), (), XR(t))
    DMA(identf[:], ident_d, "ldidf", (), ["identf"])
    DMA(identb[:], ident_d, "ldidb", (), ["identb"], eng="pool")

    gslot = [0]

    def load_gain(name):
        s = 0
        DMA(gbc[s][:], W[name].partition_broadcast(128), ("ldg", s), (), [("gbc", s)], eng="act")
        return s

    def rms_stats(t):
        ACT(junk[:], X[:, t, :], AF.Square, XR(t), ["junk", ("ss", t)], scale=1.0 / 32.0, accum_out=ss[:, t:t + 1])
        ACT(rstd[:, t:t + 1], ss[:, t:t + 1], AF.Sqrt, [("ss", t)], [("rstd", t)], bias=EPS)

    def rms_recip(t):
        p.op("dve", (lambda t: lambda e: e.reciprocal(out=rstd[:, t:t + 1], in_=rstd[:, t:t + 1]))(t), [("rstd", t)], [("rstd", t)])

    norm_state = {"ready": False, "out_done": False}

    def _stepper(apply_a, apply_b):
        def apply(t):
            apply_a(t)
            apply_b(t)

        def pre(t):
            if t >= 2:
                apply_a(t - 2)

        def mid(t):
            if t >= 2:
                apply_b(t - 2)

        def post(t):
            rms_stats(t)
            if t == NT - 1:
                apply(t - 1)
                apply(t)

        def step(t):
            pre(t)
            mid(t)
            post(t)
        step.stats = rms_stats
        step.apply = apply
        step.pre = pre
        step.mid = mid
        step.post = post
        return step

    def make_norm_HT(gain_name):
        def factory(st, tbanks=(6, 7), act_only=False):
            gs = load_gain(gain_name)
            MEMSET("dve", ss[:], 0.0, [("ss", t) for t in range(NT)])

            def apply_a(t):
                s = t % 2
                rms_recip(t)
                STT("dve", xn[s][:], X[:, t, :], rstd[:, t:t + 1], gbc[gs][:], ALU.mult, ALU.mult,
                    XR(t) + [("rstd", t), ("gbc", gs)], [("xn", s)])

            def apply_b(t):
                s = t % 2
                tbk = tbanks[t % len(tbanks)]
                pst = PS[tbk].bitcast(BF16)
                for c in range(8):
                    TR(pst[:, c * 128:(c + 1) * 128], xn[s][:, c * 128:(c + 1) * 128], identb[:],
                       [("xn", s), "identb"], [("ps", tbk)], signal=(c == 7))
                src = pst[:, :].rearrange("p (c n) -> p c n", c=8)
                if t % 2 == 0 or len(tbanks) == 1 or act_only:
                    ACT(HT[:, :, t * 128:(t + 1) * 128], src, AF.Copy, [("ps", tbk)], [("HT", t)])
                else:
                    CP("dve", HT[:, :, t * 128:(t + 1) * 128], src, [("ps", tbk)], [("HT", t)])
                if t == NT - 1:
                    norm_state["ready"] = True
            return _stepper(apply_a, apply_b)
        return factory

    def make_norm_out():
        def factory(st, tbanks=None, act_only=False):
            gs = load_gain("final_norm")
            MEMSET("dve", ss[:], 0.0, [("ss", t) for t in range(NT)])
            ot = [sbt(st, "ot%d" % i, [128, D], F32) for i in range(2)]

            def apply_a(t):
                s = t % 2
                rms_recip(t)
                STT("dve", ot[s][:], X[:, t, :], rstd[:, t:t + 1], gbc[gs][:], ALU.mult, ALU.mult,
                    XR(t) + [("rstd", t), ("gbc", gs)], [("ot", s)])

            def apply_b(t):
                s = t % 2
                DMA(out_d[t * 128:(t + 1) * 128, :], ot[s][:], ("st", s), [("ot", s)], ())
                if t == NT - 1:
                    norm_state["out_done"] = True
            return _stepper(apply_a, apply_b)
        return factory

    def ensure_norm(gain_name, st):
        if not norm_state["ready"]:
            step = make_norm_HT(gain_name)(st)
            for t in range(NT):
                step(t)
        norm_state["ready"] = False

    def ffn(pref, tail_factory):
        with ExitStack() as st:
            actT = sbt(st, "actT", [128, 11, S], BF16)
            wdt = sbt(st, "wdt", [128, 11, D], BF16)
            NSL = 4
            wgt = [sbt(st, "wgt%d" % i, [128, 8, 128], BF16) for i in range(NSL)]
            wut = [sbt(st, "wut%d" % i, [128, 8, 128], BF16) for i in range(NSL)]
            sg = [sbt(st, "sg%d" % i, [128, 512], F32) for i in range(2)]
            wgv = W[pref + "_w_gate"].rearrange("(kc p) n -> p kc n", p=128)
            wuv = W[pref + "_w_up"].rearrange("(kc p) n -> p kc n", p=128)
            wdd = W[pref + "_w_down"]

            def issue_gu(fc):
                s = fc % NSL
                DMA(wgt[s][:], wgv[:, :, fc * 128:(fc + 1) * 128], ("wg", s), (), [("wg", s)], eng="pool")
                DMA(wut[s][:], wuv[:, :, fc * 128:(fc + 1) * 128], ("wu", s), (), [("wu", s)], eng="pool")

            for fc in range(NSL - 1):
                issue_gu(fc)
            ensure_norm(pref + "_norm", st)
            step = 0
            for hf in range(2):
                for fcl in range(11):
                    fc = hf * 11 + fcl
                    if fc + NSL - 1 < NFC:
                        issue_gu(fc + NSL - 1)
                    DMA(wdt[:, fcl, :], wdd[fc * 128:(fc + 1) * 128, :], ("wd", fcl), (), [("wd", fcl)], eng="pool")
                    s = fc % NSL
                    for tb in range(4):
                        b = step % 2
                        step += 1
                        g, u = PS[b], PS[2 + b]
                        htr = [("HT", 4 * tb + i) for i in range(4)]
                        for kc in range(8):
                            MM(g[:], wgt[s][:, kc, :], HT[:, kc, tb * 512:(tb + 1) * 512], kc == 0, kc == 7,
                               [("wg", s)] + htr, [("ps", b)], kc == 7)
                        for kc in range(8):
                            MM(u[:], wut[s][:, kc, :], HT[:, kc, tb * 512:(tb + 1) * 512], kc == 0, kc == 7,
                               [("wu", s)] + htr, [("ps", 2 + b)], kc == 7)
                        ACT(sg[b][:], g[:], AF.Silu, [("ps", b)], [("sg", b)])
                        TT("dve", actT[:, fcl, tb * 512:(tb + 1) * 512], sg[b][:], u[:], ALU.mult,
                           [("sg", b), ("ps", 2 + b)], [("actT", fcl, tb)])
                dstep = 0
                tail = tail_factory(st) if hf == 1 else None
                for t in range(NT):
                    if tail is not None:
                        tail.pre(t)
                    for dh in range(2):
                        if tail is not None and dh == 1:
                            tail.mid(t)
                        b = 4 + dstep % 2
                        dstep += 1
                        for fcl in range(11):
                            MM(PS[b][:], actT[:, fcl, t * 128:(t + 1) * 128], wdt[:, fcl, dh * 512:(dh + 1) * 512],
                               fcl == 0, fcl == 10, [("actT", fcl, t // 4), ("wd", fcl)], [("ps", b)], fcl == 10)
                        STT("dve", X[:, t, dh * 512:(dh + 1) * 512], PS[b][:], 0.5, X[:, t, dh * 512:(dh + 1) * 512],
                            ALU.mult, ALU.add, [("ps", b), ("X", t, dh)], [("X", t, dh)])
                    if tail is not None:
                        tail.post(t)
        p.barrier()

    def mixer(tail_factory):
        import math
        wv_in = W["w_in"].rearrange("(kc p) n -> p kc n", p=128)
        with ExitStack() as stM:
            yaT = sbt(stM, "yaT", [128, 4, S], BF16)
            SC = sbt(stM, "SC", [128, 2, NT, 8], F32)
            rstdb = sbt(stM, "rstdb", [128, NT], F32)
            with ExitStack() as stA:
                wblk = [sbt(stA, "wblkA%d" % i, [128, 8, 512], BF16) for i in range(2)]
                DMA(wblk[0][:], wv_in[:, :, 512:1024], ("wb", 0), (), [("wblk", 0)], eng="pool")
                wsn = sbt(stA, "wsn", [128, 8, 128], F32)
                tri = sbt(stA, "tri", [128, 128], F32)
                wsT = sbt(stA, "wsT", [128, 8, 128], BF16)
                bsT = sbt(stA, "bsT", [128, 8], F32)
                lng = sbt(stA, "lng", [128, 512], F32)
                lnb = sbt(stA, "lnb", [128, 512], F32)
                gabc = sbt(stA, "gabc", [128, 512], F32)
                Gs = [sbt(stA, "Gs%d" % i, [128, 4, 512], F32) for i in range(2)]
                Ug = [sbt(stA, "Ug%d" % i, [128, 4, 512], F32) for i in range(2)]
                VLN = [sbt(stA, "VLN%d" % i, [128, 4, 512], BF16) for i in range(2)]
                sqA = sbt(stA, "sqA", [128, 512], F32)
                tmpA = [sbt(stA, "tmpA%d" % i, [128, 512], F32) for i in range(2)]
                yan = [sbt(stA, "yan%d" % i, [128, 512], BF16) for i in range(8)]
                s1 = [sbt(stA, "s1_%d" % i, [128, 32], F32) for i in range(2)]
                s2 = [sbt(stA, "s2_%d" % i, [128, 32], F32) for i in range(2)]
                mean = [sbt(stA, "mean%d" % i, [128, 32], F32) for i in range(2)]
                var = [sbt(stA, "var%d" % i, [128, 32], F32) for i in range(2)]
                rsg = [sbt(stA, "rsg%d" % i, [128, 32], F32) for i in range(2)]
                msa = sbt(stA, "msa", [128, NT], F32)
                rsa = sbt(stA, "rsa", [128, NT], F32)
                DMA(wsn[:], W["sgu_w_s"].rearrange("g i j -> i g j"), "ldws", (), ["wsn"])
                DMA(tri[:], tri_d, "ldtri", (), ["tri"])
                DMA(bsT[:], W["sgu_b_s"].rearrange("g i -> i g"), "ldbs", (), ["bsT"], slow=True)
                DMA(lng[:], W["sgu_ln_g"].partition_broadcast(128), "ldlng", (), ["lng"])
                DMA(lnb[:], W["sgu_ln_b"].partition_broadcast(128), "ldlnb", (), ["lnb"])
                DMA(gabc[:], W["out_norm_a"].partition_broadcast(128), "ldga", (), ["gabc"])
                ensure_norm("mix_norm", stA)
                for g in range(8):
                    b = 4 + g % 2
                    TR(PS[b][:, 0:128], wsn[:, g, :], identf[:], ["wsn", "identf"], [("ps", b)])
                    TT("dve", wsT[:, g, :], PS[b][:, 0:128], tri[:], ALU.mult, [("ps", b), "tri"], [("wsT", g)])
                MEMSET("dve", msa[:], 0.0, ["msa"])
                _stop("A0")
                def stage1a(G):
                    st_ = G % 2
                    for i in range(4):
                        t = 4 * G + i
                        b = t % 2
                        for kc in range(8):
                            MM(PS[b][:], HT[:, kc, t * 128:(t + 1) * 128], wblk[0][:, kc, :], kc == 0, kc == 7,
                               [("HT", t), ("wblk", 0)], [("ps", b)], kc == 7)
                        ACT(Gs[st_][:, i, :], PS[b][:], AF.Gelu, [("ps", b)], [("Gs", st_, i)])
                        REDUCE(s1[st_][:, i * 8:(i + 1) * 8], Gs[st_][:, i, :].rearrange("p (g c) -> p g c", g=8),
                               [("Gs", st_, i)], [("s1", st_, i)])
                        TT("dve", sqA[:], Gs[st_][:, i, :], Gs[st_][:, i, :], ALU.mult, [("Gs", st_, i)], ["sqA"])
                        REDUCE(s2[st_][:, i * 8:(i + 1) * 8], sqA[:].rearrange("p (g c) -> p g c", g=8),
                               ["sqA"], [("s2", st_, i)])
                    _stop("A1")
                    s1r = [("s1", st_, i) for i in range(4)]
                    s2r = [("s2", st_, i) for i in range(4)]
                    TS("dve", mean[st_][:], s1[st_][:], 1.0 / 64, None, ALU.mult, None, s1r, [("mean", st_)])
                    TT("dve", var[st_][:], mean[st_][:], mean[st_][:], ALU.mult, [("mean", st_)], [("var", st_)])
                    STT("dve", var[st_][:], s2[st_][:], 1.0 / 64, var[st_][:], ALU.mult, ALU.subtract,
                        s2r + [("var", st_)], [("var", st_)])
                    ACT(rsg[st_][:], var[st_][:], AF.Sqrt, [("var", st_)], [("rsg", st_)], bias=LN_EPS)

                def stage1b(G):
                    st_ = G % 2
                    RECIP(rsg[st_][:], rsg[st_][:], [("rsg", st_)], [("rsg", st_)])
                    _stop("A2")
                    for i in range(4):
                        ts_ = i % 2
                        g3 = Gs[st_][:, i, :].rearrange("p (g c) -> p g c", g=8)
                        t3 = tmpA[ts_][:].rearrange("p (g c) -> p g c", g=8)
                        TT("dve", t3, g3, mean[st_][:, i * 8:(i + 1) * 8].unsqueeze(2).to_broadcast([128, 8, 64]), ALU.subtract,
                           [("Gs", st_, i), ("mean", st_)], [("tmpA", ts_)])
                        TT("dve", t3, t3, rsg[st_][:, i * 8:(i + 1) * 8].unsqueeze(2).to_broadcast([128, 8, 64]), ALU.mult,
                           [("tmpA", ts_), ("rsg", st_)], [("tmpA", ts_)])
                        TT("pool", tmpA[ts_][:], tmpA[ts_][:], lng[:], ALU.mult, [("tmpA", ts_), "lng"], [("tmpA", ts_)])
                        TT("pool", VLN[st_][:, i, :], tmpA[ts_][:], lnb[:], ALU.add, [("tmpA", ts_), "lnb"], [("VLN", st_, i)])
                    _stop("A3")
                    for i in range(4):
                        t = 4 * G + i
                        b = t % 2
                        for kc in range(8):
                            MM(PS[b][:], HT[:, kc, t * 128:(t + 1) * 128], wblk[1][:, kc, :], kc == 0, kc == 7,
                               [("HT", t), ("wblk", 1)], [("ps", b)], kc == 7)
                        ACT(Ug[st_][:, i, :], PS[b][:], AF.Gelu, [("ps", b)], [("Ug", st_, i)])
                def stage2a(G):
                    st_ = G % 2
                    vr = [("VLN", st_, i) for i in range(4)]
                    ur = [("Ug", st_, i) for i in range(4)]
                    for g in range(8):
                        b = 2 + g % 2
                        MM(PS[b][:, 0:256], wsT[:, g, :], VLN[st_][:, :, g * 64:(g + 1) * 64], True, True,
                           [("wsT", g)] + vr, [("ps", b)], True)
                        STT("dve", Ug[st_][:, :, g * 64:(g + 1) * 64], PS[b][:, 0:256].rearrange("p (n c) -> p n c", n=4),
                            bsT[:, g:g + 1], Ug[st_][:, :, g * 64:(g + 1) * 64], ALU.add, ALU.mult,
                            [("ps", b), "bsT"] + ur, ur)
                    _stop("A5")
                    for i in range(4):
                        t = 4 * G + i
                        ACT(junk[:, 0:512], Ug[st_][:, i, :], AF.Square, [("Ug", st_, i)], ["junk", "msa"],
                            scale=1.0 / math.sqrt(512.0), accum_out=msa[:, t:t + 1])
                    ACT(rsa[:, 4 * G:4 * G + 4], msa[:, 4 * G:4 * G + 4], AF.Sqrt, ["msa"], [("rsa", G)], bias=EPS)

                def stage2b(G):
                    st_ = G % 2
                    RECIP(rsa[:, 4 * G:4 * G + 4], rsa[:, 4 * G:4 * G + 4], [("rsa", G)], [("rsa", G)])
                    _stop("A6")
                    for i in range(4):
                        t = 4 * G + i
                        s = t % 8
                        STT("dve", yan[s][:], Ug[st_][:, i, :], rsa[:, t:t + 1], gabc[:], ALU.mult, ALU.mult,
                            [("Ug", st_, i), ("rsa", G), "gabc"], [("yan", s)])

                def stage3(G):
                    for i in range(4):
                        t = 4 * G + i
                        s = t % 8
                        pst = PS[6 + t % 2].bitcast(BF16)
                        for c in range(4):
                            TR(pst[:, c * 128:(c + 1) * 128], yan[s][:, c * 128:(c + 1) * 128], identb[:],
                               [("yan", s), "identb"], [("ps", 6 + t % 2)], signal=(c == 3))
                        ACT(yaT[:, :, t * 128:(t + 1) * 128], pst[:, 0:512].rearrange("p (c n) -> p c n", c=4), AF.Copy,
                            [("ps", 6 + t % 2)], [("yaT", t)])

                stage1a(0)
                DMA(wblk[1][:], wv_in[:, :, 0:512], ("wb", 1), (), [("wblk", 1)], eng="pool")
                stage1a(1)
                stage1b(0)
                stage2a(0)
                stage1b(1)
                stage2b(0)
                stage1a(2)
                stage2a(1)
                stage3(0)
                stage1b(2)
                stage2b(1)
                stage1a(3)
                stage2a(2)
                stage3(1)
                stage1b(3)
                stage2b(2)
                stage2a(3)
                stage3(2)
                stage2b(3)
                stage3(3)
            p.barrier()
            _stop("A")
            with ExitStack() as stB:
                QT = sbt(stB, "QT", [128, 4, S], BF16)
                KT = sbt(stB, "KT", [128, 4, S], BF16)
                VA = sbt(stB, "VA", [128, NT, 8, 65], BF16)
                with ExitStack() as stB1:
                    wblk = [sbt(stB1, "wblkB%d" % i, [128, 8, 512], BF16) for i in range(2)]
                    DMA(wblk[0][:], wv_in[:, :, 1024:1536], ("wb", 0), (), [("wblk", 0)], eng="pool")
                    DMA(wblk[1][:], wv_in[:, :, 1536:2048], ("wb", 1), (), [("wblk", 1)], eng="pool")
                    posi = sbt(stB1, "posi", [128, NT], I32)
                    posf = sbt(stB1, "posf", [128, NT], F32)
                    invf = sbt(stB1, "invf", [128, 128], F32)
                    ANG = sbt(stB1, "ANG", [128, 2, 128], F32)
                    KF = sbt(stB1, "KF", [128, 256], F32)
                    KI = sbt(stB1, "KI", [128, 256], I32)
                    QR = [sbt(stB1, "QR%d" % i, [128, 512], BF16) for i in range(4)]
                    rt = [sbt(stB1, "rt%d" % i, [128, 8, 8], F32) for i in range(8)]
                    DMA(posi[:], pos_d, "ldpos", (), ["posi"])
                    DMA(invf[:], invf_d, "ldinvf", (), ["invf"])
                    MEMSET("pool", VA[:, :, :, 64:65], 1.0, ["VAones"])
                    CP("dve", posf[:], posi[:], ["posi"], ["posf"])
                    TT("dve", ANG[:, 0, :].rearrange("p (t i) -> p t i", i=8), invf[:].rearrange("p (t i) -> p t i", i=8),
                       posf[:].unsqueeze(2).to_broadcast([128, NT, 8]), ALU.mult, ["invf", "posf"], ["ANG"])
                    TS("dve", ANG[:, 1, :], ANG[:, 0, :], math.pi / 2, None, ALU.add, None, ["ANG"], ["ANG"])
                    A2 = ANG[:].rearrange("p a n -> p (a n)")
                    TS("dve", KF[:], A2, 1.0 / (2 * math.pi), None, ALU.mult, None, ["ANG"], ["KF"])
                    CP("dve", KI[:], KF[:], ["KF"], ["KI"])
                    CP("dve", KF[:], KI[:], ["KI"], ["KF"])
                    C1 = 6.28125
                    C2 = 2 * math.pi - C1
                    STT("dve", A2, KF[:], -C1, A2, ALU.mult, ALU.add, ["KF", "ANG"], ["ANG"])
                    STT("dve", A2, KF[:], -C2, A2, ALU.mult, ALU.add, ["KF", "ANG"], ["ANG"])
                    TS("dve", A2, A2, -math.pi, math.pi, ALU.max, ALU.min, ["ANG"], ["ANG"])
                    ACT(SC[:].rearrange("p a t i -> p (a t i)"), A2, AF.Sin, ["ANG"], ["SC"])

                    _stop("B1a")

                    def rotary(ps, bank, dst, dkey, t, k0):
                        ps3 = ps[:].rearrange("p (h e) -> p h e", h=8)
                        d3 = dst[:].rearrange("p (h e) -> p h e", h=8)
                        sinb = SC[:, 0, t:t + 1, :].to_broadcast([128, 8, 8])
                        cosb = SC[:, 1, t:t + 1, :].to_broadcast([128, 8, 8])
                        pr = [("ps", bank), "SC"]
                        CP("dve", d3[:, :, 16:64], ps3[:, :, 16:64], [("ps", bank)], [dkey])
                        TT("dve", rt[k0][:], ps3[:, :, 0:8], cosb, ALU.mult, pr, [("rt", k0)])
                        TT("dve", rt[k0 + 1][:], ps3[:, :, 8:16], sinb, ALU.mult, pr, [("rt", k0 + 1)])
                        TT("dve", d3[:, :, 0:8], rt[k0][:], rt[k0 + 1][:], ALU.subtract, [("rt", k0), ("rt", k0 + 1)], [dkey])
                        TT("dve", rt[k0 + 2][:], ps3[:, :, 8:16], cosb, ALU.mult, pr, [("rt", k0 + 2)])
                        TT("dve", rt[k0 + 3][:], ps3[:, :, 0:8], sinb, ALU.mult, pr, [("rt", k0 + 3)])
                        TT("dve", d3[:, :, 8:16], rt[k0 + 2][:], rt[k0 + 3][:], ALU.add, [("rt", k0 + 2), ("rt", k0 + 3)], [dkey])

                    def projqk(t):
                        for qk in range(2):
                            b = qk * 2 + t % 2
                            for kc in range(8):
                                MM(PS[b][:], HT[:, kc, t * 128:(t + 1) * 128], wblk[qk][:, kc, :], kc == 0, kc == 7,
                                   [("HT", t), ("wblk", qk)], [("ps", b)], kc == 7)
                            qs = qk * 2 + t % 2
                            rotary(PS[b], b, QR[qs], ("QR", qs), t, 4 * qk)

                    def trqk(t):
                        for qk in range(2):
                            qs = qk * 2 + t % 2
                            tb_ = 6 + qk
                            pst = PS[tb_].bitcast(BF16)
                            for c in range(4):
                                TR(pst[:, c * 128:(c + 1) * 128], QR[qs][:, c * 128:(c + 1) * 128], identb[:],
                                   [("QR", qs), "identb"], [("ps", tb_)], signal=(c == 3))
                            dstT = QT if qk == 0 else KT
                            ACT(dstT[:, :, t * 128:(t + 1) * 128], pst[:, 0:512].rearrange("p (c n) -> p c n", c=4),
                                AF.Copy, [("ps", tb_)], [("QKT", qk, t)])

                    projqk(0)
                    for t in range(NT):
                        if t + 1 < NT:
                            projqk(t + 1)
                        trqk(t)
                    _stop("B1c")
                    DMA(wblk[0][:], wv_in[:, :, 2048:2560], ("wb", 0), (), [("wblk", 0)], eng="pool")
                    for t in range(NT):
                        b = 4 + t % 2
                        for kc in range(8):
                            MM(PS[b][:], HT[:, kc, t * 128:(t + 1) * 128], wblk[0][:, kc, :], kc == 0, kc == 7,
                               [("HT", t), ("wblk", 0)], [("ps", b)], kc == 7)
                        ACT(VA[:, t, :, 0:64], PS[b][:].rearrange("p (h e) -> p h e", h=8), AF.Copy,
                            [("ps", b)], [("VA", t)])
                p.barrier()
                _stop("B1")
                with ExitStack() as stB2:
                    cm = sbt(stB2, "cm", [128, S], BF16)
                    NPB2 = 5
                    pb2 = [sbt(stB2, "pb%d" % i, [128, 2, 512], BF16) for i in range(NPB2)]
                    YB = [sbt(stB2, "YB%d" % i, [128, 4, 512], F32) for i in range(2)]
                    rzt = [sbt(stB2, "rzt%d" % i, [128, 4], F32) for i in range(4)]
                    ybn = [sbt(stB2, "ybn%d" % i, [128, 512], BF16) for i in range(2)]
                    gbbc = sbt(stB2, "gbbc", [128, 512], F32)
                    msb = sbt(stB2, "msb", [128, NT], F32)
                    YTb = HT[:, 0:4, :]
                    WO = HT[:, 4:8, :].rearrange("p a (b n) -> p (a b) n", n=1024)
                    DMA(cm[:], cmask_d, "ldcm", (), ["cm"], eng="pool")
                    DMA(gbbc[:], W["out_norm_b"].partition_broadcast(128), "ldgb", (), ["gbbc"])
                    DMA(WO, W["w_out"].rearrange("(kc p) n -> p kc n", p=128), "ldwo", (), ["WO"], eng="pool")
                    MEMSET("dve", msb[:], 0.0, ["msb"])
                    steps = []
                    for b in range(4):
                        for c in range(4):
                            nj = 4 * b + 4
                            for j in range(nj):
                                steps.append((b, c, j, nj))
                    LA = 4
                    SBK = [(0, 1), (6, 7)]
                    pending = []

                    def front(i):
                        b, c, j, nj = steps[i]
                        qlo = max(512 * b, 128 * j)
                        N = 512 * (b + 1) - qlo
                        for e in range(2):
                            r0 = e * 64
                            sb_ = SBK[i % 2][e]
                            MM(PS[sb_][:, 0:N], KT[r0:r0 + 64, c, j * 128:(j + 1) * 128], QT[r0:r0 + 64, c, qlo:qlo + N],
                               True, True, (), [("ps", sb_)], True)
                        pp = PSP[SBK[i % 2][0] // 2]
                        ps_ = i % NPB2
                        prs = [("ps", SBK[i % 2][0]), ("ps", SBK[i % 2][1])]
                        ACT(pb2[ps_][:, :, 0:N], pp[:, :, 0:N], AF.Exp, prs, [("pb", ps_)], scale=0.125)
                        TT("pool" if i % 3 == 0 else "dve", pb2[ps_][:, :, 0:N], pb2[ps_][:, :, 0:N],
                           cm[:, qlo - 128 * j:qlo - 128 * j + N].unsqueeze(1).to_broadcast([128, 2, N]), ALU.mult,
                           [("pb", ps_), "cm"], [("pb", ps_)])

                    def back(i):
                        b, c, j, nj = steps[i]
                        qlo = max(512 * b, 128 * j)
                        N = 512 * (b + 1) - qlo
                        nqb = N // 128
                        qb0 = 4 - nqb
                        g = b * 4 + c
                        for e in range(2):
                            h = 2 * c + e
                            ab = (2, 3)[e] if g % 2 == 0 else (4, 5)[e]
                            ps_ = i % NPB2
                            for qi in range(nqb):
                                qb = qb0 + qi
                                last = (qi == nqb - 1)
                                MM(PS[ab][:, qb * 65:(qb + 1) * 65], pb2[ps_][:, e, qi * 128:(qi + 1) * 128], VA[:, j, h, :],
                                   (j == 0 and qi == 0), (j == nj - 1), [("pb", ps_), "VAones"], [("ps", ab)],
                                   last and (j == nj - 1), sgc=True)
                        if j == nj - 1:
                            pending.append((i + 1, epi, b, c))

                    def epi(b, c):
                        g = b * 4 + c
                        ys = b % 2
                        for e in range(2):
                            h = 2 * c + e
                            ab = (2, 3)[e] if g % 2 == 0 else (4, 5)[e]
                            acc3 = PS[ab][:, 0:260].rearrange("p (q e) -> p q e", q=4)
                            rs = (2 * g + e) % 4
                            RECIP(rzt[rs][:].unsqueeze(2), acc3[:, :, 64:65], [("ps", ab)], [("rzt", rs)])
                            TT("dve", YB[ys][:, :, h * 64:(h + 1) * 64], acc3[:, :, 0:64],
                               rzt[rs][:].unsqueeze(2).to_broadcast([128, 4, 64]), ALU.mult,
                               [("ps", ab), ("rzt", rs)], [("YB", ys, h)])
                        if c == 3:
                            pending.append((0, bank_tail, b, 0))

                    def bank_tail(b, _):
                        ys = b % 2
                        ybr_ = [("YB", ys, h) for h in range(8)]
                        for i4 in range(4):
                            t = 4 * b + i4
                            s = t % 2
                            ACT(junk[:, 0:512], YB[ys][:, i4, :], AF.Square, ybr_, ["junk", "msb"],
                                scale=1.0 / math.sqrt(512.0), accum_out=msb[:, t:t + 1])
                            TT("pool", ybn[s][:], YB[ys][:, i4, :], gbbc[:], ALU.mult, ybr_ + ["gbbc"], [("ybn", s)])
                            tbk = 6 + s
                            pst = PS[tbk].bitcast(BF16)
                            for cc in range(4):
                                TR(pst[:, cc * 128:(cc + 1) * 128], ybn[s][:, cc * 128:(cc + 1) * 128], identb[:],
                                   [("ybn", s), "identb"], [("ps", tbk)], signal=(cc == 3))
                            ACT(YTb[:, :, t * 128:(t + 1) * 128], pst[:, 0:512].rearrange("p (c n) -> p c n", c=4),
                                AF.Copy, [("ps", tbk)], [("YTb", t)])

                    nst = len(steps)
                    for i in range(nst + LA + 4):
                        if i < nst:
                            front(i)
                        if 0 <= i - LA < nst:
                            back(i - LA)
                        due = [q for q in pending if q[0] <= i - LA]
                        for q in due:
                            pending.remove(q)
                            q[1](q[2], q[3])
                    while pending:
                        q = pending.pop(0)
                        q[1](q[2], q[3])
                    _stop("B2a")
                    ACT(rstdb[:], msb[:], AF.Sqrt, ["msb"], ["rstdb"], bias=EPS)
                    RECIP(rstdb[:], rstdb[:], ["rstdb"], ["rstdb"])
                    ybr = []
                    k = 0
                    tail = tail_factory(stB2, act_only=True)
                    for t in range(NT):
                        for dh in range(2):
                            ba = k % 2
                            bb = 2 + k % 2
                            k += 1
                            for c in range(4):
                                MM(PS[ba][:], yaT[:, c, t * 128:(t + 1) * 128], WO[:, c, dh * 512:(dh + 1) * 512],
                                   c == 0, c == 3, ["WO"], [("ps", ba)], c == 3)
                            for c in range(4):
                                MM(PS[bb][:], YTb[:, c, t * 128:(t + 1) * 128], WO[:, 4 + c, dh * 512:(dh + 1) * 512],
                                   c == 0, c == 3, ["WO", ("YTb", t)], [("ps", bb)], c == 3)
                            xs = X[:, t, dh * 512:(dh + 1) * 512]
                            TT("dve", xs, xs, PS[ba][:], ALU.add, [("ps", ba), ("X", t, dh)], [("X", t, dh)])
                            STT("dve", xs, PS[bb][:], rstdb[:, t:t + 1], xs, ALU.mult, ALU.add,
                                [("ps", bb), ("X", t, dh), "rstdb"], [("X", t, dh)])
                        tail.stats(t)
                    p.barrier()
                    for t in range(NT):
                        tail.apply(t)
        p.barrier()

    def cross(tail_factory):
        with ExitStack() as st:
            ensure_norm("cross_norm", st)
            memt = sbt(st, "memt", [128, 2, D], F32)
            MT = sbt(st, "MT", [128, 8, NMEM], BF16)
            KxT = sbt(st, "KxT", [128, 8, NMEM], BF16)
            Vx = sbt(st, "Vx", [128, 2, D], BF16)
            QxT = sbt(st, "QxT", [128, 8, S], BF16)
            wsl = [sbt(st, "wx%d" % i, [128, 8, 512], BF16) for i in range(3)]
            onesb = sbt(st, "onesb", [128, 128], BF16)
            pbx = [sbt(st, "pbx%d" % i, [128, 512], BF16) for i in range(4)]
            rzx = [sbt(st, "rzx%d" % i, [128, 512], F32) for i in range(2)]
            mss = sbt(st, "mss", [128, 2], F32)
            mrs = sbt(st, "mrs", [128, 2], F32)
            wplan = [("cross_wq", 0), ("cross_wk", 0), ("cross_wk", 1), ("cross_wv", 0), ("cross_wv", 1),
                     ("cross_wq", 1), ("cross_wo", 0), ("cross_wo", 1)]
            sl_of = {}

            wslot = [0, 1, 2, 1, 2, 1, 2, 0]

            def load_w(i):
                sl = wslot[i]
                nm, blk = wplan[i]
                src = W[nm].rearrange("(kc p) n -> p kc n", p=128)[:, :, blk * 512:(blk + 1) * 512]
                DMA(wsl[sl][:], src, ("wx", sl), (), [("wx", sl)], eng="pool")
                sl_of[i] = sl

            load_w(0)
            MEMSET("pool", onesb[:], 1.0, ["onesb"])
            kk = [0]

            def QP(hc, tb):
                sl = sl_of[0] if hc < 4 else sl_of[5]
                b = kk[0] % 2
                kk[0] += 1
                htr = [("HT", 4 * tb + i) for i in range(4)]
                for kc in range(8):
                    MM(PS[b][:], wsl[sl][:, kc, (hc % 4) * 128:(hc % 4 + 1) * 128], HT[:, kc, tb * 512:(tb + 1) * 512],
                       kc == 0, kc == 7, [("wx", sl)] + htr, [("ps", b)], kc == 7)
                if kk[0] % 2 == 0 or hc >= 2:
                    ACT(QxT[:, hc, tb * 512:(tb + 1) * 512], PS[b][:], AF.Copy, [("ps", b)], [("QxT", hc, tb)])
                else:
                    CP("dve", QxT[:, hc, tb * 512:(tb + 1) * 512], PS[b][:], [("ps", b)], [("QxT", hc, tb)])

            for hc in range(2):
                for tb in range(4):
                    QP(hc, tb)
                if hc == 0:
                    load_w(1)
                    load_w(2)
                    for mt in range(2):
                        DMA(memt[:, mt, :], mem_d[mt * 128:(mt + 1) * 128, :], ("ldm", mt), (), [("memt", mt)])
            gm = load_gain("mem_norm")
            MEMSET("dve", mss[:], 0.0, ["mss"])
            for mt in range(2):
                ACT(junk[:], memt[:, mt, :], AF.Square, [("memt", mt)], ["junk", "mss"], scale=1.0 / 32.0,
                    accum_out=mss[:, mt:mt + 1])
            ACT(mrs[:], mss[:], AF.Sqrt, ["mss"], ["mrs"], bias=EPS)
            RECIP(mrs[:], mrs[:], ["mrs"], ["mrs"])
            for mt in range(2):
                s = mt % 2
                STT("dve", xn[s][:], memt[:, mt, :], mrs[:, mt:mt + 1], gbc[gm][:], ALU.mult, ALU.mult,
                    [("memt", mt), "mrs", ("gbc", gm)], [("xn", s)])
                pst = PS[6 + s].bitcast(BF16)
                for c in range(8):
                    TR(pst[:, c * 128:(c + 1) * 128], xn[s][:, c * 128:(c + 1) * 128], identb[:],
                       [("xn", s), "identb"], [("ps", 6 + s)], signal=(c == 7))
                ACT(MT[:, :, mt * 128:(mt + 1) * 128], pst[:, :].rearrange("p (c n) -> p c n", c=8), AF.Copy,
                    [("ps", 6 + s)], [("MT", mt)])
            mtr = [("MT", 0), ("MT", 1)]
            for hc in range(8):
                sl = sl_of[1 + hc // 4]
                b = kk[0] % 2
                kk[0] += 1
                for kc in range(8):
                    MM(PS[b][:, 0:NMEM], wsl[sl][:, kc, (hc % 4) * 128:(hc % 4 + 1) * 128], MT[:, kc, :], kc == 0, kc == 7,
                       [("wx", sl)] + mtr, [("ps", b)], kc == 7)
                ACT(KxT[:, hc, :], PS[b][:, 0:NMEM], AF.Copy, [("ps", b)], [("KxT", hc)])
                if hc == 3:
                    load_w(3)
            load_w(4)
            for dh in range(2):
                sl = sl_of[3 + dh]
                for mt in range(2):
                    b = kk[0] % 2
                    kk[0] += 1
                    for kc in range(8):
                        MM(PS[b][:], MT[:, kc, mt * 128:(mt + 1) * 128], wsl[sl][:, kc, :], kc == 0, kc == 7,
                           [("wx", sl), ("MT", mt)], [("ps", b)], kc == 7)
                    ACT(Vx[:, mt, dh * 512:(dh + 1) * 512], PS[b][:], AF.Copy, [("ps", b)], [("Vx", mt, dh)])
                load_w(5 + dh)
            itc = [0]

            def ATT(h, tb, alt=False):
                par = itc[0] % 2
                itc[0] += 1
                sbk = (2, 3)
                ob = (4, 5)
                zb = 7
                if alt and par == 1:
                    ob = (0, 1)
                    zb = 6
                cols = slice(tb * 512, (tb + 1) * 512)
                for mt in range(2):
                    for ec in range(2):
                        MM(PS[sbk[mt]][:], KxT[:, 2 * h + ec, mt * 128:(mt + 1) * 128], QxT[:, 2 * h + ec, cols], ec == 0, ec == 1,
                           [("KxT", 2 * h + ec), ("QxT", 2 * h + ec, tb)], [("ps", sbk[mt])], ec == 1)
                    ACT(pbx[par * 2 + mt][:], PS[sbk[mt]][:], AF.Exp, [("ps", sbk[mt])], [("pbx", par * 2 + mt)], scale=1.0 / 16.0)
                for ec in range(2):
                    for mt in range(2):
                        MM(PS[ob[ec]][:], Vx[:, mt, (2 * h + ec) * 128:(2 * h + ec + 1) * 128], pbx[par * 2 + mt][:], mt == 0, mt == 1,
                           [("Vx", mt, (2 * h + ec) // 4), ("pbx", par * 2 + mt)], [("ps", ob[ec])], mt == 1)
                for mt in range(2):
                    MM(PS[zb][:], onesb[:], pbx[par * 2 + mt][:], mt == 0, mt == 1, ["onesb", ("pbx", par * 2 + mt)],
                       [("ps", zb)], mt == 1)
                RECIP(rzx[par][:], PS[zb][:], [("ps", zb)], [("rzx", par)])
                for ec in range(2):
                    TT("dve", QxT[:, 2 * h + ec, cols], PS[ob[ec]][:], rzx[par][:], ALU.mult,
                       [("ps", ob[ec]), ("rzx", par)], [("QxT", 2 * h + ec, tb)])

            for h in range(3):
                qpl = [(hc, tb) for hc in (2 * h + 2, 2 * h + 3) for tb in range(4)]
                for i, (hc, tb) in enumerate(qpl):
                    QP(hc, tb)
                    if i % 2 == 1:
                        ATT(h, i // 2)
                if h == 0:
                    load_w(7)
            for tb in range(4):
                ATT(3, tb, alt=True)
            tail = tail_factory(st)
            for t in range(NT):
                tail.pre(t)
                for dh in range(2):
                    if dh == 1:
                        tail.mid(t)
                    b = kk[0] % 2
                    kk[0] += 1
                    sl = sl_of[6 + dh]
                    for hc in range(8):
                        MM(PS[b][:], QxT[:, hc, t * 128:(t + 1) * 128], wsl[sl][:, hc, :], hc == 0, hc == 7,
                           [("QxT", hc, t // 4), ("wx", sl)], [("ps", b)], hc == 7)
                    xs = X[:, t, dh * 512:(dh + 1) * 512]
                    TT("dve", xs, xs, PS[b][:], ALU.add, [("ps", b), ("X", t, dh)], [("X", t, dh)])
                tail.post(t)
        p.barrier()

    seq = [q for q in ("ffn1", "mix", "cross", "ffn2") if q in stages]
    gains = {"ffn1": "ffn1_norm", "mix": "mix_norm", "cross": "cross_norm", "ffn2": "ffn2_norm"}
    for i, sname in enumerate(seq):
        nxt = seq[i + 1] if i + 1 < len(seq) else None
        tf = make_norm_HT(gains[nxt]) if nxt else make_norm_out()
        if sname in ("ffn1", "ffn2"):
            ffn(sname, tf)
        elif sname == "mix":
            mixer(tf)
            p.muted = False
            p.barrier()
        else:
            cross(tf)
    if not norm_state["out_done"]:
        with ExitStack() as st:
            step = make_norm_out()(st)
            for t in range(NT):
                step(t)
    p.emit()
    top.close()
    return nc


_CACHE = {}


def _consts():
    ident = np.eye(128, dtype=np.float32)
    j = np.arange(128)[:, None]
    i = np.arange(128)[None, :]
    tri = (j <= i).astype(np.float32)
    d = np.arange(S)[None, :] - np.arange(128)[:, None]
    cm = ((d >= 0) & (d <= 128)).astype(np.float32) + ((d >= 0) & (d % 4 == 0) & (d <= 512)).astype(np.float32) \
        + ((d >= 0) & (d % 16 == 0)).astype(np.float32)
    ex = (-2.0 * np.arange(8, dtype=np.float32) / np.float32(16.0)).astype(np.float32)
    invf = np.power(np.float32(500000.0), ex).astype(np.float32)
    invf_t = np.tile(invf[None, :], (128, 16)).astype(np.float32)
    return {"ident": ident, "tri": tri, "cmask": cm.astype(np.float32), "invf": invf_t}


def kernel(**inputs):
    n = 8
    if "nc" not in _CACHE:
        _CACHE["nc"] = build_nc()
    nc = _CACHE["nc"]
    cst = _consts()
    shared = {}
    for k, v in inputs.items():
        if k in ("x", "mem", "positions"):
            continue
        a = np.asarray(v)
        if k == "final_norm":
            a = a.reshape(1, D)
        elif k in ("sgu_w_s", "sgu_b_s"):
            a = a[0]
        elif a.ndim == 2:
            pass
        else:
            a = a[0]
        shared[k] = np.ascontiguousarray(a, dtype=np.float32)
    x = np.asarray(inputs["x"], dtype=np.float32)
    mem = np.asarray(inputs["mem"], dtype=np.float32)
    pos = np.asarray(inputs["positions"], dtype=np.int32)
    in_maps = []
    for b in range(n):
        m = dict(shared)
        m.update(cst)
        m["x"] = np.ascontiguousarray(x[b])
        m["mem"] = np.ascontiguousarray(mem[b])
        m["pos"] = np.ascontiguousarray(pos[b].reshape(NT, 128).T)
        in_maps.append(m)
    res = run_bass_kernel_spmd(nc, in_maps, core_ids=list(range(n)))
    return np.stack([np.asarray(r["out"], dtype=np.float32) for r in res.results], axis=0)
```

```python
import numpy as np
from contextlib import ExitStack
import concourse.bass as bass
import concourse.mybir as mybir
from concourse.bass_utils import run_bass_kernel_spmd

F32 = mybir.dt.float32
BF16 = mybir.dt.bfloat16
I32 = mybir.dt.int32
AF = mybir.ActivationFunctionType
ALU = mybir.AluOpType
AX = mybir.AxisListType

ENGS = ("pe", "act", "dve", "pool", "sp")

S = 2048
D = 1024
NT = 16
DFF = 2816
NFC = 22
NMEM = 256
EPS = 1e-6
LN_EPS = 1e-5
import os as _os
MIXSTOP = _os.environ.get("MIXSTOP", "")
SAFE_SAME_ENGINE = True


class _Stop(Exception):
    pass


_PROG = [None]


def _stop(tag):
    if MIXSTOP == tag:
        _PROG[0].muted = True


class _Op:
    __slots__ = ("eng", "fn", "deps", "signal", "dma_key", "dma_cnt", "idx", "sigcnt")


class Prog:
    def __init__(self, nc):
        self.nc = nc
        self.ops = {e: [] for e in ENGS}
        self.last_w = {}
        self.readers = {}
        self.dma_cnt = {}
        self.bar_tokens = []
        self.bar_gen = 0
        self.eng_gen = {e: 0 for e in ENGS}
        self.muted = False
        _PROG[0] = self

    def _add(self, eng, fn, reads, writes, signal, dma_key):
        if self.muted:
            return None
        op = _Op()
        op.eng = eng
        op.fn = fn
        op.signal = signal
        op.dma_key = dma_key
        op.idx = len(self.ops[eng])
        deps = []
        if self.eng_gen[eng] < self.bar_gen:
            self.eng_gen[eng] = self.bar_gen
            for tok in self.bar_tokens:
                deps.append((tok, True))
        for r in reads:
            w = self.last_w.get(r)
            if w is not None:
                deps.append((w, True))
        for r in writes:
            w = self.last_w.get(r)
            if w is not None:
                deps.append((w, False))
            for rd in self.readers.get(r, ()):
                deps.append((rd, False))
        op.deps = deps
        if dma_key is not None:
            c = self.dma_cnt.get(dma_key, 0) + 16
            self.dma_cnt[dma_key] = c
            op.dma_cnt = c
            tok = ("d", dma_key, c)
        else:
            tok = ("c", eng, op.idx)
        for r in writes:
            self.last_w[r] = tok
            self.readers[r] = []
        for r in reads:
            self.readers.setdefault(r, []).append(tok)
        self.ops[eng].append(op)
        return op

    def op(self, eng, fn, reads=(), writes=(), signal=True):
        return self._add(eng, fn, tuple(reads), tuple(writes), signal, None)

    def dma(self, fn, key, reads=(), writes=(), eng="sp"):
        return self._add(eng, fn, tuple(reads), tuple(writes), False, key)

    def barrier(self):
        if self.muted:
            return
        toks = []
        for e in ENGS:
            lst = self.ops[e]
            for i in range(len(lst) - 1, -1, -1):
                if lst[i].dma_key is None:
                    lst[i].signal = True
                    toks.append(("c", e, i))
                    break
        for k, c in self.dma_cnt.items():
            toks.append(("d", k, c))
        self.bar_tokens = toks
        self.bar_gen += 1
        self.last_w = {}
        self.readers = {}

    def emit(self):
        nc = self.nc
        sig_at = {}
        for e in ENGS:
            cnt = 0
            lst = self.ops[e]
            for op in lst:
                if op.dma_key is None and op.signal:
                    cnt += 1
                    op.sigcnt = cnt
            need = [None] * len(lst)
            nxt = None
            for i in range(len(lst) - 1, -1, -1):
                op = lst[i]
                if op.dma_key is None and op.signal:
                    nxt = op.sigcnt
                need[i] = nxt
            sig_at[e] = need
        es = ExitStack()
        sems = {}
        for e in ENGS:
            sems[e] = es.enter_context(nc.semaphore("s_" + e))
        dsems = {}
        for i, k in enumerate(self.dma_cnt):
            dsems[k] = es.enter_context(nc.semaphore("d%d" % i))
        block = es.enter_context(nc.Block())

        def run_engine(e, eo):
            waited = {}
            for op in self.ops[e]:
                wl = {}
                for (tok, raw) in op.deps:
                    if tok[0] == "c":
                        pe_, idx = tok[1], tok[2]
                        if pe_ == e:
                            if e == "pe" or e == "sp" or ((not raw) and not SAFE_SAME_ENGINE):
                                continue
                        val = sig_at[pe_][idx]
                        if val is None:
                            raise RuntimeError("dep on unsignaled tail op %s %d" % (pe_, idx))
                        s = sems[pe_]
                        k = ("c", pe_)
                    else:
                        s = dsems[tok[1]]
                        val = tok[2]
                        k = ("d", tok[1])
                    if waited.get(k, 0) >= val:
                        continue
                    if wl.get(k, (None, 0))[1] < val:
                        wl[k] = (s, val)
                for k, (s, val) in wl.items():
                    eo.wait_ge(s, val)
                    waited[k] = val
                ins = op.fn(eo)
                if op.dma_key is not None:
                    ins.then_inc(dsems[op.dma_key], 16)
                elif op.signal:
                    ins.then_inc(sems[e], 1)

        @block.tensor
        def _(eo):
            run_engine("pe", eo)

        @block.scalar
        def _(eo):
            run_engine("act", eo)

        @block.vector
        def _(eo):
            run_engine("dve", eo)

        @block.gpsimd
        def _(eo):
            run_engine("pool", eo)

        @block.sync
        def _(eo):
            run_engine("sp", eo)
            for k, c in self.dma_cnt.items():
                eo.wait_ge(dsems[k], c)

        es.close()


def build_nc(stages=("ffn1", "mix", "cross", "ffn2"), dbg=False):
    nc = bass.Bass("TRN2", target_bir_lowering=False)
    dram_in = lambda name, shape, dt=F32: nc.dram_tensor(name, shape, dt, kind="ExternalInput").ap()
    x_d = dram_in("x", [S, D])
    mem_d = dram_in("mem", [NMEM, D])
    pos_d = dram_in("pos", [128, NT], I32)
    ident_d = dram_in("ident", [128, 128])
    tri_d = dram_in("tri", [128, 128])
    cmask_d = dram_in("cmask", [128, S])
    invf_d = dram_in("invf", [128, 128])
    W = {}
    for nm, sh in [("ffn1_norm", [1, D]), ("ffn1_w_gate", [D, DFF]), ("ffn1_w_up", [D, DFF]), ("ffn1_w_down", [DFF, D]),
                   ("mix_norm", [1, D]), ("w_in", [D, 2560]), ("sgu_ln_g", [1, 512]), ("sgu_ln_b", [1, 512]),
                   ("sgu_w_s", [8, 128, 128]), ("sgu_b_s", [8, 128]), ("out_norm_a", [1, 512]), ("out_norm_b", [1, 512]),
                   ("w_out", [D, D]), ("cross_norm", [1, D]), ("mem_norm", [1, D]), ("cross_wq", [D, D]),
                   ("cross_wk", [D, D]), ("cross_wv", [D, D]), ("cross_wo", [D, D]), ("ffn2_norm", [1, D]),
                   ("ffn2_w_gate", [D, DFF]), ("ffn2_w_up", [D, DFF]), ("ffn2_w_down", [DFF, D]), ("final_norm", [1, D])]:
        W[nm] = dram_in(nm, sh)
    out_d = nc.dram_tensor("out", [S, D], F32, kind="ExternalOutput").ap()

    top = ExitStack()
    _uid = [0]

    def sbt(st, name, shape, dt):
        _uid[0] += 1
        return st.enter_context(nc.sbuf_tensor("s%d_%s" % (_uid[0], name), shape, dt))
    p = Prog(nc)

    X = sbt(top, "X", [128, NT, D], F32)
    HT = sbt(top, "HT", [128, 8, S], BF16)
    identb = sbt(top, "identb", [128, 128], BF16)
    identf = sbt(top, "identf", [128, 128], F32)
    ss = sbt(top, "ss", [128, NT], F32)
    rstd = sbt(top, "rstd", [128, NT], F32)
    gbc = [sbt(top, "gbc%d" % i, [128, D], F32) for i in range(1)]
    xn = [sbt(top, "xn%d" % i, [128, D], BF16) for i in range(2)]
    junk = sbt(top, "junk", [128, D], BF16)
    PSP = [top.enter_context(nc.psum_tensor("psp%d" % i, [128, 2, 512], F32)) for i in range(4)]
    PS = [PSP[i // 2][:, i % 2, :] for i in range(8)]

    def MM(out, lhsT, rhs, start, stop, reads, writes, signal, sgc=False):
        if sgc:
            p.op("pe", lambda e: e.matmul(out, lhsT=lhsT, rhs=rhs, start=start, stop=stop, skip_group_check=True), reads, writes, signal)
        else:
            p.op("pe", lambda e: e.matmul(out, lhsT=lhsT, rhs=rhs, start=start, stop=stop), reads, writes, signal)

    def TR(out, in_, ident, reads, writes, signal=True):
        p.op("pe", lambda e: e.transpose(out=out, in_=in_, identity=ident), reads, writes, signal)

    def ACT(out, in_, func, reads, writes, bias=None, scale=None, accum_out=None):
        kw = {}
        if bias is not None:
            kw["bias"] = bias
        if scale is not None:
            kw["scale"] = scale
        if accum_out is not None:
            kw["accum_out"] = accum_out
        p.op("act", lambda e: e.activation(out=out, in_=in_, func=func, **kw), reads, writes)

    def TT(eng, out, in0, in1, op, reads, writes):
        p.op(eng, lambda e: e.tensor_tensor(out=out, in0=in0, in1=in1, op=op), reads, writes)

    def TS(eng, out, in0, s1, s2, op0, op1, reads, writes):
        if op1 is None:
            p.op(eng, lambda e: e.tensor_scalar(out=out, in0=in0, scalar1=s1, scalar2=None, op0=op0), reads, writes)
        else:
            p.op(eng, lambda e: e.tensor_scalar(out=out, in0=in0, scalar1=s1, scalar2=s2, op0=op0, op1=op1), reads, writes)

    def STT(eng, out, in0, scalar, in1, op0, op1, reads, writes):
        p.op(eng, lambda e: e.scalar_tensor_tensor(out=out, in0=in0, scalar=scalar, in1=in1, op0=op0, op1=op1), reads, writes)

    def CP(eng, out, in_, reads, writes):
        p.op(eng, lambda e: e.tensor_copy(out=out, in_=in_), reads, writes)

    def MEMSET(eng, ap, val, writes):
        p.op(eng, lambda e: e.memset(ap, val), (), writes)

    def DMA(out, in_, key, reads, writes, eng="sp", slow=False):
        if slow:
            p.dma(lambda e: e.dma_start(out=out, in_=in_, allow_slow_non_contiguous=True), key, reads, writes, eng)
        else:
            p.dma(lambda e: e.dma_start(out=out, in_=in_), key, reads, writes, eng)

    def RECIP(out, in_, reads, writes):
        p.op("dve", lambda e: e.reciprocal(out=out, in_=in_), reads, writes)

    def REDUCE(out, in_, reads, writes):
        p.op("dve", lambda e: e.tensor_reduce(out=out, in_=in_, axis=AX.X, op=ALU.add), reads, writes)

    XR = lambda t: [("X", t, 0), ("X", t, 1)]

    for t in range(NT):
        DMA(X[:, t, :], x_d[t * 128:(t + 1) * 128, :], ("ldx", t), (), XR(t))
    DMA(identf[:], ident_d, "ldidf", (), ["identf"])
    DMA(identb[:], ident_d, "ldidb", (), ["identb"], eng="pool")

    gslot = [0]

    def load_gain(name):
        s = 0
        DMA(gbc[s][:], W[name].partition_broadcast(128), ("ldg", s), (), [("gbc", s)], eng="act")
        return s

    def rms_stats(t):
        ACT(junk[:], X[:, t, :], AF.Square, XR(t), ["junk", ("ss", t)], scale=1.0 / 32.0, accum_out=ss[:, t:t + 1])
        ACT(rstd[:, t:t + 1], ss[:, t:t + 1], AF.Sqrt, [("ss", t)], [("rstd", t)], bias=EPS)

    def rms_recip(t):
        p.op("dve", (lambda t: lambda e: e.reciprocal(out=rstd[:, t:t + 1], in_=rstd[:, t:t + 1]))(t), [("rstd", t)], [("rstd", t)])

    norm_state = {"ready": False, "out_done": False}

    def _stepper(apply_a, apply_b):
        def apply(t):
            apply_a(t)
            apply_b(t)

        def pre(t):
            if t >= 2:
                apply_a(t - 2)

        def mid(t):
            if t >= 2:
                apply_b(t - 2)

        def post(t):
            rms_stats(t)
            if t == NT - 1:
                apply(t - 1)
                apply(t)

        def step(t):
            pre(t)
            mid(t)
            post(t)
        step.stats = rms_stats
        step.apply = apply
        step.pre = pre
        step.mid = mid
        step.post = post
        return step

    def make_norm_HT(gain_name):
        def factory(st, tbanks=(6, 7), act_only=False):
            gs = load_gain(gain_name)
            MEMSET("dve", ss[:], 0.0, [("ss", t) for t in range(NT)])

            def apply_a(t):
                s = t % 2
                rms_recip(t)
                STT("dve", xn[s][:], X[:, t, :], rstd[:, t:t + 1], gbc[gs][:], ALU.mult, ALU.mult,
                    XR(t) + [("rstd", t), ("gbc", gs)], [("xn", s)])

            def apply_b(t):
                s = t % 2
                tbk = tbanks[t % len(tbanks)]
                pst = PS[tbk].bitcast(BF16)
                for c in range(8):
                    TR(pst[:, c * 128:(c + 1) * 128], xn[s][:, c * 128:(c + 1) * 128], identb[:],
                       [("xn", s), "identb"], [("ps", tbk)], signal=(c == 7))
                src = pst[:, :].rearrange("p (c n) -> p c n", c=8)
                if t % 2 == 0 or len(tbanks) == 1 or act_only:
                    ACT(HT[:, :, t * 128:(t + 1) * 128], src, AF.Copy, [("ps", tbk)], [("HT", t)])
                else:
                    CP("dve", HT[:, :, t * 128:(t + 1) * 128], src, [("ps", tbk)], [("HT", t)])
                if t == NT - 1:
                    norm_state["ready"] = True
            return _stepper(apply_a, apply_b)
        return factory

    def make_norm_out():
        def factory(st, tbanks=None, act_only=False):
            gs = load_gain("final_norm")
            MEMSET("dve", ss[:], 0.0, [("ss", t) for t in range(NT)])
            ot = [sbt(st, "ot%d" % i, [128, D], F32) for i in range(2)]

            def apply_a(t):
                s = t % 2
                rms_recip(t)
                STT("dve", ot[s][:], X[:, t, :], rstd[:, t:t + 1], gbc[gs][:], ALU.mult, ALU.mult,
                    XR(t) + [("rstd", t), ("gbc", gs)], [("ot", s)])

            def apply_b(t):
                s = t % 2
                DMA(out_d[t * 128:(t + 1) * 128, :], ot[s][:], ("st", s), [("ot", s)], ())
                if t == NT - 1:
                    norm_state["out_done"] = True
            return _stepper(apply_a, apply_b)
        return factory

    def ensure_norm(gain_name, st):
        if not norm_state["ready"]:
            step = make_norm_HT(gain_name)(st)
            for t in range(NT):
                step(t)
        norm_state["ready"] = False

    def ffn(pref, tail_factory):
        with ExitStack() as st:
            actT = sbt(st, "actT", [128, 11, S], BF16)
            wdt = sbt(st, "wdt", [128, 11, D], BF16)
            NSL = 4
            wgt = [sbt(st, "wgt%d" % i, [128, 8, 128], BF16) for i in range(NSL)]
            wut = [sbt(st, "wut%d" % i, [128, 8, 128], BF16) for i in range(NSL)]
            sg = [sbt(st, "sg%d" % i, [128, 512], F32) for i in range(2)]
            wgv = W[pref + "_w_gate"].rearrange("(kc p) n -> p kc n", p=128)
            wuv = W[pref + "_w_up"].rearrange("(kc p) n -> p kc n", p=128)
            wdd = W[pref + "_w_down"]

            def issue_gu(fc):
                s = fc % NSL
                DMA(wgt[s][:], wgv[:, :, fc * 128:(fc + 1) * 128], ("wg", s), (), [("wg", s)], eng="pool")
                DMA(wut[s][:], wuv[:, :, fc * 128:(fc + 1) * 128], ("wu", s), (), [("wu", s)], eng="pool")

            for fc in range(NSL - 1):
                issue_gu(fc)
            ensure_norm(pref + "_norm", st)
            step = 0
            for hf in range(2):
                for fcl in range(11):
                    fc = hf * 11 + fcl
                    if fc + NSL - 1 < NFC:
                        issue_gu(fc + NSL - 1)
                    DMA(wdt[:, fcl, :], wdd[fc * 128:(fc + 1) * 128, :], ("wd", fcl), (), [("wd", fcl)], eng="pool")
                    s = fc % NSL
                    for tb in range(4):
                        b = step % 2
                        step += 1
                        g, u = PS[b], PS[2 + b]
                        htr = [("HT", 4 * tb + i) for i in range(4)]
                        for kc in range(8):
                            MM(g[:], wgt[s][:, kc, :], HT[:, kc, tb * 512:(tb + 1) * 512], kc == 0, kc == 7,
                               [("wg", s)] + htr, [("ps", b)], kc == 7)
                        for kc in range(8):
                            MM(u[:], wut[s][:, kc, :], HT[:, kc, tb * 512:(tb + 1) * 512], kc == 0, kc == 7,
                               [("wu", s)] + htr, [("ps", 2 + b)], kc == 7)
                        ACT(sg[b][:], g[:], AF.Silu, [("ps", b)], [("sg", b)])
                        TT("dve", actT[:, fcl, tb * 512:(tb + 1) * 512], sg[b][:], u[:], ALU.mult,
                           [("sg", b), ("ps", 2 + b)], [("actT", fcl, tb)])
                dstep = 0
                tail = tail_factory(st) if hf == 1 else None
                for t in range(NT):
                    if tail is not None:
                        tail.pre(t)
                    for dh in range(2):
                        if tail is not None and dh == 1:
                            tail.mid(t)
                        b = 4 + dstep % 2
                        dstep += 1
                        for fcl in range(11):
                            MM(PS[b][:], actT[:, fcl, t * 128:(t + 1) * 128], wdt[:, fcl, dh * 512:(dh + 1) * 512],
                               fcl == 0, fcl == 10, [("actT", fcl, t // 4), ("wd", fcl)], [("ps", b)], fcl == 10)
                        STT("dve", X[:, t, dh * 512:(dh + 1) * 512], PS[b][:], 0.5, X[:, t, dh * 512:(dh + 1) * 512],
                            ALU.mult, ALU.add, [("ps", b), ("X", t, dh)], [("X", t, dh)])
                    if tail is not None:
                        tail.post(t)
        p.barrier()

    def mixer(tail_factory):
        import math
        wv_in = W["w_in"].rearrange("(kc p) n -> p kc n", p=128)
        with ExitStack() as stM:
            yaT = sbt(stM, "yaT", [128, 4, S], BF16)
            SC = sbt(stM, "SC", [128, 2, NT, 8], F32)
            rstdb = sbt(stM, "rstdb", [128, NT], F32)
            with ExitStack() as stA:
                wblk = [sbt(stA, "wblkA%d" % i, [128, 8, 512], BF16) for i in range(2)]
                DMA(wblk[0][:], wv_in[:, :, 512:1024], ("wb", 0), (), [("wblk", 0)], eng="pool")
                wsn = sbt(stA, "wsn", [128, 8, 128], F32)
                tri = sbt(stA, "tri", [128, 128], F32)
                wsT = sbt(stA, "wsT", [128, 8, 128], BF16)
                bsT = sbt(stA, "bsT", [128, 8], F32)
                lng = sbt(stA, "lng", [128, 512], F32)
                lnb = sbt(stA, "lnb", [128, 512], F32)
                gabc = sbt(stA, "gabc", [128, 512], F32)
                Gs = [sbt(stA, "Gs%d" % i, [128, 4, 512], F32) for i in range(2)]
                Ug = [sbt(stA, "Ug%d" % i, [128, 4, 512], F32) for i in range(2)]
                VLN = [sbt(stA, "VLN%d" % i, [128, 4, 512], BF16) for i in range(2)]
                sqA = sbt(stA, "sqA", [128, 512], F32)
                tmpA = [sbt(stA, "tmpA%d" % i, [128, 512], F32) for i in range(2)]
                yan = [sbt(stA, "yan%d" % i, [128, 512], BF16) for i in range(8)]
                s1 = [sbt(stA, "s1_%d" % i, [128, 32], F32) for i in range(2)]
                s2 = [sbt(stA, "s2_%d" % i, [128, 32], F32) for i in range(2)]
                mean = [sbt(stA, "mean%d" % i, [128, 32], F32) for i in range(2)]
                var = [sbt(stA, "var%d" % i, [128, 32], F32) for i in range(2)]
                rsg = [sbt(stA, "rsg%d" % i, [128, 32], F32) for i in range(2)]
                msa = sbt(stA, "msa", [128, NT], F32)
                rsa = sbt(stA, "rsa", [128, NT], F32)
                DMA(wsn[:], W["sgu_w_s"].rearrange("g i j -> i g j"), "ldws", (), ["wsn"])
                DMA(tri[:], tri_d, "ldtri", (), ["tri"])
                DMA(bsT[:], W["sgu_b_s"].rearrange("g i -> i g"), "ldbs", (), ["bsT"], slow=True)
                DMA(lng[:], W["sgu_ln_g"].partition_broadcast(128), "ldlng", (), ["lng"])
                DMA(lnb[:], W["sgu_ln_b"].partition_broadcast(128), "ldlnb", (), ["lnb"])
                DMA(gabc[:], W["out_norm_a"].partition_broadcast(128), "ldga", (), ["gabc"])
                ensure_norm("mix_norm", stA)
                for g in range(8):
                    b = 4 + g % 2
                    TR(PS[b][:, 0:128], wsn[:, g, :], identf[:], ["wsn", "identf"], [("ps", b)])
                    TT("dve", wsT[:, g, :], PS[b][:, 0:128], tri[:], ALU.mult, [("ps", b), "tri"], [("wsT", g)])
                MEMSET("dve", msa[:], 0.0, ["msa"])
                _stop("A0")
                def stage1a(G):
                    st_ = G % 2
                    for i in range(4):
                        t = 4 * G + i
                        b = t % 2
                        for kc in range(8):
                            MM(PS[b][:], HT[:, kc, t * 128:(t + 1) * 128], wblk[0][:, kc, :], kc == 0, kc == 7,
                               [("HT", t), ("wblk", 0)], [("ps", b)], kc == 7)
                        ACT(Gs[st_][:, i, :], PS[b][:], AF.Gelu, [("ps", b)], [("Gs", st_, i)])
                        REDUCE(s1[st_][:, i * 8:(i + 1) * 8], Gs[st_][:, i, :].rearrange("p (g c) -> p g c", g=8),
                               [("Gs", st_, i)], [("s1", st_, i)])
                        TT("dve", sqA[:], Gs[st_][:, i, :], Gs[st_][:, i, :], ALU.mult, [("Gs", st_, i)], ["sqA"])
                        REDUCE(s2[st_][:, i * 8:(i + 1) * 8], sqA[:].rearrange("p (g c) -> p g c", g=8),
                               ["sqA"], [("s2", st_, i)])
                    _stop("A1")
                    s1r = [("s1", st_, i) for i in range(4)]
                    s2r = [("s2", st_, i) for i in range(4)]
                    TS("dve", mean[st_][:], s1[st_][:], 1.0 / 64, None, ALU.mult, None, s1r, [("mean", st_)])
                    TT("dve", var[st_][:], mean[st_][:], mean[st_][:], ALU.mult, [("mean", st_)], [("var", st_)])
                    STT("dve", var[st_][:], s2[st_][:], 1.0 / 64, var[st_][:], ALU.mult, ALU.subtract,
                        s2r + [("var", st_)], [("var", st_)])
                    ACT(rsg[st_][:], var[st_][:], AF.Sqrt, [("var", st_)], [("rsg", st_)], bias=LN_EPS)

                def stage1b(G):
                    st_ = G % 2
                    RECIP(rsg[st_][:], rsg[st_][:], [("rsg", st_)], [("rsg", st_)])
                    _stop("A2")
                    for i in range(4):
                        ts_ = i % 2
                        g3 = Gs[st_][:, i, :].rearrange("p (g c) -> p g c", g=8)
                        t3 = tmpA[ts_][:].rearrange("p (g c) -> p g c", g=8)
                        TT("dve", t3, g3, mean[st_][:, i * 8:(i + 1) * 8].unsqueeze(2).to_broadcast([128, 8, 64]), ALU.subtract,
                           [("Gs", st_, i), ("mean", st_)], [("tmpA", ts_)])
                        TT("dve", t3, t3, rsg[st_][:, i * 8:(i + 1) * 8].unsqueeze(2).to_broadcast([128, 8, 64]), ALU.mult,
                           [("tmpA", ts_), ("rsg", st_)], [("tmpA", ts_)])
                        TT("pool", tmpA[ts_][:], tmpA[ts_][:], lng[:], ALU.mult, [("tmpA", ts_), "lng"], [("tmpA", ts_)])
                        TT("pool", VLN[st_][:, i, :], tmpA[ts_][:], lnb[:], ALU.add, [("tmpA", ts_), "lnb"], [("VLN", st_, i)])
                    _stop("A3")
                    for i in range(4):
                        t = 4 * G + i
                        b = t % 2
                        for kc in range(8):
                            MM(PS[b][:], HT[:, kc, t * 128:(t + 1) * 128], wblk[1][:, kc, :], kc == 0, kc == 7,
                               [("HT", t), ("wblk", 1)], [("ps", b)], kc == 7)
                        ACT(Ug[st_][:, i, :], PS[b][:], AF.Gelu, [("ps", b)], [("Ug", st_, i)])
                def stage2a(G):
                    st_ = G % 2
                    vr = [("VLN", st_, i) for i in range(4)]
                    ur = [("Ug", st_, i) for i in range(4)]
                    for g in range(8):
                        b = 2 + g % 2
                        MM(PS[b][:, 0:256], wsT[:, g, :], VLN[st_][:, :, g * 64:(g + 1) * 64], True, True,
                           [("wsT", g)] + vr, [("ps", b)], True)
                        STT("dve", Ug[st_][:, :, g * 64:(g + 1) * 64], PS[b][:, 0:256].rearrange("p (n c) -> p n c", n=4),
                            bsT[:, g:g + 1], Ug[st_][:, :, g * 64:(g + 1) * 64], ALU.add, ALU.mult,
                            [("ps", b), "bsT"] + ur, ur)
                    _stop("A5")
                    for i in range(4):
                        t = 4 * G + i
                        ACT(junk[:, 0:512], Ug[st_][:, i, :], AF.Square, [("Ug", st_, i)], ["junk", "msa"],
                            scale=1.0 / math.sqrt(512.0), accum_out=msa[:, t:t + 1])
                    ACT(rsa[:, 4 * G:4 * G + 4], msa[:, 4 * G:4 * G + 4], AF.Sqrt, ["msa"], [("rsa", G)], bias=EPS)

                def stage2b(G):
                    st_ = G % 2
                    RECIP(rsa[:, 4 * G:4 * G + 4], rsa[:, 4 * G:4 * G + 4], [("rsa", G)], [("rsa", G)])
                    _stop("A6")
                    for i in range(4):
                        t = 4 * G + i
                        s = t % 8
                        STT("dve", yan[s][:], Ug[st_][:, i, :], rsa[:, t:t + 1], gabc[:], ALU.mult, ALU.mult,
                            [("Ug", st_, i), ("rsa", G), "gabc"], [("yan", s)])

                def stage3(G):
                    for i in range(4):
                        t = 4 * G + i
                        s = t % 8
                        pst = PS[6 + t % 2].bitcast(BF16)
                        for c in range(4):
                            TR(pst[:, c * 128:(c + 1) * 128], yan[s][:, c * 128:(c + 1) * 128], identb[:],
                               [("yan", s), "identb"], [("ps", 6 + t % 2)], signal=(c == 3))
                        ACT(yaT[:, :, t * 128:(t + 1) * 128], pst[:, 0:512].rearrange("p (c n) -> p c n", c=4), AF.Copy,
                            [("ps", 6 + t % 2)], [("yaT", t)])

                stage1a(0)
                DMA(wblk[1][:], wv_in[:, :, 0:512], ("wb", 1), (), [("wblk", 1)], eng="pool")
                stage1a(1)
                stage1b(0)
                stage2a(0)
                stage1b(1)
                stage2b(0)
                stage1a(2)
                stage2a(1)
                stage3(0)
                stage1b(2)
                stage2b(1)
                stage1a(3)
                stage2a(2)
                stage3(1)
                stage1b(3)
                stage2b(2)
                stage2a(3)
                stage3(2)
                stage2b(3)
                stage3(3)
            p.barrier()
            _stop("A")
            with ExitStack() as stB:
                QT = sbt(stB, "QT", [128, 4, S], BF16)
                KT = sbt(stB, "KT", [128, 4, S], BF16)
                VA = sbt(stB, "VA", [128, NT, 8, 65], BF16)
                with ExitStack() as stB1:
                    wblk = [sbt(stB1, "wblkB%d" % i, [128, 8, 512], BF16) for i in range(2)]
                    DMA(wblk[0][:], wv_in[:, :, 1024:1536], ("wb", 0), (), [("wblk", 0)], eng="pool")
                    DMA(wblk[1][:], wv_in[:, :, 1536:2048], ("wb", 1), (), [("wblk", 1)], eng="pool")
                    posi = sbt(stB1, "posi", [128, NT], I32)
                    posf = sbt(stB1, "posf", [128, NT], F32)
                    invf = sbt(stB1, "invf", [128, 128], F32)
                    ANG = sbt(stB1, "ANG", [128, 2, 128], F32)
                    KF = sbt(stB1, "KF", [128, 256], F32)
                    KI = sbt(stB1, "KI", [128, 256], I32)
                    QR = [sbt(stB1, "QR%d" % i, [128, 512], BF16) for i in range(4)]
                    rt = [sbt(stB1, "rt%d" % i, [128, 8, 8], F32) for i in range(8)]
                    DMA(posi[:], pos_d, "ldpos", (), ["posi"])
                    DMA(invf[:], invf_d, "ldinvf", (), ["invf"])
                    MEMSET("pool", VA[:, :, :, 64:65], 1.0, ["VAones"])
                    CP("dve", posf[:], posi[:], ["posi"], ["posf"])
                    TT("dve", ANG[:, 0, :].rearrange("p (t i) -> p t i", i=8), invf[:].rearrange("p (t i) -> p t i", i=8),
                       posf[:].unsqueeze(2).to_broadcast([128, NT, 8]), ALU.mult, ["invf", "posf"], ["ANG"])
                    TS("dve", ANG[:, 1, :], ANG[:, 0, :], math.pi / 2, None, ALU.add, None, ["ANG"], ["ANG"])
                    A2 = ANG[:].rearrange("p a n -> p (a n)")
                    TS("dve", KF[:], A2, 1.0 / (2 * math.pi), None, ALU.mult, None, ["ANG"], ["KF"])
                    CP("dve", KI[:], KF[:], ["KF"], ["KI"])
                    CP("dve", KF[:], KI[:], ["KI"], ["KF"])
                    C1 = 6.28125
                    C2 = 2 * math.pi - C1
                    STT("dve", A2, KF[:], -C1, A2, ALU.mult, ALU.add, ["KF", "ANG"], ["ANG"])
                    STT("dve", A2, KF[:], -C2, A2, ALU.mult, ALU.add, ["KF", "ANG"], ["ANG"])
                    TS("dve", A2, A2, -math.pi, math.pi, ALU.max, ALU.min, ["ANG"], ["ANG"])
                    ACT(SC[:].rearrange("p a t i -> p (a t i)"), A2, AF.Sin, ["ANG"], ["SC"])

                    _stop("B1a")

                    def rotary(ps, bank, dst, dkey, t, k0):
                        ps3 = ps[:].rearrange("p (h e) -> p h e", h=8)
                        d3 = dst[:].rearrange("p (h e) -> p h e", h=8)
                        sinb = SC[:, 0, t:t + 1, :].to_broadcast([128, 8, 8])
                        cosb = SC[:, 1, t:t + 1, :].to_broadcast([128, 8, 8])
                        pr = [("ps", bank), "SC"]
                        CP("dve", d3[:, :, 16:64], ps3[:, :, 16:64], [("ps", bank)], [dkey])
                        TT("dve", rt[k0][:], ps3[:, :, 0:8], cosb, ALU.mult, pr, [("rt", k0)])
                        TT("dve", rt[k0 + 1][:], ps3[:, :, 8:16], sinb, ALU.mult, pr, [("rt", k0 + 1)])
                        TT("dve", d3[:, :, 0:8], rt[k0][:], rt[k0 + 1][:], ALU.subtract, [("rt", k0), ("rt", k0 + 1)], [dkey])
                        TT("dve", rt[k0 + 2][:], ps3[:, :, 8:16], cosb, ALU.mult, pr, [("rt", k0 + 2)])
                        TT("dve", rt[k0 + 3][:], ps3[:, :, 0:8], sinb, ALU.mult, pr, [("rt", k0 + 3)])
                        TT("dve", d3[:, :, 8:16], rt[k0 + 2][:], rt[k0 + 3][:], ALU.add, [("rt", k0 + 2), ("rt", k0 + 3)], [dkey])

                    def projqk(t):
                        for qk in range(2):
                            b = qk * 2 + t % 2
                            for kc in range(8):
                                MM(PS[b][:], HT[:, kc, t * 128:(t + 1) * 128], wblk[qk][:, kc, :], kc == 0, kc == 7,
                                   [("HT", t), ("wblk", qk)], [("ps", b)], kc == 7)
                            qs = qk * 2 + t % 2
                            rotary(PS[b], b, QR[qs], ("QR", qs), t, 4 * qk)

                    def trqk(t):
                        for qk in range(2):
                            qs = qk * 2 + t % 2
                            tb_ = 6 + qk
                            pst = PS[tb_].bitcast(BF16)
                            for c in range(4):
                                TR(pst[:, c * 128:(c + 1) * 128], QR[qs][:, c * 128:(c + 1) * 128], identb[:],
                                   [("QR", qs), "identb"], [("ps", tb_)], signal=(c == 3))
                            dstT = QT if qk == 0 else KT
                            ACT(dstT[:, :, t * 128:(t + 1) * 128], pst[:, 0:512].rearrange("p (c n) -> p c n", c=4),
                                AF.Copy, [("ps", tb_)], [("QKT", qk, t)])

                    projqk(0)
                    for t in range(NT):
                        if t + 1 < NT:
                            projqk(t + 1)
                        trqk(t)
                    _stop("B1c")
                    DMA(wblk[0][:], wv_in[:, :, 2048:2560], ("wb", 0), (), [("wblk", 0)], eng="pool")
                    for t in range(NT):
                        b = 4 + t % 2
                        for kc in range(8):
                            MM(PS[b][:], HT[:, kc, t * 128:(t + 1) * 128], wblk[0][:, kc, :], kc == 0, kc == 7,
                               [("HT", t), ("wblk", 0)], [("ps", b)], kc == 7)
                        ACT(VA[:, t, :, 0:64], PS[b][:].rearrange("p (h e) -> p h e", h=8), AF.Copy,
                            [("ps", b)], [("VA", t)])
                p.barrier()
                _stop("B1")
                with ExitStack() as stB2:
                    cm = sbt(stB2, "cm", [128, S], BF16)
                    NPB2 = 5
                    pb2 = [sbt(stB2, "pb%d" % i, [128, 2, 512], BF16) for i in range(NPB2)]
                    YB = [sbt(stB2, "YB%d" % i, [128, 4, 512], F32) for i in range(2)]
                    rzt = [sbt(stB2, "rzt%d" % i, [128, 4], F32) for i in range(4)]
                    ybn = [sbt(stB2, "ybn%d" % i, [128, 512], BF16) for i in range(2)]
                    gbbc = sbt(stB2, "gbbc", [128, 512], F32)
                    msb = sbt(stB2, "msb", [128, NT], F32)
                    YTb = HT[:, 0:4, :]
                    WO = HT[:, 4:8, :].rearrange("p a (b n) -> p (a b) n", n=1024)
                    DMA(cm[:], cmask_d, "ldcm", (), ["cm"], eng="pool")
                    DMA(gbbc[:], W["out_norm_b"].partition_broadcast(128), "ldgb", (), ["gbbc"])
                    DMA(WO, W["w_out"].rearrange("(kc p) n -> p kc n", p=128), "ldwo", (), ["WO"], eng="pool")
                    MEMSET("dve", msb[:], 0.0, ["msb"])
                    steps = []
                    for b in range(4):
                        for c in range(4):
                            nj = 4 * b + 4
                            for j in range(nj):
                                steps.append((b, c, j, nj))
                    LA = 4
                    SBK = [(0, 1), (6, 7)]
                    pending = []

                    def front(i):
                        b, c, j, nj = steps[i]
                        qlo = max(512 * b, 128 * j)
                        N = 512 * (b + 1) - qlo
                        for e in range(2):
                            r0 = e * 64
                            sb_ = SBK[i % 2][e]
                            MM(PS[sb_][:, 0:N], KT[r0:r0 + 64, c, j * 128:(j + 1) * 128], QT[r0:r0 + 64, c, qlo:qlo + N],
                               True, True, (), [("ps", sb_)], True)
                        pp = PSP[SBK[i % 2][0] // 2]
                        ps_ = i % NPB2
                        prs = [("ps", SBK[i % 2][0]), ("ps", SBK[i % 2][1])]
                        ACT(pb2[ps_][:, :, 0:N], pp[:, :, 0:N], AF.Exp, prs, [("pb", ps_)], scale=0.125)
                        TT("pool" if i % 3 == 0 else "dve", pb2[ps_][:, :, 0:N], pb2[ps_][:, :, 0:N],
                           cm[:, qlo - 128 * j:qlo - 128 * j + N].unsqueeze(1).to_broadcast([128, 2, N]), ALU.mult,
                           [("pb", ps_), "cm"], [("pb", ps_)])

                    def back(i):
                        b, c, j, nj = steps[i]
                        qlo = max(512 * b, 128 * j)
                        N = 512 * (b + 1) - qlo
                        nqb = N // 128
                        qb0 = 4 - nqb
                        g = b * 4 + c
                        for e in range(2):
                            h = 2 * c + e
                            ab = (2, 3)[e] if g % 2 == 0 else (4, 5)[e]
                            ps_ = i % NPB2
                            for qi in range(nqb):
                                qb = qb0 + qi
                                last = (qi == nqb - 1)
                                MM(PS[ab][:, qb * 65:(qb + 1) * 65], pb2[ps_][:, e, qi * 128:(qi + 1) * 128], VA[:, j, h, :],
                                   (j == 0 and qi == 0), (j == nj - 1), [("pb", ps_), "VAones"], [("ps", ab)],
                                   last and (j == nj - 1), sgc=True)
                        if j == nj - 1:
                            pending.append((i + 1, epi, b, c))

                    def epi(b, c):
                        g = b * 4 + c
                        ys = b % 2
                        for e in range(2):
                            h = 2 * c + e
                            ab = (2, 3)[e] if g % 2 == 0 else (4, 5)[e]
                            acc3 = PS[ab][:, 0:260].rearrange("p (q e) -> p q e", q=4)
                            rs = (2 * g + e) % 4
                            RECIP(rzt[rs][:].unsqueeze(2), acc3[:, :, 64:65], [("ps", ab)], [("rzt", rs)])
                            TT("dve", YB[ys][:, :, h * 64:(h + 1) * 64], acc3[:, :, 0:64],
                               rzt[rs][:].unsqueeze(2).to_broadcast([128, 4, 64]), ALU.mult,
                               [("ps", ab), ("rzt", rs)], [("YB", ys, h)])
                        if c == 3:
                            pending.append((0, bank_tail, b, 0))

                    def bank_tail(b, _):
                        ys = b % 2
                        ybr_ = [("YB", ys, h) for h in range(8)]
                        for i4 in range(4):
                            t = 4 * b + i4
                            s = t % 2
                            ACT(junk[:, 0:512], YB[ys][:, i4, :], AF.Square, ybr_, ["junk", "msb"],
                                scale=1.0 / math.sqrt(512.0), accum_out=msb[:, t:t + 1])
                            TT("pool", ybn[s][:], YB[ys][:, i4, :], gbbc[:], ALU.mult, ybr_ + ["gbbc"], [("ybn", s)])
                            tbk = 6 + s
                            pst = PS[tbk].bitcast(BF16)
                            for cc in range(4):
                                TR(pst[:, cc * 128:(cc + 1) * 128], ybn[s][:, cc * 128:(cc + 1) * 128], identb[:],
                                   [("ybn", s), "identb"], [("ps", tbk)], signal=(cc == 3))
                            CP("dve", YTb[:, :, t * 128:(t + 1) * 128], pst[:, 0:512].rearrange("p (c n) -> p c n", c=4),
                               [("ps", tbk)], [("YTb", t)])

                    nst = len(steps)
                    for i in range(nst + LA + 4):
                        if i < nst:
                            front(i)
                        if 0 <= i - LA < nst:
                            back(i - LA)
                        due = [q for q in pending if q[0] <= i - LA]
                        for q in due:
                            pending.remove(q)
                            q[1](q[2], q[3])
                    while pending:
                        q = pending.pop(0)
                        q[1](q[2], q[3])
                    _stop("B2a")
                    ACT(rstdb[:], msb[:], AF.Sqrt, ["msb"], ["rstdb"], bias=EPS)
                    RECIP(rstdb[:], rstdb[:], ["rstdb"], ["rstdb"])
                    ybr = []
                    k = 0
                    tail = tail_factory(stB2, act_only=True)
                    for t in range(NT):
                        for dh in range(2):
                            ba = k % 2
                            bb = 2 + k % 2
                            k += 1
                            for c in range(4):
                                MM(PS[ba][:], yaT[:, c, t * 128:(t + 1) * 128], WO[:, c, dh * 512:(dh + 1) * 512],
                                   c == 0, c == 3, ["WO"], [("ps", ba)], c == 3)
                            for c in range(4):
                                MM(PS[bb][:], YTb[:, c, t * 128:(t + 1) * 128], WO[:, 4 + c, dh * 512:(dh + 1) * 512],
                                   c == 0, c == 3, ["WO", ("YTb", t)], [("ps", bb)], c == 3)
                            xs = X[:, t, dh * 512:(dh + 1) * 512]
                            TT("dve", xs, xs, PS[ba][:], ALU.add, [("ps", ba), ("X", t, dh)], [("X", t, dh)])
                            STT("dve", xs, PS[bb][:], rstdb[:, t:t + 1], xs, ALU.mult, ALU.add,
                                [("ps", bb), ("X", t, dh), "rstdb"], [("X", t, dh)])
                        tail.stats(t)
                    p.barrier()
                    for t in range(NT):
                        tail.apply(t)
        p.barrier()

    def cross(tail_factory):
        with ExitStack() as st:
            ensure_norm("cross_norm", st)
            memt = sbt(st, "memt", [128, 2, D], F32)
            MT = sbt(st, "MT", [128, 8, NMEM], BF16)
            KxT = sbt(st, "KxT", [128, 8, NMEM], BF16)
            Vx = sbt(st, "Vx", [128, 2, D], BF16)
            QxT = sbt(st, "QxT", [128, 8, S], BF16)
            wsl = [sbt(st, "wx%d" % i, [128, 8, 512], BF16) for i in range(3)]
            onesb = sbt(st, "onesb", [128, 128], BF16)
            pbx = [sbt(st, "pbx%d" % i, [128, 512], BF16) for i in range(4)]
            rzx = [sbt(st, "rzx%d" % i, [128, 512], F32) for i in range(2)]
            mss = sbt(st, "mss", [128, 2], F32)
            mrs = sbt(st, "mrs", [128, 2], F32)
            wplan = [("cross_wq", 0), ("cross_wk", 0), ("cross_wk", 1), ("cross_wv", 0), ("cross_wv", 1),
                     ("cross_wq", 1), ("cross_wo", 0), ("cross_wo", 1)]
            sl_of = {}

            wslot = [0, 1, 2, 1, 2, 1, 2, 0]

            def load_w(i):
                sl = wslot[i]
                nm, blk = wplan[i]
                src = W[nm].rearrange("(kc p) n -> p kc n", p=128)[:, :, blk * 512:(blk + 1) * 512]
                DMA(wsl[sl][:], src, ("wx", sl), (), [("wx", sl)], eng="pool")
                sl_of[i] = sl

            load_w(0)
            MEMSET("pool", onesb[:], 1.0, ["onesb"])
            kk = [0]

            def QP(hc, tb):
                sl = sl_of[0] if hc < 4 else sl_of[5]
                b = kk[0] % 2
                kk[0] += 1
                htr = [("HT", 4 * tb + i) for i in range(4)]
                for kc in range(8):
                    MM(PS[b][:], wsl[sl][:, kc, (hc % 4) * 128:(hc % 4 + 1) * 128], HT[:, kc, tb * 512:(tb + 1) * 512],
                       kc == 0, kc == 7, [("wx", sl)] + htr, [("ps", b)], kc == 7)
                if kk[0] % 2 == 0 or hc >= 2:
                    ACT(QxT[:, hc, tb * 512:(tb + 1) * 512], PS[b][:], AF.Copy, [("ps", b)], [("QxT", hc, tb)])
                else:
                    CP("dve", QxT[:, hc, tb * 512:(tb + 1) * 512], PS[b][:], [("ps", b)], [("QxT", hc, tb)])

            for hc in range(2):
                for tb in range(4):
                    QP(hc, tb)
                if hc == 0:
                    load_w(1)
                    load_w(2)
                    for mt in range(2):
                        DMA(memt[:, mt, :], mem_d[mt * 128:(mt + 1) * 128, :], ("ldm", mt), (), [("memt", mt)])
            gm = load_gain("mem_norm")
            MEMSET("dve", mss[:], 0.0, ["mss"])
            for mt in range(2):
                ACT(junk[:], memt[:, mt, :], AF.Square, [("memt", mt)], ["junk", "mss"], scale=1.0 / 32.0,
                    accum_out=mss[:, mt:mt + 1])
            ACT(mrs[:], mss[:], AF.Sqrt, ["mss"], ["mrs"], bias=EPS)
            RECIP(mrs[:], mrs[:], ["mrs"], ["mrs"])
            for mt in range(2):
                s = mt % 2
                STT("dve", xn[s][:], memt[:, mt, :], mrs[:, mt:mt + 1], gbc[gm][:], ALU.mult, ALU.mult,
                    [("memt", mt), "mrs", ("gbc", gm)], [("xn", s)])
                pst = PS[6 + s].bitcast(BF16)
                for c in range(8):
                    TR(pst[:, c * 128:(c + 1) * 128], xn[s][:, c * 128:(c + 1) * 128], identb[:],
                       [("xn", s), "identb"], [("ps", 6 + s)], signal=(c == 7))
                ACT(MT[:, :, mt * 128:(mt + 1) * 128], pst[:, :].rearrange("p (c n) -> p c n", c=8), AF.Copy,
                    [("ps", 6 + s)], [("MT", mt)])
            mtr = [("MT", 0), ("MT", 1)]
            for hc in range(8):
                sl = sl_of[1 + hc // 4]
                b = kk[0] % 2
                kk[0] += 1
                for kc in range(8):
                    MM(PS[b][:, 0:NMEM], wsl[sl][:, kc, (hc % 4) * 128:(hc % 4 + 1) * 128], MT[:, kc, :], kc == 0, kc == 7,
                       [("wx", sl)] + mtr, [("ps", b)], kc == 7)
                ACT(KxT[:, hc, :], PS[b][:, 0:NMEM], AF.Copy, [("ps", b)], [("KxT", hc)])
                if hc == 3:
                    load_w(3)
            load_w(4)
            for dh in range(2):
                sl = sl_of[3 + dh]
                for mt in range(2):
                    b = kk[0] % 2
                    kk[0] += 1
                    for kc in range(8):
                        MM(PS[b][:], MT[:, kc, mt * 128:(mt + 1) * 128], wsl[sl][:, kc, :], kc == 0, kc == 7,
                           [("wx", sl), ("MT", mt)], [("ps", b)], kc == 7)
                    ACT(Vx[:, mt, dh * 512:(dh + 1) * 512], PS[b][:], AF.Copy, [("ps", b)], [("Vx", mt, dh)])
                load_w(5 + dh)
            itc = [0]

            def ATT(h, tb, alt=False):
                par = itc[0] % 2
                itc[0] += 1
                sbk = (2, 3)
                ob = (4, 5)
                zb = 7
                if alt and par == 1:
                    ob = (0, 1)
                    zb = 6
                cols = slice(tb * 512, (tb + 1) * 512)
                for mt in range(2):
                    for ec in range(2):
                        MM(PS[sbk[mt]][:], KxT[:, 2 * h + ec, mt * 128:(mt + 1) * 128], QxT[:, 2 * h + ec, cols], ec == 0, ec == 1,
                           [("KxT", 2 * h + ec), ("QxT", 2 * h + ec, tb)], [("ps", sbk[mt])], ec == 1)
                    ACT(pbx[par * 2 + mt][:], PS[sbk[mt]][:], AF.Exp, [("ps", sbk[mt])], [("pbx", par * 2 + mt)], scale=1.0 / 16.0)
                for ec in range(2):
                    for mt in range(2):
                        MM(PS[ob[ec]][:], Vx[:, mt, (2 * h + ec) * 128:(2 * h + ec + 1) * 128], pbx[par * 2 + mt][:], mt == 0, mt == 1,
                           [("Vx", mt, (2 * h + ec) // 4), ("pbx", par * 2 + mt)], [("ps", ob[ec])], mt == 1)
                for mt in range(2):
                    MM(PS[zb][:], onesb[:], pbx[par * 2 + mt][:], mt == 0, mt == 1, ["onesb", ("pbx", par * 2 + mt)],
                       [("ps", zb)], mt == 1)
                RECIP(rzx[par][:], PS[zb][:], [("ps", zb)], [("rzx", par)])
                for ec in range(2):
                    TT("dve", QxT[:, 2 * h + ec, cols], PS[ob[ec]][:], rzx[par][:], ALU.mult,
                       [("ps", ob[ec]), ("rzx", par)], [("QxT", 2 * h + ec, tb)])

            for h in range(3):
                qpl = [(hc, tb) for hc in (2 * h + 2, 2 * h + 3) for tb in range(4)]
                for i, (hc, tb) in enumerate(qpl):
                    QP(hc, tb)
                    if i % 2 == 1:
                        ATT(h, i // 2)
                if h == 0:
                    load_w(7)
            for tb in range(4):
                ATT(3, tb, alt=True)
            tail = tail_factory(st)
            for t in range(NT):
                tail.pre(t)
                for dh in range(2):
                    if dh == 1:
                        tail.mid(t)
                    b = kk[0] % 2
                    kk[0] += 1
                    sl = sl_of[6 + dh]
                    for hc in range(8):
                        MM(PS[b][:], QxT[:, hc, t * 128:(t + 1) * 128], wsl[sl][:, hc, :], hc == 0, hc == 7,
                           [("QxT", hc, t // 4), ("wx", sl)], [("ps", b)], hc == 7)
                    xs = X[:, t, dh * 512:(dh + 1) * 512]
                    TT("dve", xs, xs, PS[b][:], ALU.add, [("ps", b), ("X", t, dh)], [("X", t, dh)])
                tail.post(t)
        p.barrier()

    seq = [q for q in ("ffn1", "mix", "cross", "ffn2") if q in stages]
    gains = {"ffn1": "ffn1_norm", "mix": "mix_norm", "cross": "cross_norm", "ffn2": "ffn2_norm"}
    for i, sname in enumerate(seq):
        nxt = seq[i + 1] if i + 1 < len(seq) else None
        tf = make_norm_HT(gains[nxt]) if nxt else make_norm_out()
        if sname in ("ffn1", "ffn2"):
            ffn(sname, tf)
        elif sname == "mix":
            mixer(tf)
            p.muted = False
            p.barrier()
        else:
            cross(tf)
    if not norm_state["out_done"]:
        with ExitStack() as st:
            step = make_norm_out()(st)
            for t in range(NT):
                step(t)
    p.emit()
    top.close()
    return nc


_CACHE = {}


def _consts():
    ident = np.eye(128, dtype=np.float32)
    j = np.arange(128)[:, None]
    i = np.arange(128)[None, :]
    tri = (j <= i).astype(np.float32)
    d = np.arange(S)[None, :] - np.arange(128)[:, None]
    cm = ((d >= 0) & (d <= 128)).astype(np.float32) + ((d >= 0) & (d % 4 == 0) & (d <= 512)).astype(np.float32) \
        + ((d >= 0) & (d % 16 == 0)).astype(np.float32)
    ex = (-2.0 * np.arange(8, dtype=np.float32) / np.float32(16.0)).astype(np.float32)
    invf = np.power(np.float32(500000.0), ex).astype(np.float32)
    invf_t = np.tile(invf[None, :], (128, 16)).astype(np.float32)
    return {"ident": ident, "tri": tri, "cmask": cm.astype(np.float32), "invf": invf_t}


def kernel(**inputs):
    n = 8
    if "nc" not in _CACHE:
        _CACHE["nc"] = build_nc()
    nc = _CACHE["nc"]
    cst = _consts()
    shared = {}
    for k, v in inputs.items():
        if k in ("x", "mem", "positions"):
            continue
        a = np.asarray(v)
        if k == "final_norm":
            a = a.reshape(1, D)
        elif k in ("sgu_w_s", "sgu_b_s"):
            a = a[0]
        elif a.ndim == 2:
            pass
        else:
            a = a[0]
        shared[k] = np.ascontiguousarray(a, dtype=np.float32)
    x = np.asarray(inputs["x"], dtype=np.float32)
    mem = np.asarray(inputs["mem"], dtype=np.float32)
    pos = np.asarray(inputs["positions"], dtype=np.int32)
    in_maps = []
    for b in range(n):
        m = dict(shared)
        m.update(cst)
        m["x"] = np.ascontiguousarray(x[b])
        m["mem"] = np.ascontiguousarray(mem[b])
        m["pos"] = np.ascontiguousarray(pos[b].reshape(NT, 128).T)
        in_maps.append(m)
    res = run_bass_kernel_spmd(nc, in_maps, core_ids=list(range(n)))
    return np.stack([np.asarray(r["out"], dtype=np.float32) for r in res.results], axis=0)
```
